# Optimizing a Trainium2 kernel written in Bass

```python
import jax
import jax.numpy as jnp
from jax import lax
import numpy as np

D_MODEL = 1024
BATCH = 2
SEQ = 8192
DEPTH = 2

GRID_W = 64
CTX_LEN = 256
HEAD_DIM = 64
A_Q_HEADS = 12
A_KV_HEADS = 4
B_GROUPS = 4
B_GROUP_DIM = 64
C_CHANNELS = 512
C_KERNEL = 31
D_HEADS = 8
NA_WIN_ROWS = 8
NA_WIN_COLS = 16
D_FF = 2816
ROPE_THETA = 10000.0
Q_BLOCK = 128
NORM_EPS = 1e-6
N_MOD = 9
FFN_RES_WEIGHT = 0.5

A_Q_W = A_Q_HEADS * HEAD_DIM
A_KV_W = A_KV_HEADS * HEAD_DIM
B_W = B_GROUPS * B_GROUP_DIM
AB_IN = A_Q_W + 2 * A_KV_W + B_W
AB_OUT = A_Q_W + B_W
D_W = D_HEADS * HEAD_DIM
CD_IN = 2 * C_CHANNELS + 3 * D_W
CD_OUT = C_CHANNELS + D_W

kernel_name = 'hybrid_diffusion_gqa_fourier_conformer_natten'


def _rmsnorm(x):
    xf = x.astype(jnp.float32)
    return (xf * lax.rsqrt(jnp.mean(xf * xf, axis=-1, keepdims=True) + NORM_EPS)).astype(x.dtype)


def _layernorm(x, w, b):
    xf = x.astype(jnp.float32)
    mu = jnp.mean(xf, axis=-1, keepdims=True)
    var = jnp.mean(jnp.square(xf - mu), axis=-1, keepdims=True)
    return ((xf - mu) * lax.rsqrt(var + NORM_EPS)).astype(x.dtype) * w + b


def _modulate(x, shift, scale):
    return _rmsnorm(x) * (1 + scale) + shift


def _swiglu(h, w_gate, w_up, w_down):
    return (jax.nn.silu(h @ w_gate) * (h @ w_up)) @ w_down


def _macaron_half(x, shift, scale, gate, w_gate, w_up, w_down):
    return x + FFN_RES_WEIGHT * gate * _swiglu(_modulate(x, shift, scale), w_gate, w_up, w_down)


def _rope_2d(x, pos_row, pos_col):
    half = x.shape[-1] // 2
    n_ax = half // 2
    inv = ROPE_THETA ** (-jnp.arange(n_ax, dtype=jnp.float32) / n_ax)
    ang = jnp.concatenate([pos_row[:, None] * inv, pos_col[:, None] * inv], axis=-1)
    cos = jnp.cos(ang)[None, :, None, :].astype(x.dtype)
    sin = jnp.sin(ang)[None, :, None, :].astype(x.dtype)
    x1, x2 = x[..., :half], x[..., half:]
    return jnp.concatenate([x1 * cos - x2 * sin, x2 * cos + x1 * sin], axis=-1)


def _attend(q, k, v):
    bsz, lq, heads, hd = q.shape
    kvh = k.shape[2]
    qg = q.reshape(bsz, lq, kvh, heads // kvh, hd)
    s = jnp.einsum('bqkgd,blkd->bkgql', qg, k).astype(jnp.float32) * (hd ** -0.5)
    p = jax.nn.softmax(s, axis=-1).astype(v.dtype)
    o = jnp.einsum('bkgql,blkd->bqkgd', p, v)
    return o.reshape(bsz, lq, heads * hd)


def _gqa_blocks(q, k_all, v_all):
    bsz, seq, heads, hd = q.shape
    qb = jnp.moveaxis(q.reshape(bsz, seq // Q_BLOCK, Q_BLOCK, heads, hd), 1, 0)
    out = lax.map(lambda qblk: _attend(qblk, k_all, v_all), qb)
    return jnp.moveaxis(out, 0, 1).reshape(bsz, seq, heads * hd)


def _fourier(h):
    bsz, length, _ = h.shape
    hg = h.reshape(bsz, length, B_GROUPS, B_GROUP_DIM).astype(jnp.float32)
    f = jnp.fft.fft2(hg, axes=(1, 3), norm='ortho').real
    return f.reshape(bsz, length, B_W).astype(h.dtype)


def _conv_module(a, g, dw_w, dw_b, ln_w, ln_b):
    u = a * jax.nn.sigmoid(g)
    y = lax.conv_general_dilated(u, dw_w[:, None, :], window_strides=(1,),
                                 padding=[(C_KERNEL // 2, C_KERNEL // 2)],
                                 dimension_numbers=('NWC', 'WIO', 'NWC'),
                                 feature_group_count=u.shape[-1]) + dw_b
    return jax.nn.silu(_layernorm(y, ln_w, ln_b))


def _neighbourhood_attention(q, k, v, kc, vc, rpb):
    bsz, seq, heads, hd = q.shape
    rows = seq // GRID_W
    wr = min(NA_WIN_ROWS, rows)
    wc = NA_WIN_COLS
    qg = q.reshape(bsz, rows, GRID_W, heads, hd)
    kg = k.reshape(bsz, rows, GRID_W, heads, hd)
    vg = v.reshape(bsz, rows, GRID_W, heads, hd)
    col_q = jnp.arange(GRID_W)
    col_idx = jnp.clip(col_q - wc // 2, 0, GRID_W - wc)[:, None] + jnp.arange(wc)[None, :]
    col_off = col_idx - col_q[:, None] + (NA_WIN_COLS - 1)
    scale = hd ** -0.5

    def row_block(r):
        start = jnp.clip(r - wr // 2, 0, rows - wr)
        qr = lax.dynamic_index_in_dim(qg, r, axis=1, keepdims=False)
        kn = lax.dynamic_slice_in_dim(kg, start, wr, axis=1)[:, :, col_idx]
        vn = lax.dynamic_slice_in_dim(vg, start, wr, axis=1)[:, :, col_idx]
        row_off = start + jnp.arange(wr) - r + (NA_WIN_ROWS - 1)
        bias = rpb[:, row_off[None, :, None], col_off[:, None, :]]
        s_nb = (jnp.einsum('bqhd,brqjhd->bhqrj', qr, kn).astype(jnp.float32) * scale
                + bias[None].astype(jnp.float32)).reshape(bsz, heads, GRID_W, wr * wc)
        s_cx = jnp.einsum('bqhd,blhd->bhql', qr, kc).astype(jnp.float32) * scale
        p = jax.nn.softmax(jnp.concatenate([s_nb, s_cx], axis=-1), axis=-1).astype(v.dtype)
        p_nb = p[..., :wr * wc].reshape(bsz, heads, GRID_W, wr, wc)
        p_cx = p[..., wr * wc:]
        return (jnp.einsum('bhqrj,brqjhd->bqhd', p_nb, vn)
                + jnp.einsum('bhql,blhd->bqhd', p_cx, vc))

    out = lax.map(row_block, jnp.arange(rows))
    return jnp.moveaxis(out, 0, 1).reshape(bsz, seq, heads * hd)


def _mixer_ab(hl, hc, w_in, w_out, q_norm, k_norm, with_ctx):
    bsz, seq, _ = hl.shape
    lc = hc.shape[1]
    t = jnp.arange(seq)
    row = (t // GRID_W).astype(jnp.float32)
    col = (t % GRID_W).astype(jnp.float32)
    cuts = [A_Q_W, A_Q_W + A_KV_W, A_Q_W + 2 * A_KV_W]
    q, k, v, f = jnp.split(hl @ w_in, cuts, axis=-1)
    q = _rope_2d(_rmsnorm(q.reshape(bsz, seq, A_Q_HEADS, HEAD_DIM)) * q_norm, row, col)
    k = _rope_2d(_rmsnorm(k.reshape(bsz, seq, A_KV_HEADS, HEAD_DIM)) * k_norm, row, col)
    v = v.reshape(bsz, seq, A_KV_HEADS, HEAD_DIM)
    if with_ctx:
        qc, kc, vc, fc = jnp.split(hc @ w_in, cuts, axis=-1)
    else:
        kc, vc = jnp.split(hc @ w_in[:, A_Q_W:A_Q_W + 2 * A_KV_W], [A_KV_W], axis=-1)
    kc = _rmsnorm(kc.reshape(bsz, lc, A_KV_HEADS, HEAD_DIM)) * k_norm
    vc = vc.reshape(bsz, lc, A_KV_HEADS, HEAD_DIM)
    k_all = jnp.concatenate([kc, k], axis=1)
    v_all = jnp.concatenate([vc, v], axis=1)
    yl = jnp.concatenate([_gqa_blocks(q, k_all, v_all), _fourier(f)], axis=-1) @ w_out
    if not with_ctx:
        return yl, None
    qc = _rmsnorm(qc.reshape(bsz, lc, A_Q_HEADS, HEAD_DIM)) * q_norm
    yc = jnp.concatenate([_attend(qc, kc, vc), _fourier(fc)], axis=-1) @ w_out
    return yl, yc


def _mixer_cd(hl, hc, w_in, w_out, dw_w, dw_b, ln_w, ln_b, rpb, with_ctx):
    bsz, seq, _ = hl.shape
    lc = hc.shape[1]
    cuts = [C_CHANNELS, 2 * C_CHANNELS, 2 * C_CHANNELS + D_W, 2 * C_CHANNELS + 2 * D_W]
    ga, gb, q, k, v = jnp.split(hl @ w_in, cuts, axis=-1)
    if with_ctx:
        gac, gbc, qc, kc, vc = jnp.split(hc @ w_in, cuts, axis=-1)
    else:
        kc, vc = jnp.split(hc @ w_in[:, 2 * C_CHANNELS + D_W:], [D_W], axis=-1)
    kc = kc.reshape(bsz, lc, D_HEADS, HEAD_DIM)
    vc = vc.reshape(bsz, lc, D_HEADS, HEAD_DIM)
    y_conv = _conv_module(ga, gb, dw_w, dw_b, ln_w, ln_b)
    y_na = _neighbourhood_attention(q.reshape(bsz, seq, D_HEADS, HEAD_DIM),
                                    k.reshape(bsz, seq, D_HEADS, HEAD_DIM),
                                    v.reshape(bsz, seq, D_HEADS, HEAD_DIM), kc, vc, rpb)
    yl = jnp.concatenate([y_conv, y_na], axis=-1) @ w_out
    if not with_ctx:
        return yl, None
    yc_conv = _conv_module(gac, gbc, dw_w, dw_b, ln_w, ln_b)
    yc_att = _attend(qc.reshape(bsz, lc, D_HEADS, HEAD_DIM), kc, vc)
    yc = jnp.concatenate([yc_conv, yc_att], axis=-1) @ w_out
    return yl, yc


def setup_inputs(seed: int = 0) -> dict:
    key = jax.random.key(seed)
    ks = jax.random.split(key, 21)
    n_even = (DEPTH + 1) // 2
    n_odd = DEPTH // 2

    def nrm(k, shape, scale):
        return jax.random.normal(k, shape, jnp.float32) * scale

    return {
        'x': nrm(ks[0], (BATCH, SEQ, D_MODEL), 1.0),
        'c': nrm(ks[1], (BATCH, D_MODEL), 1.0),
        'ctx': nrm(ks[2], (BATCH, CTX_LEN, D_MODEL), 1.0),
        'c_ctx': nrm(ks[3], (D_MODEL,), 1.0),
        'w_mod': nrm(ks[4], (DEPTH, D_MODEL, N_MOD * D_MODEL), 0.5 * D_MODEL ** -0.5),
        'b_mod': nrm(ks[5], (DEPTH, N_MOD * D_MODEL), 0.02),
        'ffn_w_gate': nrm(ks[6], (DEPTH, 2, D_MODEL, D_FF), D_MODEL ** -0.5),
        'ffn_w_up': nrm(ks[7], (DEPTH, 2, D_MODEL, D_FF), D_MODEL ** -0.5),
        'ffn_w_down': nrm(ks[8], (DEPTH, 2, D_FF, D_MODEL), D_FF ** -0.5),
        'ab_w_in': nrm(ks[9], (n_even, D_MODEL, AB_IN), D_MODEL ** -0.5),
        'ab_w_out': nrm(ks[10], (n_even, AB_OUT, D_MODEL), AB_OUT ** -0.5),
        'ab_q_norm': 1.0 + nrm(ks[11], (n_even, HEAD_DIM), 0.05),
        'ab_k_norm': 1.0 + nrm(ks[12], (n_even, HEAD_DIM), 0.05),
        'cd_w_in': nrm(ks[13], (n_odd, D_MODEL, CD_IN), D_MODEL ** -0.5),
        'cd_w_out': nrm(ks[14], (n_odd, CD_OUT, D_MODEL), CD_OUT ** -0.5),
        'cd_dw_w': nrm(ks[15], (n_odd, C_KERNEL, C_CHANNELS), C_KERNEL ** -0.5),
        'cd_dw_b': nrm(ks[16], (n_odd, C_CHANNELS), 0.02),
        'cd_ln_w': 1.0 + nrm(ks[17], (n_odd, C_CHANNELS), 0.05),
        'cd_ln_b': nrm(ks[18], (n_odd, C_CHANNELS), 0.02),
        'cd_rpb': nrm(ks[19], (n_odd, D_HEADS, 2 * NA_WIN_ROWS - 1, 2 * NA_WIN_COLS - 1), 0.1),
        'final_norm': 1.0 + nrm(ks[20], (D_MODEL,), 0.05),
    }


def reference(x, c, ctx, c_ctx, w_mod, b_mod, ffn_w_gate, ffn_w_up, ffn_w_down,
              ab_w_in, ab_w_out, ab_q_norm, ab_k_norm,
              cd_w_in, cd_w_out, cd_dw_w, cd_dw_b, cd_ln_w, cd_ln_b, cd_rpb, final_norm):
    bsz = x.shape[0]
    xl, xc = x, ctx
    for layer in range(DEPTH):
        last = layer == DEPTH - 1
        ml = (jax.nn.silu(c) @ w_mod[layer] + b_mod[layer]).reshape(bsz, N_MOD, 1, D_MODEL)
        mc = (jax.nn.silu(c_ctx) @ w_mod[layer] + b_mod[layer]).reshape(N_MOD, D_MODEL)
        xl = _macaron_half(xl, ml[:, 0], ml[:, 1], ml[:, 2],
                           ffn_w_gate[layer, 0], ffn_w_up[layer, 0], ffn_w_down[layer, 0])
        xc = _macaron_half(xc, mc[0], mc[1], mc[2],
                           ffn_w_gate[layer, 0], ffn_w_up[layer, 0], ffn_w_down[layer, 0])
        hl = _modulate(xl, ml[:, 3], ml[:, 4])
        hc = _modulate(xc, mc[3], mc[4])
        i = layer // 2
        if layer % 2 == 0:
            yl, yc = _mixer_ab(hl, hc, ab_w_in[i], ab_w_out[i], ab_q_norm[i], ab_k_norm[i], not last)
        else:
            yl, yc = _mixer_cd(hl, hc, cd_w_in[i], cd_w_out[i], cd_dw_w[i], cd_dw_b[i],
                               cd_ln_w[i], cd_ln_b[i], cd_rpb[i], not last)
        xl = xl + ml[:, 5] * yl
        xl = _macaron_half(xl, ml[:, 6], ml[:, 7], ml[:, 8],
                           ffn_w_gate[layer, 1], ffn_w_up[layer, 1], ffn_w_down[layer, 1])
        if not last:
            xc = xc + mc[5] * yc
            xc = _macaron_half(xc, mc[6], mc[7], mc[8],
                               ffn_w_gate[layer, 1], ffn_w_up[layer, 1], ffn_w_down[layer, 1])
    return _rmsnorm(xl) * final_norm
```

```python
import contextlib
import numpy as np
import concourse.bass as bass
import concourse.mybir as mybir
from concourse.bass_utils import run_bass_kernel_spmd

F32 = mybir.dt.float32
BF16 = mybir.dt.bfloat16
I32 = mybir.dt.int32
AF = mybir.ActivationFunctionType
ALU = mybir.AluOpType
AX = mybir.AxisListType

ENGS = ("pe", "act", "dve", "pool", "sp")
DMA_RING = 12


class Sched:
    def __init__(self, nc, st):
        self.nc = nc
        self.ops = []
        self.start = 0
        self.last_w = {}
        self.readers = {}
        self.ring_n = {e: 0 for e in ENGS}
        self.known = {e: {} for e in ENGS}
        self.cnt = {e: 0 for e in ENGS}
        self.esem = {e: st.enter_context(nc.semaphore("s_" + e)) for e in ENGS}
        self.dsem = {}
        for e in ("sp", "act", "pool"):
            for s in range(DMA_RING):
                self.dsem[(e, s)] = st.enter_context(nc.semaphore("d_%s_%d" % (e, s)))
        self.barrier = set()

    def op(self, eng, fn, r=(), w=(), dma=False):
        pr = [k for k in r if isinstance(k, str) and len(k) == 3 and k.startswith("ps")]
        if pr:
            r = [k for k in r if k not in pr]
            w = list(w) + pr
        deps = set()
        for k in r:
            if k in self.last_w:
                deps.add(self.last_w[k])
        for k in w:
            if k in self.last_w:
                deps.add(self.last_w[k])
            for rd in self.readers.get(k, {}).values():
                deps.add(rd)
        idx = len(self.ops)
        self.ops.append(dict(eng=eng, fn=fn, deps=deps, dma=dma, inc=False))
        for k in w:
            self.last_w[k] = idx
            self.readers[k] = {}
        for k in r:
            d = self.readers.setdefault(k, {})
            d[("dma", idx) if dma else eng] = idx
        return idx

    def dma(self, eng, out, in_, r=(), w=(), **kw):
        return self.op(eng, lambda e: e.dma_start(out=out, in_=in_, **kw), r=r, w=w, dma=True)

    def flush(self):
        nc = self.nc
        ops = self.ops
        start = self.start
        dma_ops = [i for i in range(start, len(ops)) if ops[i]["dma"]]
        if dma_ops:
            idx = len(ops)
            ops.append(dict(eng="sp", fn=None, deps=set(dma_ops), dma=False, inc=False))
        end = len(ops)
        first_of = {}
        last_of = {}
        for i in range(start, end):
            E = ops[i]["eng"]
            first_of.setdefault(E, i)
            if ops[i]["fn"] is not None and not ops[i]["dma"]:
                last_of[E] = i
        for E, i in first_of.items():
            ops[i]["deps"] |= self.barrier
        for E, i in last_of.items():
            ops[i]["inc"] = True

        def resolve(d):
            p = ops[d]
            if d >= start or p["inc"]:
                return d
            j = d + 1
            while not (ops[j]["eng"] == p["eng"] and ops[j]["inc"]):
                j += 1
            return j

        for i in range(start, end):
            o = ops[i]
            E = o["eng"]
            waits_c = {}
            waits_d = {}
            for d in o["deps"]:
                if d == i:
                    continue
                p = ops[d]
                if p["dma"]:
                    key = (p["eng"], p["slot"])
                    waits_d[key] = max(waits_d.get(key, 0), p["val"])
                else:
                    if p["fn"] is None:
                        for dd in p["deps"]:
                            pp = ops[dd]
                            if pp["dma"]:
                                key = (pp["eng"], pp["slot"])
                                waits_d[key] = max(waits_d.get(key, 0), pp["val"])
                        continue
                    if p["eng"] == E and E == "pe":
                        continue
                    d = resolve(d)
                    waits_c[p["eng"]] = max(waits_c.get(p["eng"], -1), d)
            if o["dma"]:
                n = self.ring_n[E]
                self.ring_n[E] += 1
                o["slot"] = n % DMA_RING
                o["val"] = 16 * (n // DMA_RING + 1)
                if n >= DMA_RING:
                    key = (E, o["slot"])
                    waits_d[key] = max(waits_d.get(key, 0), o["val"] - 16)
            wc = []
            for pe_, d in waits_c.items():
                if self.known[E].get(pe_, -1) >= d:
                    continue
                self.known[E][pe_] = d
                ops[d]["inc"] = True
                wc.append(d)
            wd = []
            for key, v in waits_d.items():
                if self.known[E].get(key, 0) >= v:
                    continue
                self.known[E][key] = v
                wd.append((key, v))
            o["wc"] = wc
            o["wd"] = wd
        for i in range(start, end):
            o = ops[i]
            if o["inc"] and "cval" not in o:
                self.cnt[o["eng"]] += 1
                o["cval"] = self.cnt[o["eng"]]
        esem, dsem = self.esem, self.dsem
        with nc.Block() as block:
            def body(E):
                def f(eng):
                    for i in range(start, end):
                        o = ops[i]
                        if o["eng"] != E:
                            continue
                        for d in o["wc"]:
                            p = ops[d]
                            eng.wait_ge(esem[p["eng"]], p["cval"])
                        for key, v in o["wd"]:
                            eng.wait_ge(dsem[key], v)
                        if o["fn"] is None:
                            continue
                        ins = o["fn"](eng)
                        if o["dma"]:
                            ins.then_inc(dsem[(E, o["slot"])], 16)
                        elif o["inc"]:
                            ins.then_inc(esem[E], 1)
                return f

            block.tensor(body("pe"))
            block.scalar(body("act"))
            block.vector(body("dve"))
            block.gpsimd(body("pool"))
            block.sync(body("sp"))
        self.barrier = set(last_of.values())
        if dma_ops:
            self.barrier.add(idx)
        self.start = end
        for i in range(start, end):
            ops[i]["fn"] = ops[i]["fn"] is not None and True or None
        self.last_w = {}
        self.readers = {}
        return dict(n_ops=end - start, cnt=dict(self.cnt))


NT_LAT = 16
NT_CTX = 2
NT = NT_LAT + NT_CTX
NTOK = NT * 128
NLAT = NT_LAT * 128
D = 1024
DFF = 2816
NF = 22
EPS = 1e-6


class KB:
    def __init__(self, nc, st):
        self.nc = nc
        self.S = Sched(nc, st)
        self.ps = [st.enter_context(nc.psum_tensor("ps%d" % i, [128, 512], F32)) for i in range(8)]
        self.ident = st.enter_context(nc.sbuf_tensor("ident", [128, 128], BF16))
        self.identf = st.enter_context(nc.sbuf_tensor("identf", [128, 128], F32))
        self.ones_f = st.enter_context(nc.sbuf_tensor("ones_f", [128, 128], F32))
        S = self.S
        for t, k in ((self.ident, "ident"), (self.identf, "identf")):
            S.op("pool", lambda e, t=t: e.memset(t[:], 0.0), w=[k])
            S.op("pool", lambda e, t=t: e.affine_select(out=t[:], in_=t[:], pattern=[[-1, 128]],
                                                        compare_op=ALU.not_equal, fill=1.0, base=0,
                                                        channel_multiplier=1), r=[k], w=[k])
        S.op("pool", lambda e: e.memset(self.ones_f[:], 1.0), w=["ones_f"])
        self.dram_n = 0

    def dram(self, name, shape, dt, kind="Internal"):
        return self.nc.dram_tensor(name, list(shape), dt, kind=kind).ap()


def group_list():
    import os
    g = [(4 * i, 4, 0) for i in range(NT_LAT // 4)]
    g.append((NT_LAT, NT_CTX, 1))
    ng = int(os.environ.get("NGROUPS", "99"))
    return g[:ng]


def rstd_ops(S, ss, nt, keys, mean_div):
    S.op("dve", lambda e: e.tensor_scalar(out=ss[:, 0:nt], in0=ss[:, 0:nt], scalar1=1.0 / mean_div, scalar2=EPS,
                                          op0=ALU.mult, op1=ALU.add), r=keys, w=keys)
    S.op("act", lambda e: e.activation(out=ss[:, 0:nt], in_=ss[:, 0:nt], func=AF.Sqrt), r=keys, w=keys)
    S.op("dve", lambda e: e.reciprocal(out=ss[:, 0:nt], in_=ss[:, 0:nt]), r=keys, w=keys)


def norm_group(kb, tag, xts, xkeys, ss, junk, xn, hT, sc1, sh, strm, psb_i, hkey):
    S = kb.S
    import os
    NP = int(os.environ.get("NORM_PARTS", "15"))
    nt = len(xts)
    sskey = tag + "ss"
    for t in range(nt if NP & 1 else 0):
        S.op("act", lambda e, t=t: e.activation(out=junk[:], in_=xts[t][:], func=AF.Square,
                                                accum_out=ss[:, t:t + 1]), r=[xkeys[t]], w=[sskey + str(t)])
    allss = [sskey + str(t) for t in range(nt)]
    if NP & 1:
        rstd_ops(S, ss, nt, allss, D)
    psbs = [kb.ps[i][:].bitcast(BF16) for i in psb_i]
    pkeys = ["ps%d" % i for i in psb_i]
    for t in range(nt if NP & 2 else 0):
        b = t % 2
        S.op("act", lambda e, t=t, b=b: e.activation(out=xn[b][:], in_=xts[t][:], func=AF.Copy,
                                                     scale=ss[:, t:t + 1]),
             r=[xkeys[t]] + allss, w=[tag + "xn%d" % b])
        for kc in range(8 if NP & 4 else 0):
            psb, pkey = psbs[kc // 4], pkeys[kc // 4]
            S.op("pe", lambda e, kc=kc, b=b, psb=psb: e.transpose(out=psb[:, kc * 128:(kc + 1) * 128],
                                                         in_=xn[b][:, kc * 128:(kc + 1) * 128],
                                                         identity=kb.ident[:]),
                 r=[tag + "xn%d" % b, "ident"], w=[pkey])
        for kc in range(8 if NP & 8 else 0):
            eng = "dve" if kc >= 4 else "act"
            psb, pkey = psbs[kc // 4], pkeys[kc // 4]
            if eng == "dve":
                S.op("dve", lambda e, kc=kc, t=t, psb=psb: e.tensor_scalar(
                    out=hT[:, kc, t * 128:(t + 1) * 128], in0=psb[:, kc * 128:(kc + 1) * 128],
                    scalar1=sc1[:, strm, kc:kc + 1], scalar2=sh[:, strm, kc:kc + 1],
                    op0=ALU.mult, op1=ALU.add), r=[pkey, tag + "modc"], w=[hkey + "_%d_%d" % (t, kc)])
            else:
                S.op("act", lambda e, kc=kc, t=t, psb=psb: e.activation(
                    out=hT[:, kc, t * 128:(t + 1) * 128], in_=psb[:, kc * 128:(kc + 1) * 128],
                    func=AF.Identity, scale=sc1[:, strm, kc:kc + 1], bias=sh[:, strm, kc:kc + 1]),
                    r=[pkey, tag + "modc"], w=[hkey + "_%d_%d" % (t, kc)])
    return [hkey + "_%d_%d" % (t, kc) for t in range(nt) for kc in range(8)]


def load_modc(kb, st, tag, sh_d, sc_d):
    nc, S = kb.nc, kb.S
    sh = st.enter_context(nc.sbuf_tensor(tag + "sh", [128, 2, 8], F32))
    sc1 = st.enter_context(nc.sbuf_tensor(tag + "sc1", [128, 2, 8], F32))
    S.dma("sp", sh[:], sh_d, w=[tag + "modc_a"])
    S.dma("sp", sc1[:], sc_d, w=[tag + "modc_b"])
    S.op("dve", lambda e: e.tensor_scalar(out=sc1[:], in0=sc1[:], scalar1=1.0, scalar2=None, op0=ALU.add),
         r=[tag + "modc_a", tag + "modc_b"], w=[tag + "modc"])
    return sh, sc1


def ffn_phase(kb, tag, x_in, x_out, wg_d, wu_d, wd_d, sh_d, sc_d, g_d, dbg=99, latent_only=False):
    nc, S = kb.nc, kb.S
    with contextlib.ExitStack() as st:
        sb = lambda n, s, d: st.enter_context(nc.sbuf_tensor(tag + n, s, d))
        wg = sb("wg", [128, 8, DFF], BF16)
        wu = sb("wu", [128, 8, DFF], BF16)
        wd = sb("wd", [128, NF, D], BF16)
        G = sb("G", [128, 2, D], F32)
        NXB = 5
        xt = [sb("xt%d" % i, [128, D], F32) for i in range(NXB)]
        xn = [sb("xn%d" % i, [128, D], BF16) for i in range(2)]
        junk = sb("junk", [128, D], BF16)
        ss = sb("ss", [128, 4], F32)
        hT = sb("hT", [128, 8, 512], BF16)
        aT = sb("aT", [128, NF, 512], BF16)
        sg = [sb("sg%d" % i, [128, 512], F32) for i in range(2)]
        tmp = [sb("tmp%d" % i, [128, 512], F32) for i in range(2)]
        sh, sc1 = load_modc(kb, st, tag, sh_d, sc_d)
        S.dma("sp", G[:], g_d, w=[tag + "G0"])
        S.op("pool", lambda e: e.tensor_scalar(out=G[:], in0=G[:], scalar1=0.5, scalar2=0.0, op0=ALU.mult, op1=ALU.add),
             r=[tag + "G0"], w=[tag + "G"])
        wg_v = wg_d.rearrange("(kc p) f -> p kc f", p=128)
        wu_v = wu_d.rearrange("(kc p) f -> p kc f", p=128)
        wd_v = wd_d.rearrange("(f p) d -> p f d", p=128)
        nblk = (DFF + 511) // 512
        for b in range(nblk if dbg >= -1 else 0):
            c0, c1 = b * 512, min(DFF, (b + 1) * 512)
            S.dma("pool", wg[:, :, c0:c1], wg_v[:, :, c0:c1], w=[tag + "wg%d" % b])
            S.dma("pool", wu[:, :, c0:c1], wu_v[:, :, c0:c1], w=[tag + "wu%d" % b])
        WDG = 4
        for b in range((NF + WDG - 1) // WDG if dbg >= -2 else 0):
            f0, f1 = b * WDG, min(NF, (b + 1) * WDG)
            S.dma("pool", wd[:, f0:f1, :], wd_v[:, f0:f1, :], w=[tag + "wd%d" % b])
        groups = group_list()
        if latent_only:
            groups = [g for g in groups if g[2] == 0]
        xkey = lambda t: tag + "x%d" % (t % NXB)

        def load_x(t):
            S.dma("sp", xt[t % NXB][:], x_in[t * 128:(t + 1) * 128, :], w=[xkey(t)])

        loaded = 0
        for gi, (t0, nt, strm) in enumerate(groups):
            while loaded < min(groups[-1][0] + groups[-1][1], t0 + NXB):
                load_x(loaded)
                loaded += 1
            ntok = nt * 128
            xts = [xt[(t0 + t) % NXB] for t in range(nt)]
            xkeys = [xkey(t0 + t) for t in range(nt)]
            if dbg >= 1:
                hkeys = norm_group(kb, tag, xts, xkeys, ss, junk, xn, hT, sc1, sh, strm, (0, 7), tag + "hT")
            for f in range(NF if dbg >= 2 else 0):
                pg, pu = kb.ps[1 + f % 2], kb.ps[3 + f % 2]
                kg, ku = "ps%d" % (1 + f % 2), "ps%d" % (3 + f % 2)
                blk = (f * 128) // 512
                for kc in range(8):
                    S.op("pe", lambda e, kc=kc, f=f, pg=pg, ntok=ntok: e.matmul(
                        pg[:, 0:ntok], lhsT=wg[:, kc, f * 128:(f + 1) * 128], rhs=hT[:, kc, 0:ntok],
                        start=(kc == 0), stop=(kc == 7)),
                        r=[tag + "wg%d" % blk] + [tag + "hT_%d_%d" % (t, kc) for t in range(nt)], w=[kg])
                for kc in range(8):
                    S.op("pe", lambda e, kc=kc, f=f, pu=pu, ntok=ntok: e.matmul(
                        pu[:, 0:ntok], lhsT=wu[:, kc, f * 128:(f + 1) * 128], rhs=hT[:, kc, 0:ntok],
                        start=(kc == 0), stop=(kc == 7)),
                        r=[tag + "wu%d" % blk] + [tag + "hT_%d_%d" % (t, kc) for t in range(nt)], w=[ku])
                S.op("act", lambda e, f=f, pg=pg, ntok=ntok: e.activation(out=sg[f % 2][:, 0:ntok], in_=pg[:, 0:ntok],
                                                               func=AF.Silu), r=[kg], w=[tag + "sg%d" % (f % 2)])
                S.op("dve", lambda e, f=f, pu=pu, ntok=ntok: e.tensor_tensor(out=aT[:, f, 0:ntok], in0=pu[:, 0:ntok],
                                                                  in1=sg[f % 2][:, 0:ntok], op=ALU.mult),
                     r=[ku, tag + "sg%d" % (f % 2)], w=[tag + "aT%d" % f])
            for t in range(nt):
                for dh in range(2 if dbg >= 3 else 0):
                    py = kb.ps[5 + dh]
                    ky = "ps%d" % (5 + dh)
                    for f in range(NF):
                        S.op("pe", lambda e, f=f, t=t, dh=dh, py=py: e.matmul(
                            py[:, 0:512], lhsT=aT[:, f, t * 128:(t + 1) * 128], rhs=wd[:, f, dh * 512:(dh + 1) * 512],
                            start=(f == 0), stop=(f == NF - 1)),
                            r=[tag + "aT%d" % f, tag + "wd%d" % (f // WDG)], w=[ky])
                    S.op("dve", lambda e, dh=dh, py=py, strm=strm: e.tensor_tensor(
                        out=tmp[dh][:], in0=py[:, 0:512], in1=G[:, strm, dh * 512:(dh + 1) * 512], op=ALU.mult),
                        r=[ky, tag + "G"], w=[tag + "tmp%d" % dh])
                    S.op("pool", lambda e, dh=dh, t=t, xts=xts: e.tensor_tensor(
                        out=xts[t][:, dh * 512:(dh + 1) * 512], in0=xts[t][:, dh * 512:(dh + 1) * 512],
                        in1=tmp[dh][:], op=ALU.add),
                        r=[tag + "tmp%d" % dh, xkeys[t]], w=[xkeys[t]])
                S.dma("sp", x_out[(t0 + t) * 128:(t0 + t + 1) * 128, :], xts[t][:], r=[xkeys[t]], w=[tag + "xo%d" % (t0 + t)])
        return S.flush()


def mod_phase(kb, csT_d, wm_d, bm_d, mod_d):
    nc, S = kb.nc, kb.S
    NCOL = 1152
    with contextlib.ExitStack() as st:
        sb = lambda n, s, d: st.enter_context(nc.sbuf_tensor("m_" + n, s, d))
        cs = sb("cs", [128, 8, 3], F32)
        w = [sb("w%d" % l, [128, 8, NCOL], F32) for l in range(2)]
        bm = sb("bm", [3, 2, NCOL], F32)
        res = sb("res", [3, 2, NCOL], F32)
        S.dma("sp", cs[:], csT_d, w=["m_cs"])
        S.dma("sp", bm[:], bm_d, w=["m_bm"])
        for l in range(2):
            S.dma("sp" if l == 0 else "act", w[l][:], wm_d[l].rearrange("(kc p) f -> p kc f", p=128), w=["m_w%d" % l])
        S.op("act", lambda e: e.activation(out=cs[:], in_=cs[:], func=AF.Silu), r=["m_cs"], w=["m_cs"])
        i = 0
        for l in range(2):
            for c0 in range(0, NCOL, 512):
                c1 = min(NCOL, c0 + 512)
                p = kb.ps[i % 8]
                pk = "ps%d" % (i % 8)
                i += 1
                for kc in range(8):
                    S.op("pe", lambda e, kc=kc, l=l, c0=c0, c1=c1, p=p: e.matmul(
                        p[0:3, 0:c1 - c0], lhsT=cs[:, kc, :], rhs=w[l][:, kc, c0:c1], start=(kc == 0), stop=(kc == 7)),
                        r=["m_cs", "m_w%d" % l], w=[pk])
                S.op("dve", lambda e, l=l, c0=c0, c1=c1, p=p: e.tensor_tensor(
                    out=res[:, l, c0:c1], in0=p[0:3, 0:c1 - c0], in1=bm[:, l, c0:c1], op=ALU.add),
                    r=[pk, "m_bm"], w=["m_res"])
        S.dma("sp", mod_d, res[:], r=["m_res"], w=["m_out"])
        return S.flush()


def inproj0_phase(kb, tag, x_d, win_d, sh_d, sc_d, gains_d, cos_d, sin_d, qT_d, kT_d, va_d, f_d):
    nc, S = kb.nc, kb.S
    with contextlib.ExitStack() as st:
        sb = lambda n, s, d: st.enter_context(nc.sbuf_tensor(tag + n, s, d))
        win = sb("win", [128, 8, 1536], BF16)
        gains = sb("gains", [128, 1024], F32)
        cosb = sb("cos", [128, NT_LAT, 32], F32)
        sinb = sb("sin", [128, NT_LAT, 32], F32)
        NXB = 8
        xt = [sb("xt%d" % i, [128, D], F32) for i in range(NXB)]
        xn = [sb("xn%d" % i, [128, D], BF16) for i in range(2)]
        junk = sb("junk", [128, D], BF16)
        ss = sb("ss", [128, 4], F32)
        hT = sb("hT", [128, 8, 512], BF16)
        qk = [sb("qk%d" % i, [128, 1024], F32) for i in range(2)]
        sq = sb("sq", [128, 1024], F32)
        ssq = [sb("ssq%d" % i, [128, 16], F32) for i in range(2)]
        ta = [sb("ta%d" % i, [128, 16, 32], F32) for i in range(4)]
        qkr = [sb("qkr%d" % i, [128, 1024], BF16) for i in range(2)]
        qkT = [sb("qkT%d" % i, [64, 16, 512], BF16) for i in range(2)]
        vab = [sb("vab%d" % i, [128, 4, 128], BF16) for i in range(2)]
        fb_ = [sb("fb%d" % i, [128, 256], BF16) for i in range(2)]
        for i in range(2):
            S.op("pool", lambda e, i=i: e.memset(vab[i][:], 1.0), w=[tag + "vab%d" % i])
        sh, sc1 = load_modc(kb, st, tag, sh_d, sc_d)
        S.dma("sp", gains[:], gains_d, w=[tag + "gains"])
        S.dma("sp", cosb[:], cos_d.rearrange("(t p) j -> p t j", p=128), w=[tag + "cos"])
        S.dma("sp", sinb[:], sin_d.rearrange("(t p) j -> p t j", p=128), w=[tag + "sin"])
        win_v = win_d.rearrange("(kc p) f -> p kc f", p=128)
        for b in range(3):
            S.dma("pool", win[:, :, b * 512:(b + 1) * 512], win_v[:, :, b * 512:(b + 1) * 512], w=[tag + "win%d" % b])
        xkey = lambda t: tag + "x%d" % (t % NXB)
        loaded = 0
        groups = group_list()
        for gi, (t0, nt, strm) in enumerate(groups):
            while loaded < min(groups[-1][0] + groups[-1][1], t0 + NXB):
                S.dma("sp", xt[loaded % NXB][:], x_d[loaded * 128:(loaded + 1) * 128, :], w=[xkey(loaded)])
                loaded += 1
            ntok = nt * 128
            xts = [xt[(t0 + t) % NXB] for t in range(nt)]
            xkeys = [xkey(t0 + t) for t in range(nt)]
            norm_group(kb, tag, xts, xkeys, ss, junk, xn, hT, sc1, sh, strm, (0, 7), tag + "hT")
            qT_g = qkT[gi % 2]
            gk = tag + "qkT%d" % (gi % 2)
            for t in range(nt):
                tt = t0 + t
                b = tt % 2
                for c in range(3):
                    p = kb.ps[1 + c]
                    for kc in range(8):
                        S.op("pe", lambda e, kc=kc, c=c, t=t, p=p: e.matmul(
                            p[:, 0:512], lhsT=hT[:, kc, t * 128:(t + 1) * 128], rhs=win[:, kc, c * 512:(c + 1) * 512],
                            start=(kc == 0), stop=(kc == 7)),
                            r=[tag + "hT_%d_%d" % (t, kc), tag + "win%d" % c], w=["ps%d" % (1 + c)])
                qkb, qkk = qk[b], tag + "qk%d" % b
                S.op("act", lambda e, qkb=qkb: e.activation(out=qkb[:, 0:512], in_=kb.ps[1][:, 0:512], func=AF.Copy),
                     r=["ps1"], w=[qkk + "a"])
                S.op("dve", lambda e, qkb=qkb: e.tensor_copy(out=qkb[:, 512:1024], in_=kb.ps[2][:, 0:512]),
                     r=["ps2"], w=[qkk + "b"])
                S.op("act", lambda e, b=b: e.activation(
                    out=vab[b][:, :, 0:64], in_=kb.ps[3][:, 0:256].rearrange("p (k d) -> p k d", d=64), func=AF.Copy),
                    r=["ps3"], w=[tag + "vab%d" % b])
                S.op("act", lambda e, b=b: e.activation(out=fb_[b][:], in_=kb.ps[3][:, 256:512], func=AF.Copy),
                     r=["ps3"], w=[tag + "fb%d" % b])
                S.dma("sp", va_d[tt * 128:(tt + 1) * 128, :], vab[b][:].rearrange("p k d -> p (k d)"),
                      r=[tag + "vab%d" % b], w=[tag + "vao%d" % tt])
                S.dma("sp", f_d[tt * 128:(tt + 1) * 128, :], fb_[b][:], r=[tag + "fb%d" % b], w=[tag + "fo%d" % tt])
                S.op("dve", lambda e, qkb=qkb: e.tensor_tensor(out=sq[:], in0=qkb[:], in1=qkb[:], op=ALU.mult),
                     r=[qkk + "a", qkk + "b"], w=[tag + "sq"])
                sk = tag + "ssq%d" % b
                S.op("dve", lambda e, b=b: e.tensor_reduce(out=ssq[b][:], in_=sq[:].rearrange("p (h d) -> p h d", d=64),
                                                           axis=AX.X, op=ALU.add), r=[tag + "sq"], w=[sk])
                rstd_ops(S, ssq[b], 16, [sk], 64)
                qk3 = qkb[:].rearrange("p (h d) -> p h d", d=64)
                S.op("dve", lambda e, qk3=qk3, b=b: e.tensor_tensor(
                    out=qk3, in0=qk3, in1=ssq[b][:].unsqueeze(2).to_broadcast([128, 16, 64]), op=ALU.mult),
                    r=[qkk + "a", qkk + "b", sk], w=[qkk])
                S.op("pool", lambda e, qkb=qkb: e.tensor_tensor(out=qkb[:], in0=qkb[:], in1=gains[:], op=ALU.mult),
                     r=[qkk, tag + "gains"], w=[qkk])
                rk = tag + "qkr%d" % b
                r3 = qkr[b][:].rearrange("p (h d) -> p h d", d=64)
                if strm == 0:
                    X1, X2 = qk3[:, :, 0:32], qk3[:, :, 32:64]
                    cb = cosb[:, tt, :].unsqueeze(1).to_broadcast([128, 16, 32])
                    sb_ = sinb[:, tt, :].unsqueeze(1).to_broadcast([128, 16, 32])
                    tk = [tag + "ta%d" % i for i in range(4)]
                    S.op("dve", lambda e, X1=X1, cb=cb: e.tensor_tensor(out=ta[0][:], in0=X1, in1=cb, op=ALU.mult),
                         r=[qkk, tag + "cos"], w=[tk[0]])
                    S.op("dve", lambda e, X2=X2, sb_=sb_: e.tensor_tensor(out=ta[1][:], in0=X2, in1=sb_, op=ALU.mult),
                         r=[qkk, tag + "sin"], w=[tk[1]])
                    S.op("dve", lambda e, r3=r3: e.tensor_tensor(out=r3[:, :, 0:32], in0=ta[0][:], in1=ta[1][:],
                                                                 op=ALU.subtract), r=[tk[0], tk[1]], w=[rk + "a"])
                    S.op("pool", lambda e, X2=X2, cb=cb: e.tensor_tensor(out=ta[2][:], in0=X2, in1=cb, op=ALU.mult),
                         r=[qkk, tag + "cos"], w=[tk[2]])
                    S.op("pool", lambda e, X1=X1, sb_=sb_: e.tensor_tensor(out=ta[3][:], in0=X1, in1=sb_, op=ALU.mult),
                         r=[qkk, tag + "sin"], w=[tk[3]])
                    S.op("pool", lambda e, r3=r3: e.tensor_tensor(out=r3[:, :, 32:64], in0=ta[2][:], in1=ta[3][:],
                                                                  op=ALU.add), r=[tk[2], tk[3]], w=[rk + "b"])
                else:
                    S.op("dve", lambda e, qkb=qkb, b=b: e.tensor_copy(out=qkr[b][:], in_=qkb[:]),
                         r=[qkk], w=[rk + "a", rk + "b"])
                for hb in range(2):
                    pT = kb.ps[4 + hb][:].bitcast(BF16)
                    pk = "ps%d" % (4 + hb)
                    for hh in range(8):
                        h = hb * 8 + hh
                        S.op("pe", lambda e, h=h, hh=hh, pT=pT, b=b: e.transpose(
                            out=pT[0:64, hh * 128:(hh + 1) * 128], in_=qkr[b][:, h * 64:(h + 1) * 64],
                            identity=kb.ident[:]), r=[rk + "a", rk + "b", "ident"], w=[pk])
                    src = pT[0:64, :].rearrange("p (h t) -> p h t", h=8)
                    dst = qT_g[:, hb * 8:(hb + 1) * 8, t * 128:(t + 1) * 128]
                    if hb == 0:
                        S.op("act", lambda e, src=src, dst=dst: e.activation(out=dst, in_=src, func=AF.Copy),
                             r=[pk], w=[gk + "_%d_%d" % (t, hb)])
                    else:
                        S.op("dve", lambda e, src=src, dst=dst: e.tensor_copy(out=dst, in_=src),
                             r=[pk], w=[gk + "_%d_%d" % (t, hb)])
            gks = [gk + "_%d_%d" % (t, hb) for t in range(nt) for hb in range(2)]
            tok0 = t0 * 128
            S.dma("sp", qT_d[:, :, tok0:tok0 + ntok], qT_g[:, 0:12, 0:ntok], r=gks, w=[tag + "qo%d" % gi])
            S.dma("sp", kT_d[:, :, tok0:tok0 + ntok], qT_g[:, 12:16, 0:ntok], r=gks, w=[tag + "ko%d" % gi])
        return S.flush()


def fourier_phase(kb, tag, catT, f_all_d, fc_d, tabs):
    nc, S = kb.nc, kb.S
    with contextlib.ExitStack() as st:
        sb = lambda n, s, d: st.enter_context(nc.sbuf_tensor(tag + n, s, d))
        xs = sb("xs", [128, 64, 256], BF16)
        A = [sb("Are", [128, 128, 128], BF16), sb("Aim", [128, 128, 128], BF16)]
        c128 = sb("c128", [128, 128], BF16)
        ns128 = sb("ns128", [128, 128], BF16)
        tw = {n: sb(n, [128, 128, 32], BF16) for n in ("twc", "tws", "twns")}
        dd = {n: sb(n, [128, 2, 256], BF16) for n in ("dc", "ds")}
        O = [sb("Ore", [128, 128, 2, 16], BF16), sb("Oim", [128, 128, 2, 16], BF16)]
        S.dma("sp", xs[:].rearrange("p k f -> p (k f)"), f_all_d.rearrange("(a b) f -> a (b f)", b=64), w=[tag + "xs"])
        S.dma("sp", c128[:], tabs["c128"], w=[tag + "c128"])
        S.dma("sp", ns128[:], tabs["ns128"], w=[tag + "ns128"])
        for n in tw:
            S.dma("act", tw[n][:], tabs[n], w=[tag + n])
        for n in dd:
            S.dma("act", dd[n][:], tabs[n], w=[tag + n])
        tabA = [(c128, tag + "c128"), (ns128, tag + "ns128")]
        for fb in range(32):
            for ri in range(2):
                p = kb.ps[ri * 2 + fb % 2]
                pk = "ps%d" % (ri * 2 + fb % 2)
                for i in range(4):
                    fp = fb * 4 + i
                    for f2 in range(2):
                        lhsT = xs[:, :, 2 * fp + f2]
                        S.op("pe", lambda e, lhsT=lhsT, p=p, i=i, ri=ri, f2=f2: e.matmul(
                            p[f2 * 64:(f2 + 1) * 64, i * 128:(i + 1) * 128], lhsT=lhsT, rhs=tabA[ri][0][:],
                            start=True, stop=True),
                            r=[tag + "xs", tabA[ri][1]], w=[pk])
                dst = A[ri][:, fb * 4:(fb + 1) * 4, :]
                src = p[:, 0:512].rearrange("p (a n) -> p a n", a=4)
                if ri == 0:
                    S.op("act", lambda e, dst=dst, src=src: e.activation(out=dst, in_=src, func=AF.Copy),
                         r=[pk], w=[tag + "A%d_%d" % (ri, fb)])
                else:
                    S.op("dve", lambda e, dst=dst, src=src: e.tensor_copy(out=dst, in_=src),
                         r=[pk], w=[tag + "A%d_%d" % (ri, fb)])
        Akeys = [[tag + "A%d_%d" % (ri, fb) for fb in range(32)] for ri in range(2)]
        for nb in range(8):
            for ri in range(2):
                p = kb.ps[4 + ri * 2 + nb % 2]
                pk = "ps%d" % (4 + ri * 2 + nb % 2)
                for i in range(16):
                    n1 = nb * 16 + i
                    if ri == 0:
                        terms = [(A[0], "twc", 0), (A[1], "tws", 1)]
                    else:
                        terms = [(A[1], "twc", 1), (A[0], "twns", 0)]
                    for ti, (At, tn, ai) in enumerate(terms):
                        S.op("pe", lambda e, At=At, tn=tn, n1=n1, p=p, i=i, ti=ti: e.matmul(
                            p[:, i * 32:(i + 1) * 32], lhsT=At[:, :, n1], rhs=tw[tn][:, n1, :],
                            start=(ti == 0), stop=(ti == 1)),
                            r=Akeys[ai] + [tag + tn], w=[pk])
                dst = O[ri][:, nb * 16:(nb + 1) * 16, :, :]
                src = p[:, 0:512].rearrange("p (a f n) -> p a f n", a=16, f=2)
                if ri == 0:
                    S.op("act", lambda e, dst=dst, src=src: e.activation(out=dst, in_=src, func=AF.Copy),
                         r=[pk], w=[tag + "O%d_%d" % (ri, nb)])
                else:
                    S.op("dve", lambda e, dst=dst, src=src: e.tensor_copy(out=dst, in_=src),
                         r=[pk], w=[tag + "O%d_%d" % (ri, nb)])
        Okeys = [[tag + "O%d_%d" % (ri, nb) for nb in range(8)] for ri in range(2)]
        for ch in range(2):
            for nb in range(4):
                p = kb.ps[(ch * 4 + nb) % 4]
                pk = "ps%d" % ((ch * 4 + nb) % 4)
                k = 0
                for f2 in range(2):
                    for ri, dn in ((0, "dc"), (1, "ds")):
                        rhs = O[ri][:, :, f2, nb * 4:(nb + 1) * 4].rearrange("p n a -> p a n")
                        S.op("pe", lambda e, rhs=rhs, dn=dn, f2=f2, ch=ch, p=p, k=k: e.matmul(
                            p[:, 0:512], lhsT=dd[dn][:, f2, ch * 128:(ch + 1) * 128], rhs=rhs,
                            start=(k == 0), stop=(k == 3)),
                            r=Okeys[ri] + [tag + dn], w=[pk])
                        k += 1
                dst = catT[:, 6 + ch, nb * 512:(nb + 1) * 512]
                if nb % 2 == 0:
                    S.op("act", lambda e, dst=dst, p=p: e.activation(out=dst, in_=p[:, 0:512], func=AF.Copy),
                         r=[pk], w=[tag + "cat%d_%d" % (ch, nb)])
                else:
                    S.op("dve", lambda e, dst=dst, p=p: e.tensor_copy(out=dst, in_=p[:, 0:512]),
                         r=[pk], w=[tag + "cat%d_%d" % (ch, nb)])
        fcs = sb("fcs", [128, 2, 256], BF16)
        c256 = sb("c256", [128, 2, 256], BF16)
        s256 = sb("s256", [128, 2, 256], BF16)
        dcf = sb("dcf", [128, 128], BF16)
        ndsf = sb("ndsf", [128, 128], BF16)
        Z = [[sb("Z%d_%d" % (a, b), [128, 256], BF16) for b in range(2)] for a in range(2)]
        S.dma("sp", fcs[:], fc_d.rearrange("(kc p) f -> p kc f", p=128), w=[tag + "fcs"])
        S.dma("sp", c256[:], tabs["c256"], w=[tag + "c256"])
        S.dma("sp", s256[:], tabs["s256"], w=[tag + "s256"])
        S.dma("sp", dcf[:], tabs["dcf"], w=[tag + "dcf"])
        S.dma("sp", ndsf[:], tabs["ndsf"], w=[tag + "ndsf"])
        for fch in range(2):
            for ti, (tb, tk) in enumerate(((c256, "c256"), (s256, "s256"))):
                p = kb.ps[4 + fch * 2 + ti]
                pk = "ps%d" % (4 + fch * 2 + ti)
                for kc in range(2):
                    S.op("pe", lambda e, fch=fch, tb=tb, kc=kc, p=p: e.matmul(
                        p[:, 0:256], lhsT=fcs[:, kc, fch * 128:(fch + 1) * 128], rhs=tb[:, kc, :],
                        start=(kc == 0), stop=(kc == 1)), r=[tag + "fcs", tag + tk], w=[pk])
                S.op("dve" if ti else "act",
                     (lambda e, fch=fch, ti=ti, p=p: e.tensor_copy(out=Z[fch][ti][:], in_=p[:, 0:256])) if ti else
                     (lambda e, fch=fch, ti=ti, p=p: e.activation(out=Z[fch][ti][:], in_=p[:, 0:256], func=AF.Copy)),
                     r=[pk], w=[tag + "Z%d_%d" % (fch, ti)])
        for ch in range(2):
            p = kb.ps[ch]
            pk = "ps%d" % ch
            S.op("pe", lambda e, ch=ch, p=p: e.matmul(p[:, 0:256], lhsT=dcf[:], rhs=Z[ch][0][:], start=True, stop=False),
                 r=[tag + "dcf", tag + "Z%d_0" % ch], w=[pk])
            S.op("pe", lambda e, ch=ch, p=p: e.matmul(p[:, 0:256], lhsT=ndsf[:], rhs=Z[ch][1][:], start=False, stop=True),
                 r=[tag + "ndsf", tag + "Z%d_1" % ch], w=[pk])
            S.op("act", lambda e, ch=ch, p=p: e.activation(out=catT[:, 6 + ch, NLAT:NTOK], in_=p[:, 0:256], func=AF.Copy),
                 r=[pk], w=[tag + "catc%d" % ch])
        return S.flush()


def fourier_tables(j):
    import ml_dtypes
    bf = lambda a: np.ascontiguousarray(a.astype(np.float32)).astype(ml_dtypes.bfloat16)
    k1 = np.arange(128)[:, None]
    n1 = np.arange(128)[None, :]
    T = {}
    T["c128"] = bf(np.cos(2 * np.pi * k1 * n1 / 128))
    T["ns128"] = bf(-np.sin(2 * np.pi * k1 * n1 / 128))
    k2 = np.arange(64)
    n = 2048 * j + 128 * np.arange(16)[None, None, :] + np.arange(128)[None, :, None]
    th = 2 * np.pi * ((n * k2[:, None, None]) % 8192) / 8192.0
    twc = np.zeros((2, 64, 128, 2, 16))
    tws = np.zeros((2, 64, 128, 2, 16))
    for f2 in range(2):
        twc[f2, :, :, f2, :] = np.cos(th)
        tws[f2, :, :, f2, :] = np.sin(th)
    T["twc"] = bf(twc.reshape(128, 128, 32))
    T["tws"] = bf(tws.reshape(128, 128, 32))
    T["twns"] = bf(-tws.reshape(128, 128, 32))
    sc = 1.0 / np.sqrt(8192.0 * 64.0)
    dc = np.zeros((4, 32, 2, 4, 64))
    ds = np.zeros((4, 32, 2, 4, 64))
    m = np.arange(64)[None, :]
    for f2 in range(2):
        c = (2 * np.arange(32) + f2)[:, None]
        for g in range(4):
            dc[g, :, f2, g, :] = np.cos(2 * np.pi * m * c / 64) * sc
            ds[g, :, f2, g, :] = np.sin(2 * np.pi * m * c / 64) * sc
    T["dc"] = bf(dc.reshape(128, 2, 256))
    T["ds"] = bf(ds.reshape(128, 2, 256))
    kk = np.arange(256)[:, None]
    nn = np.arange(256)[None, :]
    c256 = np.cos(2 * np.pi * kk * nn / 256).reshape(2, 128, 256).transpose(1, 0, 2)
    s256 = np.sin(2 * np.pi * kk * nn / 256).reshape(2, 128, 256).transpose(1, 0, 2)
    T["c256"] = bf(c256)
    T["s256"] = bf(s256)
    scc = 1.0 / np.sqrt(256.0 * 64.0)
    dcf = np.zeros((2, 64, 2, 64))
    dsf = np.zeros((2, 64, 2, 64))
    cc = np.arange(64)[:, None]
    for g in range(2):
        dcf[g, :, g, :] = np.cos(2 * np.pi * m * cc / 64) * scc
        dsf[g, :, g, :] = np.sin(2 * np.pi * m * cc / 64) * scc
    T["dcf"] = bf(dcf.reshape(128, 128))
    T["ndsf"] = bf(-dsf.reshape(128, 128))
    return T


FTAB_SHAPES = dict(c128=[128, 128], ns128=[128, 128], twc=[128, 128, 32], tws=[128, 128, 32], twns=[128, 128, 32],
                   dc=[128, 2, 256], ds=[128, 2, 256], c256=[128, 2, 256], s256=[128, 2, 256],
                   dcf=[128, 128], ndsf=[128, 128])


NKC = 66


def attn0_phase(kb, tag, catT, qT_d, kT_all_d, v_all_d):
    nc, S = kb.nc, kb.S
    with contextlib.ExitStack() as st:
        sb = lambda n, s, d: st.enter_context(nc.sbuf_tensor(tag + n, s, d))
        V = sb("V", [128, NKC, 512], BF16)
        kT = [sb("kT%d" % i, [64, NKC * 128], BF16) for i in range(2)]
        qTs = [sb("qT%d" % i, [64, 3, NTOK], BF16) for i in range(2)]
        pT = [sb("pT%d" % i, [128, 512], BF16) for i in range(3)]
        rinv = [sb("rinv%d" % i, [128, 512], F32) for i in range(2)]
        v_v = v_all_d.rearrange("(c p) e -> p c e", p=128)
        for i in range(3):
            S.dma("sp", V[:, i * 22:(i + 1) * 22, :], v_v[:, i * 22:(i + 1) * 22, :], w=[tag + "V%d" % i])
        blk = 0
        pending = None
        for kvh in range(4):
            kb_, kk = kT[kvh % 2], tag + "kT%d" % (kvh % 2)
            qb_, qk_ = qTs[kvh % 2], tag + "qT%d" % (kvh % 2)
            S.dma("sp", kb_[:], kT_all_d[:, kvh, :], w=[kk])
            S.dma("act", qb_[:], qT_d[:, 3 * kvh:3 * kvh + 3, :], w=[qk_])
            for g in range(3):
                h = 3 * kvh + g
                for qb in range(5):
                    if qb < 4:
                        q0, nq, chunks = qb * 512, 512, list(range(NKC))
                    else:
                        q0, nq, chunks = NLAT, 256, [0, 1]
                    po = kb.ps[3 + blk % 2]
                    pok = "ps%d" % (3 + blk % 2)
                    ri = rinv[blk % 2]
                    rik = tag + "rinv%d" % (blk % 2)
                    blk += 1

                    def emit_S(c, kb_=kb_, qb_=qb_, g=g, q0=q0, nq=nq, kk=kk, qk_=qk_):
                        S.op("pe", lambda e: e.matmul(kb.ps[c % 3][:, 0:nq], lhsT=kb_[0:64, c * 128:(c + 1) * 128],
                                                      rhs=qb_[0:64, g, q0:q0 + nq], start=True, stop=True),
                             r=[kk, qk_], w=["ps%d" % (c % 3)])

                    def emit_E(c, nq=nq):
                        S.op("act", lambda e: e.activation(out=pT[c % 3][:, 0:nq], in_=kb.ps[c % 3][:, 0:nq],
                                                           func=AF.Exp, scale=0.125),
                             r=["ps%d" % (c % 3)], w=[tag + "pT%d" % (c % 3)])

                    def emit_PV(c, first, last, po=po, pok=pok, kvh=kvh, nq=nq):
                        S.op("pe", lambda e: e.matmul(po[:, 0:nq], lhsT=V[:, c, kvh * 128:(kvh + 1) * 128],
                                                      rhs=pT[c % 3][:, 0:nq], start=first, stop=last),
                             r=[tag + "V%d" % (c // 22), tag + "pT%d" % (c % 3)], w=[pok])

                    emit_S(chunks[0])
                    if len(chunks) > 1:
                        emit_S(chunks[1])
                    for i, c in enumerate(chunks):
                        emit_E(c)
                        if i + 2 < len(chunks):
                            emit_S(chunks[i + 2])
                        emit_PV(c, i == 0, i == len(chunks) - 1)
                        if i == 1 and pending is not None:
                            pending()
                            pending = None

                    def fin(po=po, pok=pok, ri=ri, rik=rik, h=h, q0=q0, nq=nq):
                        S.op("dve", lambda e: e.reciprocal(out=ri[64:128, 0:nq], in_=po[64:128, 0:nq]), r=[pok], w=[rik])
                        pb = (h % 2) * 64
                        S.op("dve", lambda e: e.tensor_tensor(out=catT[pb:pb + 64, h // 2, q0:q0 + nq],
                                                              in0=po[0:64, 0:nq], in1=ri[64:128, 0:nq], op=ALU.mult),
                             r=[pok, rik], w=[tag + "cat_%d_%d" % (h, q0)])
                    pending = fin
        if pending is not None:
            pending()
        return S.flush()


def wout_phase(kb, tag, catT, x_in, x_out, wout_d, g_d, nt=NT):
    nc, S = kb.nc, kb.S
    with contextlib.ExitStack() as st:
        sb = lambda n, s, d: st.enter_context(nc.sbuf_tensor(tag + n, s, d))
        wo = sb("wo", [128, 8, D], BF16)
        G = sb("G", [128, 2, D], F32)
        NXB = 4
        xt = [sb("xt%d" % i, [128, D], F32) for i in range(NXB)]
        tmp = [sb("tmp%d" % i, [128, 512], F32) for i in range(2)]
        S.dma("sp", G[:], g_d, w=[tag + "G"])
        wv = wout_d.rearrange("(kc p) f -> p kc f", p=128)
        for b in range(2):
            S.dma("pool", wo[:, :, b * 512:(b + 1) * 512], wv[:, :, b * 512:(b + 1) * 512], w=[tag + "wo%d" % b])
        for t in range(nt):
            strm = 0 if t < NT_LAT else 1
            xk = tag + "x%d" % (t % NXB)
            xb = xt[t % NXB]
            S.dma("sp", xb[:], x_in[t * 128:(t + 1) * 128, :], w=[xk])
            for dh in range(2):
                p = kb.ps[(2 * t + dh) % 4]
                pk = "ps%d" % ((2 * t + dh) % 4)
                for kc in range(8):
                    S.op("pe", lambda e, kc=kc, t=t, dh=dh, p=p: e.matmul(
                        p[:, 0:512], lhsT=catT[:, kc, t * 128:(t + 1) * 128], rhs=wo[:, kc, dh * 512:(dh + 1) * 512],
                        start=(kc == 0), stop=(kc == 7)), r=[tag + "wo%d" % dh], w=[pk])
                S.op("dve", lambda e, dh=dh, p=p, strm=strm: e.tensor_tensor(
                    out=tmp[dh][:], in0=p[:, 0:512], in1=G[:, strm, dh * 512:(dh + 1) * 512], op=ALU.mult),
                    r=[pk, tag + "G"], w=[tag + "tmp%d" % dh])
                S.op("pool", lambda e, dh=dh, xb=xb: e.tensor_tensor(
                    out=xb[:, dh * 512:(dh + 1) * 512], in0=xb[:, dh * 512:(dh + 1) * 512], in1=tmp[dh][:], op=ALU.add),
                    r=[tag + "tmp%d" % dh, xk], w=[xk])
            S.dma("sp", x_out[t * 128:(t + 1) * 128, :], xb[:], r=[xk], w=[tag + "xo%d" % t])
        return S.flush()


def inproj1_phase(kb, tag, x_d, win_d, sh_d, sc_d, uT_d, qT_d, kT_d, va_d):
    nc, S = kb.nc, kb.S
    with contextlib.ExitStack() as st:
        sb = lambda n, s, d: st.enter_context(nc.sbuf_tensor(tag + n, s, d))
        win = sb("win", [128, 8, 2560], BF16)
        NXB = 8
        xt = [sb("xt%d" % i, [128, D], F32) for i in range(NXB)]
        xn = [sb("xn%d" % i, [128, D], BF16) for i in range(2)]
        junk = sb("junk", [128, D], BF16)
        ss = sb("ss", [128, 4], F32)
        hT = sb("hT", [128, 8, 512], BF16)
        sg = [sb("sg%d" % i, [128, 512], F32) for i in range(2)]
        uT = [sb("uT%d" % i, [128, 4, 512], BF16) for i in range(2)]
        qT = [sb("qT%d" % i, [128, 4, 512], BF16) for i in range(2)]
        kT = [sb("kT%d" % i, [128, 4, 512], BF16) for i in range(2)]
        vab = [sb("vab%d" % i, [128, 8, 128], BF16) for i in range(2)]
        for i in range(2):
            S.op("pool", lambda e, i=i: e.memset(vab[i][:], 1.0), w=[tag + "vab%d" % i])
        sh, sc1 = load_modc(kb, st, tag, sh_d, sc_d)
        win_v = win_d.rearrange("(kc p) f -> p kc f", p=128)
        for b in range(5):
            S.dma("pool", win[:, :, b * 512:(b + 1) * 512], win_v[:, :, b * 512:(b + 1) * 512], w=[tag + "win%d" % b])
        xkey = lambda t: tag + "x%d" % (t % NXB)
        loaded = 0
        groups = group_list()
        bank = 0
        for gi, (t0, nt, strm) in enumerate(groups):
            while loaded < min(groups[-1][0] + groups[-1][1], t0 + NXB):
                S.dma("sp", xt[loaded % NXB][:], x_d[loaded * 128:(loaded + 1) * 128, :], w=[xkey(loaded)])
                loaded += 1
            ntok = nt * 128
            tok0 = t0 * 128
            xts = [xt[(t0 + t) % NXB] for t in range(nt)]
            xkeys = [xkey(t0 + t) for t in range(nt)]
            norm_group(kb, tag, xts, xkeys, ss, junk, xn, hT, sc1, sh, strm, (0, 7), tag + "hT")
            hk = [tag + "hT_%d_%d" % (t, kc) for t in range(nt) for kc in range(8)]
            gb2 = gi % 2

            def fm(col0, ntok=ntok):
                nonlocal bank
                bi = 1 + bank % 6
                bank += 1
                p = kb.ps[bi]
                for kc in range(8):
                    S.op("pe", lambda e, kc=kc, p=p: e.matmul(
                        p[:, 0:ntok], lhsT=win[:, kc, col0:col0 + 128], rhs=hT[:, kc, 0:ntok],
                        start=(kc == 0), stop=(kc == 7)), r=hk + [tag + "win%d" % (col0 // 512)], w=["ps%d" % bi])
                return p, "ps%d" % bi

            if strm == 0:
                for c in range(4):
                    pa, pak = fm(c * 128)
                    pb, pbk = fm(512 + c * 128)
                    S.op("act", lambda e, pb=pb, c=c, ntok=ntok: e.activation(out=sg[c % 2][:, 0:ntok], in_=pb[:, 0:ntok],
                                                                              func=AF.Sigmoid),
                         r=[pbk], w=[tag + "sg%d" % (c % 2)])
                    S.op("dve", lambda e, pa=pa, c=c, ntok=ntok, gb2=gb2: e.tensor_tensor(
                        out=uT[gb2][:, c, 0:ntok], in0=pa[:, 0:ntok], in1=sg[c % 2][:, 0:ntok], op=ALU.mult),
                        r=[pak, tag + "sg%d" % (c % 2)], w=[tag + "uT%d_%d" % (gb2, c)])
                S.dma("sp", uT_d[:, :, tok0:tok0 + ntok], uT[gb2][:, :, 0:ntok],
                      r=[tag + "uT%d_%d" % (gb2, c) for c in range(4)], w=[tag + "uo%d" % gi])
                for c in range(4):
                    pq, pqk = fm(1024 + c * 128)
                    S.op("act", lambda e, pq=pq, c=c, ntok=ntok, gb2=gb2: e.activation(
                        out=qT[gb2][:, c, 0:ntok], in_=pq[:, 0:ntok], func=AF.Copy),
                        r=[pqk], w=[tag + "qT%d_%d" % (gb2, c)])
                S.dma("sp", qT_d[:, :, tok0:tok0 + ntok], qT[gb2][:, :, 0:ntok],
                      r=[tag + "qT%d_%d" % (gb2, c) for c in range(4)], w=[tag + "qo%d" % gi])
            for c in range(4):
                pk_, pkk = fm(1536 + c * 128)
                S.op("dve", lambda e, pk_=pk_, c=c, ntok=ntok, gb2=gb2: e.tensor_copy(
                    out=kT[gb2][:, c, 0:ntok], in_=pk_[:, 0:ntok]), r=[pkk], w=[tag + "kT%d_%d" % (gb2, c)])
            S.dma("sp", kT_d[:, :, tok0:tok0 + ntok], kT[gb2][:, :, 0:ntok],
                  r=[tag + "kT%d_%d" % (gb2, c) for c in range(4)], w=[tag + "ko%d" % gi])
            for t in range(nt):
                tt = t0 + t
                b = tt % 2
                bi = 1 + bank % 6
                bank += 1
                p = kb.ps[bi]
                for kc in range(8):
                    S.op("pe", lambda e, kc=kc, t=t, p=p: e.matmul(
                        p[:, 0:512], lhsT=hT[:, kc, t * 128:(t + 1) * 128], rhs=win[:, kc, 2048:2560],
                        start=(kc == 0), stop=(kc == 7)), r=hk + [tag + "win4"], w=["ps%d" % bi])
                S.op("act", lambda e, b=b, p=p: e.activation(
                    out=vab[b][:, :, 0:64], in_=p[:, 0:512].rearrange("p (k d) -> p k d", d=64), func=AF.Copy),
                    r=["ps%d" % bi], w=[tag + "vab%d" % b])
                S.dma("sp", va_d[tt * 128:(tt + 1) * 128, :], vab[b][:].rearrange("p k d -> p (k d)"),
                      r=[tag + "vab%d" % b], w=[tag + "vao%d" % tt])
        return S.flush()


def conv_phase(kb, tag, catT, uTh_d, dww_d, dwb_d, lnw_d, lnb_d):
    nc, S = kb.nc, kb.S
    with contextlib.ExitStack() as st:
        sb = lambda n, s, d: st.enter_context(nc.sbuf_tensor(tag + n, s, d))
        uT = sb("uT", [128, 4, NLAT + 30], BF16)
        dwd = sb("dwd", [128, 4, 31, 128], BF16)
        dww = sb("dww", [128, 4, 31], F32)
        dwb = sb("dwb", [128, 4], F32)
        lnw = sb("lnw", [128, 4], F32)
        lnb = sb("lnb", [128, 4], F32)
        yT = [sb("yT%d" % c, [128, 512], F32) for c in range(4)]
        st6 = [sb("st6_%d" % i, [128, 6], F32) for i in range(2)]
        mv = [sb("mv%d" % i, [128, 2], F32) for i in range(2)]
        z = [sb("z%d" % i, [128, 512], BF16) for i in range(2)]
        S.dma("sp", uT[:], uTh_d, w=[tag + "uT"])
        S.dma("sp", dww[:], dww_d, w=[tag + "dww"])
        S.dma("sp", dwb[:], dwb_d, w=[tag + "dwb"])
        S.dma("sp", lnw[:], lnw_d, w=[tag + "lnw"])
        S.dma("sp", lnb[:], lnb_d, w=[tag + "lnb"])
        for c in range(4):
            for j in range(31):
                eng = "dve" if (c * 31 + j) % 2 else "pool"
                S.op(eng, lambda e, c=c, j=j: e.tensor_scalar(out=dwd[:, c, j, :], in0=kb.ident[:],
                                                              scalar1=dww[:, c, j:j + 1], scalar2=0.0,
                                                              op0=ALU.mult, op1=ALU.add),
                     r=["ident", tag + "dww"], w=[tag + "dwd%d_%d" % (c, j)])
        for tb in range(NLAT // 512):
            for c in range(4):
                p = kb.ps[c % 2]
                pk = "ps%d" % (c % 2)
                for j in range(31):
                    S.op("pe", lambda e, c=c, j=j, tb=tb, p=p: e.matmul(
                        p[:, 0:512], lhsT=dwd[:, c, j, :], rhs=uT[:, c, tb * 512 + j:tb * 512 + j + 512],
                        start=(j == 0), stop=(j == 30)), r=[tag + "uT", tag + "dwd%d_%d" % (c, j)], w=[pk])
                S.op("act", lambda e, c=c, p=p: e.activation(out=yT[c][:], in_=p[:, 0:512], func=AF.Identity,
                                                             bias=dwb[:, c:c + 1]),
                     r=[pk, tag + "dwb"], w=[tag + "yT%d" % c])
            for tt in range(4):
                tok = tb * 512 + tt * 128
                b = tt % 2
                p = kb.ps[2 + b]
                pk = "ps%d" % (2 + b)
                for c in range(4):
                    S.op("pe", lambda e, c=c, tt=tt, p=p: e.transpose(
                        out=p[:, c * 128:(c + 1) * 128], in_=yT[c][:, tt * 128:(tt + 1) * 128], identity=kb.identf[:]),
                        r=[tag + "yT%d" % c, "identf"], w=[pk])
                S.op("dve", lambda e, b=b, p=p: e.bn_stats(out=st6[b][:], in_=p[:, 0:512]), r=[pk], w=[tag + "st%d" % b])
                S.op("dve", lambda e, b=b: e.bn_aggr(out=mv[b][:], in_=st6[b][:]), r=[tag + "st%d" % b], w=[tag + "mv%d" % b])
                S.op("dve", lambda e, b=b: e.tensor_scalar(out=mv[b][:, 1:2], in0=mv[b][:, 1:2], scalar1=EPS,
                                                           scalar2=None, op0=ALU.add),
                     r=[tag + "mv%d" % b], w=[tag + "mv%d" % b])
                S.op("act", lambda e, b=b: e.activation(out=mv[b][:, 1:2], in_=mv[b][:, 1:2], func=AF.Sqrt),
                     r=[tag + "mv%d" % b], w=[tag + "mv%d" % b])
                S.op("dve", lambda e, b=b: e.reciprocal(out=mv[b][:, 1:2], in_=mv[b][:, 1:2]),
                     r=[tag + "mv%d" % b], w=[tag + "mv%d" % b])
                S.op("dve", lambda e, b=b, p=p: e.tensor_scalar(out=z[b][:], in0=p[:, 0:512], scalar1=mv[b][:, 0:1],
                                                                scalar2=mv[b][:, 1:2], op0=ALU.subtract, op1=ALU.mult),
                     r=[pk, tag + "mv%d" % b], w=[tag + "z%d" % b])
                pz = kb.ps[4 + b][:].bitcast(BF16)
                pzk = "ps%d" % (4 + b)
                for c in range(4):
                    S.op("pe", lambda e, c=c, b=b, pz=pz: e.transpose(
                        out=pz[:, c * 128:(c + 1) * 128], in_=z[b][:, c * 128:(c + 1) * 128], identity=kb.ident[:]),
                        r=[tag + "z%d" % b, "ident"], w=[pzk])
                for c in range(4):
                    S.op("act", lambda e, c=c, pz=pz, tok=tok: e.activation(
                        out=catT[:, c, tok:tok + 128], in_=pz[:, c * 128:(c + 1) * 128], func=AF.Silu,
                        scale=lnw[:, c:c + 1], bias=lnb[:, c:c + 1]),
                        r=[pzk, tag + "lnw", tag + "lnb"], w=[tag + "cat%d_%d" % (c, tok)])
        return S.flush()


NA_EDGE_ROWS = [0, 1, 2, 3, 28, 29, 30, 31]


def na_variant(q):
    if q < 4:
        return 0, 6, "x"
    if q >= 28:
        return 14, 6, "x"
    if q % 2 == 0:
        return q // 2, 4, "e"
    return (q - 1) // 2, 5, "o"


def na_bias_tables(rpb, j):
    NEG = -30000.0
    R0 = 32 * j
    ccol = np.arange(64)
    cs = np.clip(ccol - 8, 0, 48)

    def table(q, r_glob):
        c0, nch, kind = na_variant(q)
        start = int(np.clip(r_glob - 4, 0, 120))
        t = np.full((128, 8, nch, 64), NEG, np.float32)
        for i in range(nch):
            for rr2 in range(2):
                gr = R0 - 4 + 2 * (c0 + i) + rr2
                if not (start <= gr < start + 8):
                    continue
                ro = gr - r_glob + 7
                for kcol in range(64):
                    valid = (kcol >= cs) & (kcol < cs + 16)
                    co = kcol - ccol + 15
                    vals = rpb[:, ro, np.clip(co, 0, 30)]
                    t[rr2 * 64 + kcol, :, i, :] = np.where(valid[None, :], vals, NEG)
        return t

    be, bo = table(8, R0 + 8), table(9, R0 + 9)
    bx = np.stack([table(q, R0 + q) for q in NA_EDGE_ROWS], 0)
    return be, bo, bx


def na_phase(kb, tag, catT, qT_d, kTh_d, kTc_d, vah_d, vac_d, be_d, bo_d, bx_d):
    nc, S = kb.nc, kb.S
    with contextlib.ExitStack() as st:
        sb = lambda n, s, d: st.enter_context(nc.sbuf_tensor(tag + n, s, d))
        qT = sb("qT", [128, 4, NLAT], BF16)
        kTh = sb("kTh", [128, 4, 2560], BF16)
        kTc = sb("kTc", [128, 4, 256], BF16)
        Vh = sb("Vh", [128, 20, 1024], BF16)
        Vc = sb("Vc", [128, 2, 1024], BF16)
        be = sb("be", [128, 8, 4, 64], F32)
        bo = sb("bo", [128, 8, 5, 64], F32)
        bx = [sb("bx%d" % i, [128, 8, 6, 64], F32) for i in range(2)]
        sbf = [sb("sbf%d" % i, [128, 384], F32) for i in range(2)]
        pT = [sb("pT%d" % i, [128, 512], BF16) for i in range(2)]
        rinv = sb("rinv", [128, 512], F32)
        S.dma("sp", qT[:], qT_d, w=[tag + "qT"])
        S.dma("sp", kTh[:], kTh_d, w=[tag + "kTh"])
        S.dma("sp", kTc[:], kTc_d, w=[tag + "kTc"])
        vv = vah_d.rearrange("(c p) e -> p c e", p=128)
        for i in range(2):
            S.dma("act", Vh[:, i * 10:(i + 1) * 10, :], vv[:, i * 10:(i + 1) * 10, :], w=[tag + "Vh%d" % i])
        S.dma("act", Vc[:], vac_d.rearrange("(c p) e -> p c e", p=128), w=[tag + "Vc"])
        S.dma("sp", be[:], be_d, w=[tag + "be"])
        S.dma("sp", bo[:], bo_d, w=[tag + "bo"])
        nedge = 0
        unit = 0
        for q in range(32):
            c0, nch, kind = na_variant(q)
            if kind == "x":
                bt, btk = bx[nedge % 2], tag + "bx%d" % (nedge % 2)
                S.dma("sp", bt[:], bx_d[NA_EDGE_ROWS.index(q)], w=[btk])
                nedge += 1
            elif kind == "e":
                bt, btk = be, tag + "be"
            else:
                bt, btk = bo, tag + "bo"
            po = kb.ps[3 + q % 2]
            pok = "ps%d" % (3 + q % 2)
            nw = nch * 64

            def emit_QK(h, q=q, c0=c0, nch=nch):
                ps_ = kb.ps[h % 3]
                pb, hc = (h % 2) * 64, h // 2
                for i in range(nch + 2):
                    if i < nch:
                        lhsT = kTh[pb:pb + 64, hc, (c0 + i) * 128:(c0 + i + 1) * 128]
                        rk = tag + "kTh"
                    else:
                        lhsT = kTc[pb:pb + 64, hc, (i - nch) * 128:(i - nch + 1) * 128]
                        rk = tag + "kTc"
                    S.op("pe", lambda e, lhsT=lhsT, i=i, ps_=ps_: e.matmul(
                        ps_[:, i * 64:(i + 1) * 64], lhsT=lhsT, rhs=qT[pb:pb + 64, hc, q * 64:(q + 1) * 64],
                        start=True, stop=True), r=[rk, tag + "qT"], w=["ps%d" % (h % 3)])

            def emit_soft(h, nw=nw, nch=nch, bt=bt, btk=btk):
                ps_ = kb.ps[h % 3]
                psk = "ps%d" % (h % 3)
                b = h % 2
                S.op("dve", lambda e: e.scalar_tensor_tensor(
                    out=sbf[b][:, 0:nw], in0=ps_[:, 0:nw], scalar=0.125,
                    in1=bt[:, h, :, :].rearrange("p c q -> p (c q)"), op0=ALU.mult, op1=ALU.add),
                    r=[psk, btk], w=[tag + "sbf%d" % b])
                S.op("act", lambda e: e.activation(out=pT[b][:, 0:nw], in_=sbf[b][:, 0:nw], func=AF.Exp),
                     r=[tag + "sbf%d" % b], w=[tag + "pTa%d" % b])
                S.op("act", lambda e: e.activation(out=pT[b][:, nw:nw + 128], in_=ps_[:, nw:nw + 128], func=AF.Exp,
                                                   scale=0.125), r=[psk], w=[tag + "pTb%d" % b])

            def emit_PV(h, c0=c0, nch=nch, po=po, pok=pok):
                b = h % 2
                for i in range(nch + 2):
                    if i < nch:
                        lhsT = Vh[:, c0 + i, h * 128:(h + 1) * 128]
                        rk = tag + "Vh%d" % ((c0 + i) // 10)
                    else:
                        lhsT = Vc[:, i - nch, h * 128:(h + 1) * 128]
                        rk = tag + "Vc"
                    S.op("pe", lambda e, lhsT=lhsT, i=i: e.matmul(
                        po[:, h * 64:(h + 1) * 64], lhsT=lhsT, rhs=pT[b][:, i * 64:(i + 1) * 64],
                        start=(i == 0), stop=(i == nch + 1)),
                        r=[rk, tag + "pTa%d" % b, tag + "pTb%d" % b], w=[pok])

            emit_QK(0)
            for h in range(8):
                emit_soft(h)
                if h + 1 < 8:
                    emit_QK(h + 1)
                emit_PV(h)
            S.op("dve", lambda e, po=po: e.reciprocal(out=rinv[64:128, :], in_=po[64:128, 0:512]), r=[pok], w=[tag + "rinv"])
            for ev in range(2):
                o_v = po[0:64, 0:512].rearrange("p (hp e d) -> p hp e d", hp=4, e=2)[:, :, ev, :]
                r_v = rinv[64:128, :].rearrange("p (hp e d) -> p hp e d", hp=4, e=2)[:, :, ev, :]
                S.op("dve", lambda e, o_v=o_v, r_v=r_v, ev=ev, q=q: e.tensor_tensor(
                    out=catT[ev * 64:(ev + 1) * 64, 4:8, q * 64:(q + 1) * 64], in0=o_v, in1=r_v, op=ALU.mult),
                    r=[pok, tag + "rinv"], w=[tag + "cat%d_%d" % (q, ev)])
        return S.flush()


def final_phase(kb, tag, x_in, out_d, fn_d, nt):
    nc, S = kb.nc, kb.S
    with contextlib.ExitStack() as st:
        sb = lambda n, s, d: st.enter_context(nc.sbuf_tensor(tag + n, s, d))
        fn = sb("fn", [128, D], F32)
        xt = [sb("xt%d" % i, [128, D], F32) for i in range(4)]
        junk = sb("junk", [128, D], BF16)
        ss = [sb("ss%d" % i, [128, 1], F32) for i in range(4)]
        S.dma("sp", fn[:], fn_d, w=[tag + "fn"])
        for t in range(nt):
            b = t % 4
            xk = tag + "x%d" % b
            S.dma("sp", xt[b][:], x_in[t * 128:(t + 1) * 128, :], w=[xk])
            S.op("act", lambda e, b=b: e.activation(out=junk[:], in_=xt[b][:], func=AF.Square, accum_out=ss[b][:, 0:1]),
                 r=[xk], w=[tag + "ss%d" % b])
            rstd_ops(S, ss[b], 1, [tag + "ss%d" % b], D)
            S.op("dve", lambda e, b=b: e.scalar_tensor_tensor(out=xt[b][:], in0=xt[b][:], scalar=ss[b][:, 0:1], in1=fn[:],
                                                              op0=ALU.mult, op1=ALU.mult),
                 r=[xk, tag + "ss%d" % b, tag + "fn"], w=[xk])
            S.dma("sp", out_d[t * 128:(t + 1) * 128, :], xt[b][:], r=[xk], w=[tag + "o%d" % t])
        return S.flush()


NCORES = 8
_PROGS = {}


def _mk(nc):
    I = lambda n, s, dt=F32: nc.dram_tensor(n, list(s), dt, kind="ExternalInput").ap()
    O = lambda n, s, dt=F32: nc.dram_tensor(n, list(s), dt, kind="ExternalOutput").ap()
    T = lambda n, s, dt=F32: nc.dram_tensor(n, list(s), dt, kind="Internal").ap()
    return I, O, T


def _ffn_inputs(I, p):
    return dict(wg=I(p + "wg", [D, DFF]), wu=I(p + "wu", [D, DFF]), wd=I(p + "wd", [DFF, D]),
                sh=I(p + "sh", [128, 2, 8]), sc=I(p + "sc", [128, 2, 8]), g=I(p + "g", [128, 2, D]))


def _ffn(kb, tag, x_in, x_out, a, **kw):
    return ffn_phase(kb, tag, x_in, x_out, a["wg"], a["wu"], a["wd"], a["sh"], a["sc"], a["g"], **kw)


def build_l1():
    nc = bass.Bass("TRN2", target_bir_lowering=False)
    I, O, T = _mk(nc)
    csT = I("csT", [128, 8, 3]); wm = I("wm", [2, D, 1152]); bm = I("bm", [3, 2, 1152])
    mod = O("mod", [3, 2, 1152])
    with contextlib.ExitStack() as st:
        kb = KB(nc, st)
        mod_phase(kb, csT, wm, bm, mod)
    return nc


def build_l2():
    nc = bass.Bass("TRN2", target_bir_lowering=False)
    I, O, T = _mk(nc)
    x = I("x", [NTOK, D]); f1 = _ffn_inputs(I, "f1_")
    win = I("win", [D, 1536]); sh2 = I("p_sh", [128, 2, 8]); sc2 = I("p_sc", [128, 2, 8])
    gn = I("gains", [128, 1024]); co = I("cos", [NLAT, 32]); si = I("sin", [NLAT, 32])
    x1 = O("x1", [NTOK, D]); qT = O("qT", [64, 12, NTOK], BF16); kT = O("kT", [64, 4, NTOK], BF16)
    va = O("va", [NTOK, 512], BF16); f = O("f", [NTOK, 256], BF16)
    with contextlib.ExitStack() as st:
        kb = KB(nc, st)
        _ffn(kb, "f1", x, x1, f1)
        inproj0_phase(kb, "p1", x1, win, sh2, sc2, gn, co, si, qT, kT, va, f)
    return nc


def build_l3():
    nc = bass.Bass("TRN2", target_bir_lowering=False)
    I, O, T = _mk(nc)
    fa = I("f_all", [8192, 256], BF16); fcd = I("fc", [256, 256], BF16)
    tabs = {n: I("t_" + n, s, BF16) for n, s in FTAB_SHAPES.items()}
    qTd = I("qT", [64, 12, NTOK], BF16); kTd = I("kT_all", [64, 4, 8448], BF16); vd = I("v_all", [8448, 512], BF16)
    x1 = I("x1", [NTOK, D]); wo = I("wout", [D, D]); g5 = I("g5", [128, 2, D])
    f2 = _ffn_inputs(I, "f2_"); f3 = _ffn_inputs(I, "f3_")
    win = I("win", [D, 2560]); sh = I("p_sh", [128, 2, 8]); sc = I("p_sc", [128, 2, 8])
    x2 = T("x2", [NTOK, D]); x3 = T("x3", [NTOK, D])
    x4 = O("x4", [NTOK, D]); uT = O("uT", [128, 4, NLAT], BF16); qT1 = O("qT1", [128, 4, NLAT], BF16)
    kT1 = O("kT1", [128, 4, NTOK], BF16); va1 = O("va1", [NTOK, 1024], BF16)
    with contextlib.ExitStack() as st:
        kb = KB(nc, st)
        with contextlib.ExitStack() as st2:
            catT = st2.enter_context(nc.sbuf_tensor("catT", [128, 8, NTOK], BF16))
            fourier_phase(kb, "fo", catT, fa, fcd, tabs)
            attn0_phase(kb, "at", catT, qTd, kTd, vd)
            wout_phase(kb, "wo", catT, x1, x2, wo, g5)
        _ffn(kb, "f2", x2, x3, f2)
        _ffn(kb, "f3", x3, x4, f3)
        inproj1_phase(kb, "p2", x4, win, sh, sc, uT, qT1, kT1, va1)
    return nc


def build_l4():
    nc = bass.Bass("TRN2", target_bir_lowering=False)
    I, O, T = _mk(nc)
    uTh = I("uTh", [128, 4, NLAT + 30], BF16); dww = I("dww", [128, 4, 31]); dwb = I("dwb", [128, 4])
    lnw = I("lnw", [128, 4]); lnb = I("lnb", [128, 4])
    qTd = I("qT1", [128, 4, NLAT], BF16); kTh = I("kTh", [128, 4, 2560], BF16); kTc = I("kTc", [128, 4, 256], BF16)
    vah = I("vah", [2560, 1024], BF16); vac = I("vac", [256, 1024], BF16)
    be = I("be", [128, 8, 4, 64]); bo = I("bo", [128, 8, 5, 64]); bx = I("bx", [8, 128, 8, 6, 64])
    x4 = I("x4", [NTOK, D]); wo = I("wout", [D, D]); g5 = I("g5", [128, 2, D])
    f4 = _ffn_inputs(I, "f4_"); fn = I("fn", [128, D])
    x5 = T("x5", [NTOK, D]); x6 = T("x6", [NTOK, D])
    out = O("out", [NLAT, D])
    with contextlib.ExitStack() as st:
        kb = KB(nc, st)
        with contextlib.ExitStack() as st2:
            catT = st2.enter_context(nc.sbuf_tensor("catT", [128, 8, NTOK], BF16))
            conv_phase(kb, "cv", catT, uTh, dww, dwb, lnw, lnb)
            na_phase(kb, "na", catT, qTd, kTh, kTc, vah, vac, be, bo, bx)
            wout_phase(kb, "wo", catT, x4, x5, wo, g5, nt=NT_LAT)
        _ffn(kb, "f4", x5, x6, f4, latent_only=True)
        final_phase(kb, "fi", x6, out, fn, NT_LAT)
    return nc


def _prog(name, fn):
    if name not in _PROGS:
        _PROGS[name] = fn()
    return _PROGS[name]


def _run(name, fn, in_maps):
    res = run_bass_kernel_spmd(_prog(name, fn), in_maps, core_ids=list(range(NCORES)))
    return res.results


def _c(a, dt=np.float32):
    return np.ascontiguousarray(a, dtype=dt)


def kernel(x, c, ctx, c_ctx, w_mod, b_mod, ffn_w_gate, ffn_w_up, ffn_w_down,
           ab_w_in, ab_w_out, ab_q_norm, ab_k_norm,
           cd_w_in, cd_w_out, cd_dw_w, cd_dw_b, cd_ln_w, cd_ln_b, cd_rpb, final_norm):
    f32 = np.float32
    x = np.asarray(x, f32); ctx = np.asarray(ctx, f32)
    cores = [(i // 4, i % 4) for i in range(NCORES)]
    cs = np.stack([np.asarray(c, f32)[0], np.asarray(c, f32)[1], np.asarray(c_ctx, f32)], 0)
    csT = _c(cs.reshape(3, 8, 128).transpose(2, 1, 0))
    w_mod = np.asarray(w_mod, f32); b_mod = np.asarray(b_mod, f32)
    maps = []
    for i in range(NCORES):
        sl = slice(1152 * i, 1152 * (i + 1))
        maps.append(dict(csT=csT, wm=_c(w_mod[:, :, sl]), bm=_c(np.broadcast_to(b_mod[None, :, sl], (3, 2, 1152)))))
    r1 = _run("l1", build_l1, maps)
    mod = np.concatenate([r1[i]["mod"] for i in range(NCORES)], axis=2)
    mod = mod.reshape(3, 2, 9, D)

    def colv(b, l, v):
        return _c(np.stack([mod[b, l, v].reshape(8, 128).T, mod[2, l, v].reshape(8, 128).T], 1))

    def rowv(b, l, v):
        return _c(np.broadcast_to(np.stack([mod[b, l, v], mod[2, l, v]], 0)[None], (128, 2, D)))

    def ffn_in(p, b, l, half):
        v0 = 0 if half == 0 else 6
        return {p + "wg": ffn_w_gate[l, half], p + "wu": ffn_w_up[l, half], p + "wd": ffn_w_down[l, half],
                p + "sh": colv(b, l, v0), p + "sc": colv(b, l, v0 + 1), p + "g": rowv(b, l, v0 + 2)}

    ffn_w_gate = np.asarray(ffn_w_gate, f32); ffn_w_up = np.asarray(ffn_w_up, f32); ffn_w_down = np.asarray(ffn_w_down, f32)
    gains = _c(np.broadcast_to(np.concatenate([np.tile(np.asarray(ab_q_norm, f32)[0], 12),
                                               np.tile(np.asarray(ab_k_norm, f32)[0], 4)])[None], (128, 1024)))
    inv = 10000.0 ** (-np.arange(16, dtype=np.float64) / 16.0)
    maps = []
    for (b, j) in cores:
        t = np.arange(2048 * j, 2048 * (j + 1))
        ang = np.concatenate([(t // 64)[:, None] * inv, (t % 64)[:, None] * inv], -1)
        m = dict(x=_c(np.concatenate([x[b, 2048 * j:2048 * (j + 1)], ctx[b]], 0)),
                 win=np.asarray(ab_w_in, f32)[0], p_sh=colv(b, 0, 3), p_sc=colv(b, 0, 4), gains=gains,
                 cos=_c(np.cos(ang)), sin=_c(np.sin(ang)))
        m.update(ffn_in("f1_", b, 0, 0))
        maps.append(m)
    r2 = _run("l2", build_l2, maps)
    maps = []
    for i, (b, j) in enumerate(cores):
        grp = [r2[4 * b + jj] for jj in range(4)]
        kT_all = np.concatenate([grp[0]["kT"][:, :, NLAT:]] + [g_["kT"][:, :, :NLAT] for g_ in grp], axis=2)
        v_all = np.concatenate([grp[0]["va"][NLAT:]] + [g_["va"][:NLAT] for g_ in grp], axis=0)
        f_all = np.concatenate([g_["f"][:NLAT] for g_ in grp], axis=0)
        m = dict(f_all=np.ascontiguousarray(f_all), fc=np.ascontiguousarray(grp[0]["f"][NLAT:]),
                 qT=r2[i]["qT"], kT_all=np.ascontiguousarray(kT_all), v_all=np.ascontiguousarray(v_all),
                 x1=r2[i]["x1"], wout=np.asarray(ab_w_out, f32)[0], g5=rowv(b, 0, 5),
                 win=np.asarray(cd_w_in, f32)[0], p_sh=colv(b, 1, 3), p_sc=colv(b, 1, 4))
        for n, a in fourier_tables(j).items():
            m["t_" + n] = a
        m.update(ffn_in("f2_", b, 0, 1))
        m.update(ffn_in("f3_", b, 1, 0))
        maps.append(m)
    r3 = _run("l3", build_l3, maps)
    colp = lambda v: _c(np.asarray(v, f32).reshape(4, 128).T)
    dww = _c(np.asarray(cd_dw_w, f32)[0].T.reshape(4, 128, 31).transpose(1, 0, 2))
    fnr = _c(np.broadcast_to(np.asarray(final_norm, f32)[None], (128, D)))
    rpb = np.asarray(cd_rpb, f32)[0]
    maps = []
    for i, (b, j) in enumerate(cores):
        grp = [r3[4 * b + jj] for jj in range(4)]
        k_all = np.concatenate([g_["kT1"][:, :, :NLAT] for g_ in grp], axis=2)
        u_all = np.concatenate([g_["uT"] for g_ in grp], axis=2)
        v_allr = np.concatenate([g_["va1"][:NLAT] for g_ in grp], axis=0)
        kTh = np.zeros((128, 4, 2560), k_all.dtype); vah = np.zeros((2560, 1024), v_allr.dtype)
        lo = 2048 * j - 256
        s0, s1 = max(lo, 0), min(lo + 2560, 8192)
        kTh[:, :, s0 - lo:s1 - lo] = k_all[:, :, s0:s1]
        vah[s0 - lo:s1 - lo] = v_allr[s0:s1]
        uTh = np.zeros((128, 4, NLAT + 30), u_all.dtype)
        lo = 2048 * j - 15
        s0, s1 = max(lo, 0), min(lo + NLAT + 30, 8192)
        uTh[:, :, s0 - lo:s1 - lo] = u_all[:, :, s0:s1]
        be, bo, bx = na_bias_tables(rpb, j)
        m = dict(uTh=uTh, dww=dww, dwb=colp(np.asarray(cd_dw_b)[0]), lnw=colp(np.asarray(cd_ln_w)[0]),
                 lnb=colp(np.asarray(cd_ln_b)[0]), qT1=r3[i]["qT1"], kTh=kTh,
                 kTc=np.ascontiguousarray(grp[0]["kT1"][:, :, NLAT:]), vah=vah,
                 vac=np.ascontiguousarray(grp[0]["va1"][NLAT:]), be=be, bo=bo, bx=bx,
                 x4=r3[i]["x4"], wout=np.asarray(cd_w_out, f32)[0], g5=rowv(b, 1, 5), fn=fnr)
        m.update(ffn_in("f4_", b, 1, 1))
        maps.append(m)
    r4 = _run("l4", build_l4, maps)
    out = np.empty((2, 8192, D), f32)
    for i, (b, j) in enumerate(cores):
        out[b, 2048 * j:2048 * (j + 1)] = r4[i]["out"]
    return out
```

```python
import contextlib
import numpy as np
import concourse.bass as bass
import concourse.mybir as mybir
from concourse.bass_utils import run_bass_kernel_spmd

F32 = mybir.dt.float32
BF16 = mybir.dt.bfloat16
I32 = mybir.dt.int32
AF = mybir.ActivationFunctionType
ALU = mybir.AluOpType
AX = mybir.AxisListType

ENGS = ("pe", "act", "dve", "pool", "sp")
DMA_RING = 12


class Sched:
    def __init__(self, nc, st):
        self.nc = nc
        self.ops = []
        self.start = 0
        self.last_w = {}
        self.readers = {}
        self.ring_n = {e: 0 for e in ENGS}
        self.known = {e: {} for e in ENGS}
        self.cnt = {e: 0 for e in ENGS}
        self.esem = {e: st.enter_context(nc.semaphore("s_" + e)) for e in ENGS}
        self.dsem = {}
        for e in ("sp", "act", "pool"):
            for s in range(DMA_RING):
                self.dsem[(e, s)] = st.enter_context(nc.semaphore("d_%s_%d" % (e, s)))
        self.barrier = set()

    def op(self, eng, fn, r=(), w=(), dma=False):
        pr = [k for k in r if isinstance(k, str) and len(k) == 3 and k.startswith("ps")]
        if pr:
            r = [k for k in r if k not in pr]
            w = list(w) + pr
        deps = set()
        for k in r:
            if k in self.last_w:
                deps.add(self.last_w[k])
        for k in w:
            if k in self.last_w:
                deps.add(self.last_w[k])
            for rd in self.readers.get(k, {}).values():
                deps.add(rd)
        idx = len(self.ops)
        self.ops.append(dict(eng=eng, fn=fn, deps=deps, dma=dma, inc=False))
        for k in w:
            self.last_w[k] = idx
            self.readers[k] = {}
        for k in r:
            d = self.readers.setdefault(k, {})
            d[("dma", idx) if dma else eng] = idx
        return idx

    def dma(self, eng, out, in_, r=(), w=(), **kw):
        return self.op(eng, lambda e: e.dma_start(out=out, in_=in_, **kw), r=r, w=w, dma=True)

    def flush(self):
        nc = self.nc
        ops = self.ops
        start = self.start
        dma_ops = [i for i in range(start, len(ops)) if ops[i]["dma"]]
        if dma_ops:
            idx = len(ops)
            ops.append(dict(eng="sp", fn=None, deps=set(dma_ops), dma=False, inc=False))
        end = len(ops)
        first_of = {}
        last_of = {}
        for i in range(start, end):
            E = ops[i]["eng"]
            first_of.setdefault(E, i)
            if ops[i]["fn"] is not None and not ops[i]["dma"]:
                last_of[E] = i
        for E, i in first_of.items():
            ops[i]["deps"] |= self.barrier
        for E, i in last_of.items():
            ops[i]["inc"] = True

        def resolve(d):
            p = ops[d]
            if d >= start or p["inc"]:
                return d
            j = d + 1
            while not (ops[j]["eng"] == p["eng"] and ops[j]["inc"]):
                j += 1
            return j

        for i in range(start, end):
            o = ops[i]
            E = o["eng"]
            waits_c = {}
            waits_d = {}
            for d in o["deps"]:
                if d == i:
                    continue
                p = ops[d]
                if p["dma"]:
                    key = (p["eng"], p["slot"])
                    waits_d[key] = max(waits_d.get(key, 0), p["val"])
                else:
                    if p["fn"] is None:
                        for dd in p["deps"]:
                            pp = ops[dd]
                            if pp["dma"]:
                                key = (pp["eng"], pp["slot"])
                                waits_d[key] = max(waits_d.get(key, 0), pp["val"])
                        continue
                    if p["eng"] == E and E == "pe":
                        continue
                    d = resolve(d)
                    waits_c[p["eng"]] = max(waits_c.get(p["eng"], -1), d)
            if o["dma"]:
                n = self.ring_n[E]
                self.ring_n[E] += 1
                o["slot"] = n % DMA_RING
                o["val"] = 16 * (n // DMA_RING + 1)
                if n >= DMA_RING:
                    key = (E, o["slot"])
                    waits_d[key] = max(waits_d.get(key, 0), o["val"] - 16)
            wc = []
            for pe_, d in waits_c.items():
                if self.known[E].get(pe_, -1) >= d:
                    continue
                self.known[E][pe_] = d
                ops[d]["inc"] = True
                wc.append(d)
            wd = []
            for key, v in waits_d.items():
                if self.known[E].get(key, 0) >= v:
                    continue
                self.known[E][key] = v
                wd.append((key, v))
            o["wc"] = wc
            o["wd"] = wd
        for i in range(start, end):
            o = ops[i]
            if o["inc"] and "cval" not in o:
                self.cnt[o["eng"]] += 1
                o["cval"] = self.cnt[o["eng"]]
        esem, dsem = self.esem, self.dsem
        with nc.Block() as block:
            def body(E):
                def f(eng):
                    for i in range(start, end):
                        o = ops[i]
                        if o["eng"] != E:
                            continue
                        for d in o["wc"]:
                            p = ops[d]
                            eng.wait_ge(esem[p["eng"]], p["cval"])
                        for key, v in o["wd"]:
                            eng.wait_ge(dsem[key], v)
                        if o["fn"] is None:
                            continue
                        ins = o["fn"](eng)
                        if o["dma"]:
                            ins.then_inc(dsem[(E, o["slot"])], 16)
                        elif o["inc"]:
                            ins.then_inc(esem[E], 1)
                return f

            block.tensor(body("pe"))
            block.scalar(body("act"))
            block.vector(body("dve"))
            block.gpsimd(body("pool"))
            block.sync(body("sp"))
        self.barrier = set(last_of.values())
        if dma_ops:
            self.barrier.add(idx)
        self.start = end
        for i in range(start, end):
            ops[i]["fn"] = ops[i]["fn"] is not None and True or None
        self.last_w = {}
        self.readers = {}
        return dict(n_ops=end - start, cnt=dict(self.cnt))


NT_LAT = 16
NT_CTX = 2
NT = NT_LAT + NT_CTX
NTOK = NT * 128
NLAT = NT_LAT * 128
NALL = 66
NTOKALL = NALL * 128
NACT = 22
NACTTOK = NACT * 128
NLH = 20 * 128
TILES_ALL = [(t, t, 0) for t in range(64)] + [(64, 64, 1), (65, 65, 1)]
ACT_SRC = list(range(18)) + [62, 63, 64, 65]
TILES_ACT_FROM_ALL = [(ACT_SRC[a], a, 0 if a < 20 else 1) for a in range(NACT)]
TILES_ACT = [(a, a, 0 if a < 20 else 1) for a in range(NACT)]
TILES_OWN = [(a, a, 0) for a in range(NT_LAT)]


NEEDQ0 = set(range(18)) | {62, 63, 64, 65}


def make_groups(tiles, key=None):
    groups, cur = [], []
    for t in tiles:
        if cur and (len(cur) == 4 or cur[-1][2] != t[2] or cur[-1][0] + 1 != t[0] or cur[-1][1] + 1 != t[1]
                    or (key is not None and key(cur[-1]) != key(t))):
            groups.append(cur)
            cur = []
        cur.append(t)
    if cur:
        groups.append(cur)
    return groups
D = 1024
DFF = 2816
NF = 22
EPS = 1e-6


class KB:
    def __init__(self, nc, st):
        self.nc = nc
        self.S = Sched(nc, st)
        self.ps = [st.enter_context(nc.psum_tensor("ps%d" % i, [128, 512], F32)) for i in range(8)]
        self.ident = st.enter_context(nc.sbuf_tensor("ident", [128, 128], BF16))
        self.identf = st.enter_context(nc.sbuf_tensor("identf", [128, 128], F32))
        self.ones_f = st.enter_context(nc.sbuf_tensor("ones_f", [128, 128], F32))
        S = self.S
        for t, k in ((self.ident, "ident"), (self.identf, "identf")):
            S.op("pool", lambda e, t=t: e.memset(t[:], 0.0), w=[k])
            S.op("pool", lambda e, t=t: e.affine_select(out=t[:], in_=t[:], pattern=[[-1, 128]],
                                                        compare_op=ALU.not_equal, fill=1.0, base=0,
                                                        channel_multiplier=1), r=[k], w=[k])
        S.op("pool", lambda e: e.memset(self.ones_f[:], 1.0), w=["ones_f"])
        self.dram_n = 0

    def dram(self, name, shape, dt, kind="Internal"):
        return self.nc.dram_tensor(name, list(shape), dt, kind=kind).ap()


def group_list():
    import os
    g = [(4 * i, 4, 0) for i in range(NT_LAT // 4)]
    g.append((NT_LAT, NT_CTX, 1))
    ng = int(os.environ.get("NGROUPS", "99"))
    return g[:ng]


def rstd_ops(S, ss, nt, keys, mean_div):
    S.op("dve", lambda e: e.tensor_scalar(out=ss[:, 0:nt], in0=ss[:, 0:nt], scalar1=1.0 / mean_div, scalar2=EPS,
                                          op0=ALU.mult, op1=ALU.add), r=keys, w=keys)
    S.op("act", lambda e: e.activation(out=ss[:, 0:nt], in_=ss[:, 0:nt], func=AF.Sqrt), r=keys, w=keys)
    S.op("dve", lambda e: e.reciprocal(out=ss[:, 0:nt], in_=ss[:, 0:nt]), r=keys, w=keys)


def norm_group(kb, tag, xts, xkeys, ss, junk, xn, hT, sc1, sh, strm, psb_i, hkey, ktag=None):
    S = kb.S
    import os
    NP = int(os.environ.get("NORM_PARTS", "15"))
    nt = len(xts)
    ktag = tag if ktag is None else ktag
    sskey = ktag + "ss"
    for t in range(nt if NP & 1 else 0):
        S.op("act", lambda e, t=t: e.activation(out=junk[:], in_=xts[t][:], func=AF.Square,
                                                accum_out=ss[:, t:t + 1]), r=[xkeys[t]], w=[sskey + str(t)])
    allss = [sskey + str(t) for t in range(nt)]
    if NP & 1:
        rstd_ops(S, ss, nt, allss, D)
    psbs = [kb.ps[i][:].bitcast(BF16) for i in psb_i]
    pkeys = ["ps%d" % i for i in psb_i]
    for t in range(nt if NP & 2 else 0):
        b = t % 2
        S.op("act", lambda e, t=t, b=b: e.activation(out=xn[b][:], in_=xts[t][:], func=AF.Copy,
                                                     scale=ss[:, t:t + 1]),
             r=[xkeys[t]] + allss, w=[tag + "xn%d" % b])
        for kc in range(8 if NP & 4 else 0):
            psb, pkey = psbs[kc // 4], pkeys[kc // 4]
            S.op("pe", lambda e, kc=kc, b=b, psb=psb: e.transpose(out=psb[:, kc * 128:(kc + 1) * 128],
                                                         in_=xn[b][:, kc * 128:(kc + 1) * 128],
                                                         identity=kb.ident[:]),
                 r=[tag + "xn%d" % b, "ident"], w=[pkey])
        for kc in range(8 if NP & 8 else 0):
            eng = "dve" if kc >= 4 else "act"
            psb, pkey = psbs[kc // 4], pkeys[kc // 4]
            if eng == "dve":
                S.op("dve", lambda e, kc=kc, t=t, psb=psb: e.tensor_scalar(
                    out=hT[:, kc, t * 128:(t + 1) * 128], in0=psb[:, kc * 128:(kc + 1) * 128],
                    scalar1=sc1[:, strm, kc:kc + 1], scalar2=sh[:, strm, kc:kc + 1],
                    op0=ALU.mult, op1=ALU.add), r=[pkey, tag + "modc"], w=[hkey + "_%d_%d" % (t, kc)])
            else:
                S.op("act", lambda e, kc=kc, t=t, psb=psb: e.activation(
                    out=hT[:, kc, t * 128:(t + 1) * 128], in_=psb[:, kc * 128:(kc + 1) * 128],
                    func=AF.Identity, scale=sc1[:, strm, kc:kc + 1], bias=sh[:, strm, kc:kc + 1]),
                    r=[pkey, tag + "modc"], w=[hkey + "_%d_%d" % (t, kc)])
    return [hkey + "_%d_%d" % (t, kc) for t in range(nt) for kc in range(8)]


def load_modc(kb, st, tag, sh_d, sc_d):
    nc, S = kb.nc, kb.S
    sh = st.enter_context(nc.sbuf_tensor(tag + "sh", [128, 2, 8], F32))
    sc1 = st.enter_context(nc.sbuf_tensor(tag + "sc1", [128, 2, 8], F32))
    S.dma("sp", sh[:], sh_d, w=[tag + "modc_a"])
    S.dma("sp", sc1[:], sc_d, w=[tag + "modc_b"])
    S.op("dve", lambda e: e.tensor_scalar(out=sc1[:], in0=sc1[:], scalar1=1.0, scalar2=None, op0=ALU.add),
         r=[tag + "modc_a", tag + "modc_b"], w=[tag + "modc"])
    return sh, sc1


def ffn_phase(kb, tag, x_in, x_out, wg_d, wu_d, wd_d, sh_d, sc_d, g_d, tiles, dbg=99, bg=()):
    nc, S = kb.nc, kb.S
    with contextlib.ExitStack() as st:
        sb = lambda n, s, d: st.enter_context(nc.sbuf_tensor(tag + n, s, d))
        wg = sb("wg", [128, 8, DFF], BF16)
        wu = sb("wu", [128, 8, DFF], BF16)
        wd = sb("wd", [128, NF, D], BF16)
        G = sb("G", [128, 2, D], F32)
        NXB = 5
        xt = [sb("xt%d" % i, [128, D], F32) for i in range(NXB)]
        xn = [sb("xn%d" % i, [128, D], BF16) for i in range(2)]
        junk = sb("junk", [128, D], BF16)
        ss = sb("ss", [128, 4], F32)
        hT = sb("hT", [128, 8, 512], BF16)
        aT = sb("aT", [128, NF, 512], BF16)
        sg = [sb("sg%d" % i, [128, 512], F32) for i in range(2)]
        tmp = [sb("tmp%d" % i, [128, 512], F32) for i in range(2)]
        sh, sc1 = load_modc(kb, st, tag, sh_d, sc_d)
        S.dma("sp", G[:], g_d, w=[tag + "G0"])
        S.op("pool", lambda e: e.tensor_scalar(out=G[:], in0=G[:], scalar1=0.5, scalar2=0.0, op0=ALU.mult, op1=ALU.add),
             r=[tag + "G0"], w=[tag + "G"])
        wg_v = wg_d.rearrange("(kc p) f -> p kc f", p=128)
        wu_v = wu_d.rearrange("(kc p) f -> p kc f", p=128)
        wd_v = wd_d.rearrange("(f p) d -> p f d", p=128)
        nblk = (DFF + 511) // 512
        pre = wg_d.dtype == BF16
        qs = ("sp", "act") if pre else ("pool", "pool")
        for b in range(nblk if dbg >= -1 else 0):
            c0, c1 = b * 512, min(DFF, (b + 1) * 512)
            S.dma(qs[0], wg[:, :, c0:c1], wg_v[:, :, c0:c1], w=[tag + "wg%d" % b])
            S.dma(qs[1], wu[:, :, c0:c1], wu_v[:, :, c0:c1], w=[tag + "wu%d" % b])
        WDG = 4
        for b in range((NF + WDG - 1) // WDG if dbg >= -2 else 0):
            f0, f1 = b * WDG, min(NF, (b + 1) * WDG)
            S.dma(qs[b % 2], wd[:, f0:f1, :], wd_v[:, f0:f1, :], w=[tag + "wd%d" % b])
        bg = list(bg)
        groups = make_groups(tiles)
        xkey = lambda n: tag + "x%d" % (n % NXB)

        def load_x(n):
            S.dma("sp", xt[n % NXB][:], x_in[tiles[n][0] * 128:(tiles[n][0] + 1) * 128, :], w=[xkey(n)])

        loaded = 0
        n0 = 0
        for gi, grp in enumerate(groups):
            nt, strm = len(grp), grp[0][2]
            while loaded < min(len(tiles), n0 + NXB):
                load_x(loaded)
                loaded += 1
            ntok = nt * 128
            xts = [xt[(n0 + t) % NXB] for t in range(nt)]
            xkeys = [xkey(n0 + t) for t in range(nt)]
            dsts = [g_[1] for g_ in grp]
            n0 += nt
            for _ in range(3):
                if bg:
                    dst_, src_ = bg.pop(0)
                    S.dma("pool", dst_, src_, w=[tag + "bg%d" % len(bg)])
            if dbg >= 1:
                hkeys = norm_group(kb, tag, xts, xkeys, ss, junk, xn, hT, sc1, sh, strm, (0, 7), tag + "hT")
            for f in range(NF if dbg >= 2 else 0):
                pg, pu = kb.ps[1 + f % 2], kb.ps[3 + f % 2]
                kg, ku = "ps%d" % (1 + f % 2), "ps%d" % (3 + f % 2)
                blk = (f * 128) // 512
                for kc in range(8):
                    S.op("pe", lambda e, kc=kc, f=f, pg=pg, ntok=ntok: e.matmul(
                        pg[:, 0:ntok], lhsT=wg[:, kc, f * 128:(f + 1) * 128], rhs=hT[:, kc, 0:ntok],
                        start=(kc == 0), stop=(kc == 7)),
                        r=[tag + "wg%d" % blk] + [tag + "hT_%d_%d" % (t, kc) for t in range(nt)], w=[kg])
                for kc in range(8):
                    S.op("pe", lambda e, kc=kc, f=f, pu=pu, ntok=ntok: e.matmul(
                        pu[:, 0:ntok], lhsT=wu[:, kc, f * 128:(f + 1) * 128], rhs=hT[:, kc, 0:ntok],
                        start=(kc == 0), stop=(kc == 7)),
                        r=[tag + "wu%d" % blk] + [tag + "hT_%d_%d" % (t, kc) for t in range(nt)], w=[ku])
                S.op("act", lambda e, f=f, pg=pg, ntok=ntok: e.activation(out=sg[f % 2][:, 0:ntok], in_=pg[:, 0:ntok],
                                                               func=AF.Silu), r=[kg], w=[tag + "sg%d" % (f % 2)])
                S.op("dve", lambda e, f=f, pu=pu, ntok=ntok: e.tensor_tensor(out=aT[:, f, 0:ntok], in0=pu[:, 0:ntok],
                                                                  in1=sg[f % 2][:, 0:ntok], op=ALU.mult),
                     r=[ku, tag + "sg%d" % (f % 2)], w=[tag + "aT%d" % f])
            for t in range(nt):
                for dh in range(2 if dbg >= 3 else 0):
                    py = kb.ps[5 + dh]
                    ky = "ps%d" % (5 + dh)
                    for f in range(NF):
                        S.op("pe", lambda e, f=f, t=t, dh=dh, py=py: e.matmul(
                            py[:, 0:512], lhsT=aT[:, f, t * 128:(t + 1) * 128], rhs=wd[:, f, dh * 512:(dh + 1) * 512],
                            start=(f == 0), stop=(f == NF - 1)),
                            r=[tag + "aT%d" % f, tag + "wd%d" % (f // WDG)], w=[ky])
                    S.op("dve", lambda e, dh=dh, py=py, strm=strm: e.tensor_tensor(
                        out=tmp[dh][:], in0=py[:, 0:512], in1=G[:, strm, dh * 512:(dh + 1) * 512], op=ALU.mult),
                        r=[ky, tag + "G"], w=[tag + "tmp%d" % dh])
                    S.op("pool", lambda e, dh=dh, t=t, xts=xts: e.tensor_tensor(
                        out=xts[t][:, dh * 512:(dh + 1) * 512], in0=xts[t][:, dh * 512:(dh + 1) * 512],
                        in1=tmp[dh][:], op=ALU.add),
                        r=[tag + "tmp%d" % dh, xkeys[t]], w=[xkeys[t]])
                S.dma("sp", x_out[dsts[t] * 128:(dsts[t] + 1) * 128, :], xts[t][:], r=[xkeys[t]], w=[tag + "xo%d" % dsts[t]])
        while bg:
            dst_, src_ = bg.pop(0)
            S.dma("pool", dst_, src_, w=[tag + "bg%d" % len(bg)])
        return S.flush()


def mod_phase(kb, csT_d, wm_d, bm_d, mod_d):
    nc, S = kb.nc, kb.S
    NCOL = 1152
    with contextlib.ExitStack() as st:
        sb = lambda n, s, d: st.enter_context(nc.sbuf_tensor("m_" + n, s, d))
        cs = sb("cs", [128, 8, 3], F32)
        w = [sb("w%d" % l, [128, 8, NCOL], F32) for l in range(2)]
        bm = sb("bm", [3, 2, NCOL], F32)
        res = sb("res", [3, 2, NCOL], F32)
        S.dma("sp", cs[:], csT_d, w=["m_cs"])
        S.dma("sp", bm[:], bm_d, w=["m_bm"])
        for l in range(2):
            S.dma("sp" if l == 0 else "act", w[l][:], wm_d[l].rearrange("(kc p) f -> p kc f", p=128), w=["m_w%d" % l])
        S.op("act", lambda e: e.activation(out=cs[:], in_=cs[:], func=AF.Silu), r=["m_cs"], w=["m_cs"])
        i = 0
        for l in range(2):
            for c0 in range(0, NCOL, 512):
                c1 = min(NCOL, c0 + 512)
                p = kb.ps[i % 8]
                pk = "ps%d" % (i % 8)
                i += 1
                for kc in range(8):
                    S.op("pe", lambda e, kc=kc, l=l, c0=c0, c1=c1, p=p: e.matmul(
                        p[0:3, 0:c1 - c0], lhsT=cs[:, kc, :], rhs=w[l][:, kc, c0:c1], start=(kc == 0), stop=(kc == 7)),
                        r=["m_cs", "m_w%d" % l], w=[pk])
                S.op("dve", lambda e, l=l, c0=c0, c1=c1, p=p: e.tensor_tensor(
                    out=res[:, l, c0:c1], in0=p[0:3, 0:c1 - c0], in1=bm[:, l, c0:c1], op=ALU.add),
                    r=[pk, "m_bm"], w=["m_res"])
        S.dma("sp", mod_d, res[:], r=["m_res"], w=["m_out"])
        return S.flush()


def inproj0_phase(kb, tag, x_d, win_d, sh_d, sc_d, gains_d, cos_d, sin_d, qT_d, kT_d, va_d, f_d, tiles, needq=None):
    nc, S = kb.nc, kb.S
    with contextlib.ExitStack() as st:
        sb = lambda n, s_, d: st.enter_context(nc.sbuf_tensor(tag + n, s_, d))
        win = sb("win", [128, 8, 1536], BF16)
        gains = sb("gains", [128, 1024], F32)
        NROPE = cos_d.shape[0] // 128
        cosb = sb("cos", [128, NROPE, 32], F32)
        sinb = sb("sin", [128, NROPE, 32], F32)
        NXB = 6
        xt = [sb("xt%d" % i, [128, D], F32) for i in range(NXB)]
        xn = [sb("xn%d" % i, [128, D], BF16) for i in range(2)]
        junk = sb("junk", [128, D], BF16)
        ss2 = [sb("ss%d" % i, [128, 4], F32) for i in range(2)]
        hT2 = [sb("hT%d" % i, [128, 8, 512], BF16) for i in range(2)]
        qkg = sb("qkg", [128, 4, 1024], F32)
        sqg = sb("sqg", [128, 4, 1024], F32)
        ssq = sb("ssq", [128, 4, 16], F32)
        ta = [sb("ta%d" % i, [128, 4, 16, 32], F32) for i in range(4)]
        qkr = sb("qkr", [128, 4, 1024], BF16)
        qkT = [sb("qkT%d" % i, [64, 16, 512], BF16) for i in range(2)]
        vab = [sb("vab%d" % i, [128, 4, 128], BF16) for i in range(2)]
        fb_ = [sb("fb%d" % i, [128, 256], BF16) for i in range(2)]
        for i in range(2):
            S.op("pool", lambda e, i=i: e.memset(vab[i][:], 1.0), w=[tag + "vab%d" % i])
        sh, sc1 = load_modc(kb, st, tag, sh_d, sc_d)
        S.dma("sp", gains[:], gains_d, w=[tag + "gains"])
        S.dma("sp", cosb[:], cos_d.rearrange("(t p) j -> p t j", p=128), w=[tag + "cos"])
        S.dma("sp", sinb[:], sin_d.rearrange("(t p) j -> p t j", p=128), w=[tag + "sin"])
        win_v = win_d.rearrange("(kc p) f -> p kc f", p=128)
        for b in range(3):
            S.dma("pool", win[:, :, b * 512:(b + 1) * 512], win_v[:, :, b * 512:(b + 1) * 512], w=[tag + "win%d" % b])
        xkey = lambda n: tag + "x%d" % (n % NXB)
        loaded = 0
        nq_of = (lambda t: True) if needq is None else (lambda t: t in needq)
        groups = make_groups(tiles, key=lambda t: nq_of(t[1]))
        n0 = 0
        ginfo = []
        for gi, grp in enumerate(groups):
            ginfo.append((n0, len(grp)))
            n0 += len(grp)

        def stage_norm(gi):
            nonlocal loaded
            grp = groups[gi]
            n0_, nt_ = ginfo[gi]
            while loaded < min(len(tiles), n0_ + NXB):
                S.dma("sp", xt[loaded % NXB][:], x_d[tiles[loaded][0] * 128:(tiles[loaded][0] + 1) * 128, :],
                      w=[xkey(loaded)])
                loaded += 1
            xts_ = [xt[(n0_ + t) % NXB] for t in range(nt_)]
            xkeys_ = [xkey(n0_ + t) for t in range(nt_)]
            norm_group(kb, tag, xts_, xkeys_, ss2[gi % 2], junk, xn, hT2[gi % 2], sc1, sh, grp[0][2], (0, 7),
                       tag + "hT%d" % (gi % 2), ktag=tag + "n%d" % (gi % 2))

        stage_norm(0)
        for gi, grp in enumerate(groups):
            nt, strm, t0 = len(grp), grp[0][2], grp[0][1]
            nq = nq_of(t0)
            ntok = nt * 128
            hT = hT2[gi % 2]
            hkp = tag + "hT%d" % (gi % 2)
            qT_g = qkT[gi % 2]
            gk = tag + "qkT%d" % (gi % 2)
            H0 = 0 if nq else 12
            nh = 16 - H0
            C0 = H0 * 64
            qkeys = []
            for t in range(nt):
                tt = t0 + t
                b = tt % 2
                bank0 = 1 + 3 * (t % 2)
                for c in (range(3) if nq else (1, 2)):
                    p = kb.ps[bank0 + c]
                    for kc in range(8):
                        S.op("pe", lambda e, kc=kc, c=c, t=t, p=p, hT=hT: e.matmul(
                            p[:, 0:512], lhsT=hT[:, kc, t * 128:(t + 1) * 128], rhs=win[:, kc, c * 512:(c + 1) * 512],
                            start=(kc == 0), stop=(kc == 7)),
                            r=[hkp + "_%d_%d" % (t, kc), tag + "win%d" % c], w=["ps%d" % (bank0 + c)])
                if nq:
                    S.op("act", lambda e, t=t, bank0=bank0: e.activation(out=qkg[:, t, 0:512], in_=kb.ps[bank0][:, 0:512],
                                                                         func=AF.Copy),
                         r=["ps%d" % bank0], w=[tag + "qkg%da" % t])
                    qkeys.append(tag + "qkg%da" % t)
                S.op("dve", lambda e, t=t, bank0=bank0: e.tensor_copy(out=qkg[:, t, 512:1024], in_=kb.ps[bank0 + 1][:, 0:512]),
                     r=["ps%d" % (bank0 + 1)], w=[tag + "qkg%db" % t])
                qkeys.append(tag + "qkg%db" % t)
                S.op("act", lambda e, b=b, bank0=bank0: e.activation(
                    out=vab[b][:, :, 0:64], in_=kb.ps[bank0 + 2][:, 0:256].rearrange("p (k d) -> p k d", d=64), func=AF.Copy),
                    r=["ps%d" % (bank0 + 2)], w=[tag + "vab%d" % b])
                S.op("act", lambda e, b=b, bank0=bank0: e.activation(out=fb_[b][:], in_=kb.ps[bank0 + 2][:, 256:512], func=AF.Copy),
                     r=["ps%d" % (bank0 + 2)], w=[tag + "fb%d" % b])
                S.dma("sp", va_d[tt * 128:(tt + 1) * 128, :], vab[b][:].rearrange("p k d -> p (k d)"),
                      r=[tag + "vab%d" % b], w=[tag + "vao%d" % tt])
                S.dma("sp", f_d[tt * 128:(tt + 1) * 128, :], fb_[b][:], r=[tag + "fb%d" % b], w=[tag + "fo%d" % tt])
            if gi + 1 < len(groups):
                stage_norm(gi + 1)
            Q = qkg[:, 0:nt, C0:1024]
            Q4 = Q.rearrange("p t (h d) -> p t h d", d=64)
            SQ4 = sqg[:, 0:nt, C0:1024].rearrange("p t (h d) -> p t h d", d=64)
            SS = ssq[:, 0:nt, H0:16]
            qk_ = tag + "Q"
            S.op("dve", lambda e, Q=Q, nt=nt, C0=C0: e.tensor_tensor(out=sqg[:, 0:nt, C0:1024], in0=Q, in1=Q, op=ALU.mult),
                 r=qkeys, w=[tag + "sqg"])
            S.op("dve", lambda e, SQ4=SQ4, SS=SS: e.tensor_reduce(out=SS, in_=SQ4, axis=AX.X, op=ALU.add),
                 r=[tag + "sqg"], w=[tag + "ssq"])
            S.op("dve", lambda e, SS=SS: e.tensor_scalar(out=SS, in0=SS, scalar1=1.0 / 64, scalar2=EPS, op0=ALU.mult, op1=ALU.add),
                 r=[tag + "ssq"], w=[tag + "ssq"])
            S.op("act", lambda e, SS=SS: e.activation(out=SS, in_=SS, func=AF.Sqrt), r=[tag + "ssq"], w=[tag + "ssq"])
            S.op("dve", lambda e, SS=SS: e.reciprocal(out=SS, in_=SS), r=[tag + "ssq"], w=[tag + "ssq"])
            S.op("dve", lambda e, Q4=Q4, SS=SS, nt=nt, nh=nh: e.tensor_tensor(
                out=Q4, in0=Q4, in1=SS.unsqueeze(3).to_broadcast([128, nt, nh, 64]), op=ALU.mult),
                r=qkeys + [tag + "ssq"], w=[qk_])
            S.op("dve", lambda e, Q=Q, nt=nt, nh=nh, C0=C0: e.tensor_tensor(
                out=Q, in0=Q, in1=gains[:, C0:1024].unsqueeze(1).to_broadcast([128, nt, nh * 64]), op=ALU.mult),
                r=[qk_, tag + "gains"], w=[qk_])
            R4 = qkr[:, 0:nt, C0:1024].rearrange("p t (h d) -> p t h d", d=64)
            rk = tag + "qkr"
            if strm == 0:
                X1, X2 = Q4[:, :, :, 0:32], Q4[:, :, :, 32:64]
                cb = cosb[:, t0:t0 + nt, :].unsqueeze(2).to_broadcast([128, nt, nh, 32])
                sb_ = sinb[:, t0:t0 + nt, :].unsqueeze(2).to_broadcast([128, nt, nh, 32])
                tav = [ta[i][:, 0:nt, H0:16, :] for i in range(4)]
                tk = [tag + "ta%d" % i for i in range(4)]
                S.op("dve", lambda e, X1=X1, cb=cb, tav=tav: e.tensor_tensor(out=tav[0], in0=X1, in1=cb, op=ALU.mult),
                     r=[qk_, tag + "cos"], w=[tk[0]])
                S.op("dve", lambda e, X2=X2, sb_=sb_, tav=tav: e.tensor_tensor(out=tav[1], in0=X2, in1=sb_, op=ALU.mult),
                     r=[qk_, tag + "sin"], w=[tk[1]])
                S.op("pool", lambda e, X2=X2, cb=cb, tav=tav: e.tensor_tensor(out=tav[2], in0=X2, in1=cb, op=ALU.mult),
                     r=[qk_, tag + "cos"], w=[tk[2]])
                S.op("pool", lambda e, X1=X1, sb_=sb_, tav=tav: e.tensor_tensor(out=tav[3], in0=X1, in1=sb_, op=ALU.mult),
                     r=[qk_, tag + "sin"], w=[tk[3]])
                S.op("dve", lambda e, R4=R4, tav=tav: e.tensor_tensor(out=R4[:, :, :, 0:32], in0=tav[0], in1=tav[1],
                                                                      op=ALU.subtract), r=[tk[0], tk[1]], w=[rk + "a"])
                S.op("pool", lambda e, R4=R4, tav=tav: e.tensor_tensor(out=R4[:, :, :, 32:64], in0=tav[2], in1=tav[3],
                                                                       op=ALU.add), r=[tk[2], tk[3]], w=[rk + "b"])
            else:
                S.op("dve", lambda e, Q=Q, nt=nt, C0=C0: e.tensor_copy(out=qkr[:, 0:nt, C0:1024], in_=Q),
                     r=[qk_], w=[rk + "a", rk + "b"])
            gks = []
            for t in range(nt):
                for hb in ((0, 1) if nq else (1,)):
                    h_lo = max(H0, hb * 8)
                    nhh = (hb + 1) * 8 - h_lo
                    bi = 1 + 3 * (t % 2) + hb
                    pT = kb.ps[bi][:].bitcast(BF16)
                    pk = "ps%d" % bi
                    for hh in range(nhh):
                        h = h_lo + hh
                        S.op("pe", lambda e, h=h, hh=hh, pT=pT, t=t: e.transpose(
                            out=pT[0:64, hh * 128:(hh + 1) * 128], in_=qkr[:, t, h * 64:(h + 1) * 64],
                            identity=kb.ident[:]), r=[rk + "a", rk + "b", "ident"], w=[pk])
                    src = pT[0:64, 0:nhh * 128].rearrange("p (h t) -> p h t", h=nhh)
                    dst = qT_g[:, h_lo:h_lo + nhh, t * 128:(t + 1) * 128]
                    k_ = gk + "_%d_%d" % (t, hb)
                    gks.append(k_)
                    if hb == 0:
                        S.op("act", lambda e, src=src, dst=dst: e.activation(out=dst, in_=src, func=AF.Copy), r=[pk], w=[k_])
                    else:
                        S.op("dve", lambda e, src=src, dst=dst: e.tensor_copy(out=dst, in_=src), r=[pk], w=[k_])
            tok0 = t0 * 128
            if nq:
                S.dma("sp", qT_d[:, :, tok0:tok0 + ntok], qT_g[:, 0:12, 0:ntok], r=gks, w=[tag + "qo%d" % gi])
            S.dma("sp", kT_d[:, :, tok0:tok0 + ntok], qT_g[:, 12:16, 0:ntok], r=gks, w=[tag + "ko%d" % gi])
        return S.flush()


def fourier_phase(kb, tag, catT, f_all_d, fc_d, tabs, nb2=20):
    nc, S = kb.nc, kb.S
    with contextlib.ExitStack() as st:
        sb = lambda n, s, d: st.enter_context(nc.sbuf_tensor(tag + n, s, d))
        xs = sb("xs", [128, 64, 256], BF16)
        A = [sb("Are", [128, 128, 128], BF16), sb("Aim", [128, 128, 128], BF16)]
        c128 = sb("c128", [128, 128], BF16)
        ns128 = sb("ns128", [128, 128], BF16)
        tw = {n: sb(n, [128, 128, 2 * nb2], BF16) for n in ("twc", "tws", "twns")}
        dd = {n: sb(n, [128, 2, 256], BF16) for n in ("dc", "ds")}
        O = [sb("Ore", [128, 128, 2, nb2], BF16), sb("Oim", [128, 128, 2, nb2], BF16)]
        S.dma("sp", xs[:].rearrange("p k f -> p (k f)"), f_all_d.rearrange("(a b) f -> a (b f)", b=64), w=[tag + "xs"])
        S.dma("sp", c128[:], tabs["c128"], w=[tag + "c128"])
        S.dma("sp", ns128[:], tabs["ns128"], w=[tag + "ns128"])
        for n in tw:
            S.dma("act", tw[n][:], tabs[n], w=[tag + n])
        for n in dd:
            S.dma("act", dd[n][:], tabs[n], w=[tag + n])
        tabA = [(c128, tag + "c128"), (ns128, tag + "ns128")]
        for fb in range(32):
            for ri in range(2):
                p = kb.ps[ri * 2 + fb % 2]
                pk = "ps%d" % (ri * 2 + fb % 2)
                for i in range(4):
                    fp = fb * 4 + i
                    for f2 in range(2):
                        lhsT = xs[:, :, 2 * fp + f2]
                        S.op("pe", lambda e, lhsT=lhsT, p=p, i=i, ri=ri, f2=f2: e.matmul(
                            p[f2 * 64:(f2 + 1) * 64, i * 128:(i + 1) * 128], lhsT=lhsT, rhs=tabA[ri][0][:],
                            start=True, stop=True),
                            r=[tag + "xs", tabA[ri][1]], w=[pk])
                dst = A[ri][:, fb * 4:(fb + 1) * 4, :]
                src = p[:, 0:512].rearrange("p (a n) -> p a n", a=4)
                if ri == 0:
                    S.op("act", lambda e, dst=dst, src=src: e.activation(out=dst, in_=src, func=AF.Copy),
                         r=[pk], w=[tag + "A%d_%d" % (ri, fb)])
                else:
                    S.op("dve", lambda e, dst=dst, src=src: e.tensor_copy(out=dst, in_=src),
                         r=[pk], w=[tag + "A%d_%d" % (ri, fb)])
        Akeys = [[tag + "A%d_%d" % (ri, fb) for fb in range(32)] for ri in range(2)]
        W2 = 2 * nb2
        NPB = 512 // W2
        nbanks = (128 + NPB - 1) // NPB
        for nb in range(nbanks):
            n1s = list(range(nb * NPB, min(128, (nb + 1) * NPB)))
            for ri in range(2):
                p = kb.ps[4 + ri * 2 + nb % 2]
                pk = "ps%d" % (4 + ri * 2 + nb % 2)
                for i, n1 in enumerate(n1s):
                    if ri == 0:
                        terms = [(A[0], "twc", 0), (A[1], "tws", 1)]
                    else:
                        terms = [(A[1], "twc", 1), (A[0], "twns", 0)]
                    for ti, (At, tn, ai) in enumerate(terms):
                        S.op("pe", lambda e, At=At, tn=tn, n1=n1, p=p, i=i, ti=ti: e.matmul(
                            p[:, i * W2:(i + 1) * W2], lhsT=At[:, :, n1], rhs=tw[tn][:, n1, :],
                            start=(ti == 0), stop=(ti == 1)),
                            r=Akeys[ai] + [tag + tn], w=[pk])
                dst = O[ri][:, n1s[0]:n1s[-1] + 1, :, :]
                src = p[:, 0:len(n1s) * W2].rearrange("p (a f n) -> p a f n", a=len(n1s), f=2)
                if ri == 0:
                    S.op("act", lambda e, dst=dst, src=src: e.activation(out=dst, in_=src, func=AF.Copy),
                         r=[pk], w=[tag + "O%d_%d" % (ri, nb)])
                else:
                    S.op("dve", lambda e, dst=dst, src=src: e.tensor_copy(out=dst, in_=src),
                         r=[pk], w=[tag + "O%d_%d" % (ri, nb)])
        Okeys = [[tag + "O%d_%d" % (ri, nb) for nb in range(nbanks)] for ri in range(2)]
        for ch in range(2):
            for nb in range(nb2 // 4):
                p = kb.ps[(ch * 5 + nb) % 4]
                pk = "ps%d" % ((ch * 5 + nb) % 4)
                k = 0
                for f2 in range(2):
                    for ri, dn in ((0, "dc"), (1, "ds")):
                        rhs = O[ri][:, :, f2, nb * 4:(nb + 1) * 4].rearrange("p n a -> p a n")
                        S.op("pe", lambda e, rhs=rhs, dn=dn, f2=f2, ch=ch, p=p, k=k: e.matmul(
                            p[:, 0:512], lhsT=dd[dn][:, f2, ch * 128:(ch + 1) * 128], rhs=rhs,
                            start=(k == 0), stop=(k == 3)),
                            r=Okeys[ri] + [tag + dn], w=[pk])
                        k += 1
                dst = catT[:, 6 + ch, nb * 512:(nb + 1) * 512]
                if nb % 2 == 0:
                    S.op("act", lambda e, dst=dst, p=p: e.activation(out=dst, in_=p[:, 0:512], func=AF.Copy),
                         r=[pk], w=[tag + "cat%d_%d" % (ch, nb)])
                else:
                    S.op("dve", lambda e, dst=dst, p=p: e.tensor_copy(out=dst, in_=p[:, 0:512]),
                         r=[pk], w=[tag + "cat%d_%d" % (ch, nb)])
        fcs = sb("fcs", [128, 2, 256], BF16)
        c256 = sb("c256", [128, 2, 256], BF16)
        s256 = sb("s256", [128, 2, 256], BF16)
        dcf = sb("dcf", [128, 128], BF16)
        ndsf = sb("ndsf", [128, 128], BF16)
        Z = [[sb("Z%d_%d" % (a, b), [128, 256], BF16) for b in range(2)] for a in range(2)]
        S.dma("sp", fcs[:], fc_d.rearrange("(kc p) f -> p kc f", p=128), w=[tag + "fcs"])
        S.dma("sp", c256[:], tabs["c256"], w=[tag + "c256"])
        S.dma("sp", s256[:], tabs["s256"], w=[tag + "s256"])
        S.dma("sp", dcf[:], tabs["dcf"], w=[tag + "dcf"])
        S.dma("sp", ndsf[:], tabs["ndsf"], w=[tag + "ndsf"])
        for fch in range(2):
            for ti, (tb, tk) in enumerate(((c256, "c256"), (s256, "s256"))):
                p = kb.ps[4 + fch * 2 + ti]
                pk = "ps%d" % (4 + fch * 2 + ti)
                for kc in range(2):
                    S.op("pe", lambda e, fch=fch, tb=tb, kc=kc, p=p: e.matmul(
                        p[:, 0:256], lhsT=fcs[:, kc, fch * 128:(fch + 1) * 128], rhs=tb[:, kc, :],
                        start=(kc == 0), stop=(kc == 1)), r=[tag + "fcs", tag + tk], w=[pk])
                S.op("dve" if ti else "act",
                     (lambda e, fch=fch, ti=ti, p=p: e.tensor_copy(out=Z[fch][ti][:], in_=p[:, 0:256])) if ti else
                     (lambda e, fch=fch, ti=ti, p=p: e.activation(out=Z[fch][ti][:], in_=p[:, 0:256], func=AF.Copy)),
                     r=[pk], w=[tag + "Z%d_%d" % (fch, ti)])
        for ch in range(2):
            p = kb.ps[ch]
            pk = "ps%d" % ch
            S.op("pe", lambda e, ch=ch, p=p: e.matmul(p[:, 0:256], lhsT=dcf[:], rhs=Z[ch][0][:], start=True, stop=False),
                 r=[tag + "dcf", tag + "Z%d_0" % ch], w=[pk])
            S.op("pe", lambda e, ch=ch, p=p: e.matmul(p[:, 0:256], lhsT=ndsf[:], rhs=Z[ch][1][:], start=False, stop=True),
                 r=[tag + "ndsf", tag + "Z%d_1" % ch], w=[pk])
            S.op("act", lambda e, ch=ch, p=p: e.activation(out=catT[:, 6 + ch, nb2 * 128:nb2 * 128 + 256], in_=p[:, 0:256], func=AF.Copy),
                 r=[pk], w=[tag + "catc%d" % ch])
        return S.flush()


def fourier_tables(j):
    import ml_dtypes
    bf = lambda a: np.ascontiguousarray(a.astype(np.float32)).astype(ml_dtypes.bfloat16)
    k1 = np.arange(128)[:, None]
    n1 = np.arange(128)[None, :]
    T = {}
    T["c128"] = bf(np.cos(2 * np.pi * k1 * n1 / 128))
    T["ns128"] = bf(-np.sin(2 * np.pi * k1 * n1 / 128))
    k2 = np.arange(64)
    NB2 = np.array(list(range(18)) + [62, 63])
    nb2 = len(NB2)
    n = 128 * NB2[None, None, :] + np.arange(128)[None, :, None]
    th = 2 * np.pi * ((n * k2[:, None, None]) % 8192) / 8192.0 \
        + (np.pi / 2) * ((j * (n + k2[:, None, None])) % 4)
    twc = np.zeros((2, 64, 128, 2, nb2))
    tws = np.zeros((2, 64, 128, 2, nb2))
    for f2 in range(2):
        twc[f2, :, :, f2, :] = np.cos(th)
        tws[f2, :, :, f2, :] = np.sin(th)
    T["twc"] = bf(twc.reshape(128, 128, 2 * nb2))
    T["tws"] = bf(tws.reshape(128, 128, 2 * nb2))
    T["twns"] = bf(-tws.reshape(128, 128, 2 * nb2))
    sc = 1.0 / np.sqrt(8192.0 * 64.0)
    dc = np.zeros((4, 32, 2, 4, 64))
    ds = np.zeros((4, 32, 2, 4, 64))
    m = np.arange(64)[None, :]
    for f2 in range(2):
        c = (2 * np.arange(32) + f2)[:, None]
        for g in range(4):
            dc[g, :, f2, g, :] = np.cos(2 * np.pi * m * c / 64) * sc
            ds[g, :, f2, g, :] = np.sin(2 * np.pi * m * c / 64) * sc
    T["dc"] = bf(dc.reshape(128, 2, 256))
    T["ds"] = bf(ds.reshape(128, 2, 256))
    kk = np.arange(256)[:, None]
    nn = np.arange(256)[None, :]
    c256 = np.cos(2 * np.pi * kk * nn / 256).reshape(2, 128, 256).transpose(1, 0, 2)
    s256 = np.sin(2 * np.pi * kk * nn / 256).reshape(2, 128, 256).transpose(1, 0, 2)
    T["c256"] = bf(c256)
    T["s256"] = bf(s256)
    scc = 1.0 / np.sqrt(256.0 * 64.0)
    dcf = np.zeros((2, 64, 2, 64))
    dsf = np.zeros((2, 64, 2, 64))
    cc = np.arange(64)[:, None]
    for g in range(2):
        dcf[g, :, g, :] = np.cos(2 * np.pi * m * cc / 64) * scc
        dsf[g, :, g, :] = np.sin(2 * np.pi * m * cc / 64) * scc
    T["dcf"] = bf(dcf.reshape(128, 128))
    T["ndsf"] = bf(-dsf.reshape(128, 128))
    return T


FTAB_SHAPES = dict(c128=[128, 128], ns128=[128, 128], twc=[128, 128, 40], tws=[128, 128, 40], twns=[128, 128, 40],
                   dc=[128, 2, 256], ds=[128, 2, 256], c256=[128, 2, 256], s256=[128, 2, 256],
                   dcf=[128, 128], ndsf=[128, 128])


NKC = 66


def attn0_phase(kb, tag, catT, qT_d, kT_all_d, v_all_d):
    nc, S = kb.nc, kb.S
    with contextlib.ExitStack() as st:
        sb = lambda n, s, d: st.enter_context(nc.sbuf_tensor(tag + n, s, d))
        V = sb("V", [128, NKC, 512], BF16)
        kT = [sb("kT%d" % i, [128, NKC * 128], BF16) for i in range(2)]
        qTs = [sb("qT%d" % i, [128, 3, NACTTOK], BF16) for i in range(2)]
        for i in range(2):
            S.op("pool", lambda e, i=i: e.memset(kT[i][64:128, :], 0.0), w=[tag + "kTz%d" % i])
            S.op("pool", lambda e, i=i: e.memset(qTs[i][64:128, :, :], 0.0), w=[tag + "qTz%d" % i])
        pT = [sb("pT%d" % i, [128, 512], BF16) for i in range(3)]
        rinv = [sb("rinv%d" % i, [128, 512], F32) for i in range(2)]
        v_v = v_all_d.rearrange("(c p) e -> p c e", p=128)
        for i in range(3):
            S.dma("sp", V[:, i * 22:(i + 1) * 22, :], v_v[:, i * 22:(i + 1) * 22, :], w=[tag + "V%d" % i])
        blk = 0
        pending = None
        for kvh in range(4):
            kb_, kk = kT[kvh % 2], tag + "kT%d" % (kvh % 2)
            qb_, qk_ = qTs[kvh % 2], tag + "qT%d" % (kvh % 2)
            S.dma("sp", kb_[0:64, :], kT_all_d[:, kvh, :], w=[kk])
            S.dma("act", qb_[0:64, :, 0:2304], qT_d[:, 3 * kvh:3 * kvh + 3, 0:2304], w=[qk_ + "a"])
            S.dma("act", qb_[0:64, :, 2304:NACTTOK], qT_d[:, 3 * kvh:3 * kvh + 3, 7936:NTOKALL], w=[qk_ + "b"])
            for g in range(3):
                h = 3 * kvh + g
                for qb in range(6):
                    if qb < 5:
                        q0, nq, chunks = qb * 512, 512, list(range(NKC))
                    else:
                        q0, nq, chunks = NLH, 256, [64, 65]
                    po = kb.ps[3 + blk % 2]
                    pok = "ps%d" % (3 + blk % 2)
                    ri = rinv[blk % 2]
                    rik = tag + "rinv%d" % (blk % 2)
                    blk += 1

                    def emit_S(c, kb_=kb_, qb_=qb_, g=g, q0=q0, nq=nq, kk=kk, qk_=qk_, kvh=kvh):
                        S.op("pe", lambda e: e.matmul(kb.ps[c % 3][:, 0:nq], lhsT=kb_[:, c * 128:(c + 1) * 128],
                                                      rhs=qb_[:, g, q0:q0 + nq], start=True, stop=True),
                             r=[kk, qk_ + "a", qk_ + "b", tag + "kTz%d" % (kvh % 2), tag + "qTz%d" % (kvh % 2)],
                             w=["ps%d" % (c % 3)])

                    def emit_E(c, nq=nq):
                        S.op("act", lambda e: e.activation(out=pT[c % 3][:, 0:nq], in_=kb.ps[c % 3][:, 0:nq],
                                                           func=AF.Exp, scale=0.125),
                             r=["ps%d" % (c % 3)], w=[tag + "pT%d" % (c % 3)])

                    def emit_PV(c, first, last, po=po, pok=pok, kvh=kvh, nq=nq):
                        S.op("pe", lambda e: e.matmul(po[:, 0:nq], lhsT=V[:, c, kvh * 128:(kvh + 1) * 128],
                                                      rhs=pT[c % 3][:, 0:nq], start=first, stop=last),
                             r=[tag + "V%d" % (c // 22), tag + "pT%d" % (c % 3)], w=[pok])

                    emit_S(chunks[0])
                    if len(chunks) > 1:
                        emit_S(chunks[1])
                    for i, c in enumerate(chunks):
                        emit_E(c)
                        if i + 2 < len(chunks):
                            emit_S(chunks[i + 2])
                        emit_PV(c, i == 0, i == len(chunks) - 1)
                        if i == 1 and pending is not None:
                            pending()
                            pending = None

                    def fin(po=po, pok=pok, ri=ri, rik=rik, h=h, q0=q0, nq=nq):
                        S.op("dve", lambda e: e.reciprocal(out=ri[64:128, 0:nq], in_=po[64:128, 0:nq]), r=[pok], w=[rik])
                        pb = (h % 2) * 64
                        S.op("dve", lambda e: e.tensor_tensor(out=catT[pb:pb + 64, h // 2, q0:q0 + nq],
                                                              in0=po[0:64, 0:nq], in1=ri[64:128, 0:nq], op=ALU.mult),
                             r=[pok, rik], w=[tag + "cat_%d_%d" % (h, q0)])
                    pending = fin
        if pending is not None:
            pending()
        return S.flush()


def wout_phase(kb, tag, catT, x_in, x_out, wout_d, g_d, tiles):
    nc, S = kb.nc, kb.S
    with contextlib.ExitStack() as st:
        sb = lambda n, s, d: st.enter_context(nc.sbuf_tensor(tag + n, s, d))
        wo = sb("wo", [128, 8, D], BF16)
        G = sb("G", [128, 2, D], F32)
        NXB = 4
        xt = [sb("xt%d" % i, [128, D], F32) for i in range(NXB)]
        tmp = [sb("tmp%d" % i, [128, 512], F32) for i in range(2)]
        S.dma("sp", G[:], g_d, w=[tag + "G"])
        wv = wout_d.rearrange("(kc p) f -> p kc f", p=128)
        for b in range(2):
            S.dma("pool", wo[:, :, b * 512:(b + 1) * 512], wv[:, :, b * 512:(b + 1) * 512], w=[tag + "wo%d" % b])
        for n, (src, t, strm) in enumerate(tiles):
            xk = tag + "x%d" % (n % NXB)
            xb = xt[n % NXB]
            S.dma("sp", xb[:], x_in[src * 128:(src + 1) * 128, :], w=[xk])
            for dh in range(2):
                p = kb.ps[(2 * t + dh) % 4]
                pk = "ps%d" % ((2 * t + dh) % 4)
                for kc in range(8):
                    S.op("pe", lambda e, kc=kc, t=t, dh=dh, p=p: e.matmul(
                        p[:, 0:512], lhsT=catT[:, kc, t * 128:(t + 1) * 128], rhs=wo[:, kc, dh * 512:(dh + 1) * 512],
                        start=(kc == 0), stop=(kc == 7)), r=[tag + "wo%d" % dh], w=[pk])
                S.op("dve", lambda e, dh=dh, p=p, strm=strm: e.tensor_tensor(
                    out=tmp[dh][:], in0=p[:, 0:512], in1=G[:, strm, dh * 512:(dh + 1) * 512], op=ALU.mult),
                    r=[pk, tag + "G"], w=[tag + "tmp%d" % dh])
                S.op("pool", lambda e, dh=dh, xb=xb: e.tensor_tensor(
                    out=xb[:, dh * 512:(dh + 1) * 512], in0=xb[:, dh * 512:(dh + 1) * 512], in1=tmp[dh][:], op=ALU.add),
                    r=[tag + "tmp%d" % dh, xk], w=[xk])
            S.dma("sp", x_out[t * 128:(t + 1) * 128, :], xb[:], r=[xk], w=[tag + "xo%d" % t])
        return S.flush()


def inproj1_phase(kb, tag, x_d, win_d, sh_d, sc_d, uT_d, qT_d, kT_d, va_d, tiles):
    nc, S = kb.nc, kb.S
    with contextlib.ExitStack() as st:
        sb = lambda n, s, d: st.enter_context(nc.sbuf_tensor(tag + n, s, d))
        win = sb("win", [128, 8, 2560], BF16)
        NXB = 8
        xt = [sb("xt%d" % i, [128, D], F32) for i in range(NXB)]
        xn = [sb("xn%d" % i, [128, D], BF16) for i in range(2)]
        junk = sb("junk", [128, D], BF16)
        ss = sb("ss", [128, 4], F32)
        hT = sb("hT", [128, 8, 512], BF16)
        sg = [sb("sg%d" % i, [128, 512], F32) for i in range(2)]
        uT = [sb("uT%d" % i, [128, 4, 512], BF16) for i in range(2)]
        qT = [sb("qT%d" % i, [128, 4, 512], BF16) for i in range(2)]
        kT = [sb("kT%d" % i, [128, 4, 512], BF16) for i in range(2)]
        vab = [sb("vab%d" % i, [128, 8, 128], BF16) for i in range(2)]
        for i in range(2):
            S.op("pool", lambda e, i=i: e.memset(vab[i][:], 1.0), w=[tag + "vab%d" % i])
        sh, sc1 = load_modc(kb, st, tag, sh_d, sc_d)
        win_v = win_d.rearrange("(kc p) f -> p kc f", p=128)
        for b in range(5):
            S.dma("pool", win[:, :, b * 512:(b + 1) * 512], win_v[:, :, b * 512:(b + 1) * 512], w=[tag + "win%d" % b])
        xkey = lambda n: tag + "x%d" % (n % NXB)
        loaded = 0
        groups = make_groups(tiles)
        bank = 0
        n0 = 0
        for gi, grp in enumerate(groups):
            nt, strm, t0 = len(grp), grp[0][2], grp[0][1]
            while loaded < min(len(tiles), n0 + NXB):
                S.dma("sp", xt[loaded % NXB][:], x_d[tiles[loaded][0] * 128:(tiles[loaded][0] + 1) * 128, :],
                      w=[xkey(loaded)])
                loaded += 1
            ntok = nt * 128
            tok0 = t0 * 128
            xts = [xt[(n0 + t) % NXB] for t in range(nt)]
            xkeys = [xkey(n0 + t) for t in range(nt)]
            n0 += nt
            norm_group(kb, tag, xts, xkeys, ss, junk, xn, hT, sc1, sh, strm, (0, 7), tag + "hT")
            hk = [tag + "hT_%d_%d" % (t, kc) for t in range(nt) for kc in range(8)]
            gb2 = gi % 2

            def fm(col0, ntok=ntok):
                nonlocal bank
                bi = 1 + bank % 6
                bank += 1
                p = kb.ps[bi]
                for kc in range(8):
                    S.op("pe", lambda e, kc=kc, p=p: e.matmul(
                        p[:, 0:ntok], lhsT=win[:, kc, col0:col0 + 128], rhs=hT[:, kc, 0:ntok],
                        start=(kc == 0), stop=(kc == 7)), r=hk + [tag + "win%d" % (col0 // 512)], w=["ps%d" % bi])
                return p, "ps%d" % bi

            if strm == 0:
                for c in range(4):
                    pa, pak = fm(c * 128)
                    pb, pbk = fm(512 + c * 128)
                    S.op("act", lambda e, pb=pb, c=c, ntok=ntok: e.activation(out=sg[c % 2][:, 0:ntok], in_=pb[:, 0:ntok],
                                                                              func=AF.Sigmoid),
                         r=[pbk], w=[tag + "sg%d" % (c % 2)])
                    S.op("dve", lambda e, pa=pa, c=c, ntok=ntok, gb2=gb2: e.tensor_tensor(
                        out=uT[gb2][:, c, 0:ntok], in0=pa[:, 0:ntok], in1=sg[c % 2][:, 0:ntok], op=ALU.mult),
                        r=[pak, tag + "sg%d" % (c % 2)], w=[tag + "uT%d_%d" % (gb2, c)])
                S.dma("sp", uT_d[:, :, tok0:tok0 + ntok], uT[gb2][:, :, 0:ntok],
                      r=[tag + "uT%d_%d" % (gb2, c) for c in range(4)], w=[tag + "uo%d" % gi])
                for c in range(4):
                    pq, pqk = fm(1024 + c * 128)
                    S.op("act", lambda e, pq=pq, c=c, ntok=ntok, gb2=gb2: e.activation(
                        out=qT[gb2][:, c, 0:ntok], in_=pq[:, 0:ntok], func=AF.Copy),
                        r=[pqk], w=[tag + "qT%d_%d" % (gb2, c)])
                S.dma("sp", qT_d[:, :, tok0:tok0 + ntok], qT[gb2][:, :, 0:ntok],
                      r=[tag + "qT%d_%d" % (gb2, c) for c in range(4)], w=[tag + "qo%d" % gi])
            for c in range(4):
                pk_, pkk = fm(1536 + c * 128)
                S.op("dve", lambda e, pk_=pk_, c=c, ntok=ntok, gb2=gb2: e.tensor_copy(
                    out=kT[gb2][:, c, 0:ntok], in_=pk_[:, 0:ntok]), r=[pkk], w=[tag + "kT%d_%d" % (gb2, c)])
            S.dma("sp", kT_d[:, :, tok0:tok0 + ntok], kT[gb2][:, :, 0:ntok],
                  r=[tag + "kT%d_%d" % (gb2, c) for c in range(4)], w=[tag + "ko%d" % gi])
            for t in range(nt):
                tt = t0 + t
                b = tt % 2
                bi = 1 + bank % 6
                bank += 1
                p = kb.ps[bi]
                for kc in range(8):
                    S.op("pe", lambda e, kc=kc, t=t, p=p: e.matmul(
                        p[:, 0:512], lhsT=hT[:, kc, t * 128:(t + 1) * 128], rhs=win[:, kc, 2048:2560],
                        start=(kc == 0), stop=(kc == 7)), r=hk + [tag + "win4"], w=["ps%d" % bi])
                S.op("act", lambda e, b=b, p=p: e.activation(
                    out=vab[b][:, :, 0:64], in_=p[:, 0:512].rearrange("p (k d) -> p k d", d=64), func=AF.Copy),
                    r=["ps%d" % bi], w=[tag + "vab%d" % b])
                S.dma("sp", va_d[tt * 128:(tt + 1) * 128, :], vab[b][:].rearrange("p k d -> p (k d)"),
                      r=[tag + "vab%d" % b], w=[tag + "vao%d" % tt])
        return S.flush()


def conv_phase(kb, tag, catT, uT_d, msk_d, dww_d, dwb_d, lnw_d, lnb_d):
    nc, S = kb.nc, kb.S
    with contextlib.ExitStack() as st:
        sb = lambda n, s, d: st.enter_context(nc.sbuf_tensor(tag + n, s, d))
        uT = sb("uT", [128, 4, NLAT + 30], BF16)
        dwd = sb("dwd", [128, 4, 31, 128], BF16)
        dww = sb("dww", [128, 4, 31], F32)
        dwb = sb("dwb", [128, 4], F32)
        lnw = sb("lnw", [128, 4], F32)
        lnb = sb("lnb", [128, 4], F32)
        yT = [sb("yT%d" % c, [128, 512], F32) for c in range(4)]
        st6 = [sb("st6_%d" % i, [128, 6], F32) for i in range(2)]
        mv = [sb("mv%d" % i, [128, 2], F32) for i in range(2)]
        z = [sb("z%d" % i, [128, 512], BF16) for i in range(2)]
        msk = sb("msk", [128, 2], F32)
        S.dma("sp", msk[:], msk_d, w=[tag + "msk"])
        S.dma("sp", uT[:, :, 0:15], uT_d[:, :, NLH - 15:NLH], w=[tag + "uTa"])
        S.dma("sp", uT[:, :, 15:NLAT + 30], uT_d[:, :, 0:NLAT + 15], w=[tag + "uTb"])
        S.op("dve", lambda e: e.tensor_scalar(out=uT[:, :, 0:15], in0=uT[:, :, 0:15], scalar1=msk[:, 0:1], scalar2=None,
                                              op0=ALU.mult), r=[tag + "uTa", tag + "msk"], w=[tag + "uTa"])
        S.op("dve", lambda e: e.tensor_scalar(out=uT[:, :, NLAT + 15:NLAT + 30], in0=uT[:, :, NLAT + 15:NLAT + 30],
                                              scalar1=msk[:, 1:2], scalar2=None, op0=ALU.mult),
             r=[tag + "uTb", tag + "msk"], w=[tag + "uTb"])
        S.dma("sp", dww[:], dww_d, w=[tag + "dww"])
        S.dma("sp", dwb[:], dwb_d, w=[tag + "dwb"])
        S.dma("sp", lnw[:], lnw_d, w=[tag + "lnw"])
        S.dma("sp", lnb[:], lnb_d, w=[tag + "lnb"])
        for c in range(4):
            for j in range(31):
                eng = "dve" if (c * 31 + j) % 2 else "pool"
                S.op(eng, lambda e, c=c, j=j: e.tensor_scalar(out=dwd[:, c, j, :], in0=kb.ident[:],
                                                              scalar1=dww[:, c, j:j + 1], scalar2=0.0,
                                                              op0=ALU.mult, op1=ALU.add),
                     r=["ident", tag + "dww"], w=[tag + "dwd%d_%d" % (c, j)])
        for tb in range(NLAT // 512):
            for c in range(4):
                p = kb.ps[c % 2]
                pk = "ps%d" % (c % 2)
                for j in range(31):
                    S.op("pe", lambda e, c=c, j=j, tb=tb, p=p: e.matmul(
                        p[:, 0:512], lhsT=dwd[:, c, j, :], rhs=uT[:, c, tb * 512 + j:tb * 512 + j + 512],
                        start=(j == 0), stop=(j == 30)), r=[tag + "uTa", tag + "uTb", tag + "dwd%d_%d" % (c, j)], w=[pk])
                S.op("act", lambda e, c=c, p=p: e.activation(out=yT[c][:], in_=p[:, 0:512], func=AF.Identity,
                                                             bias=dwb[:, c:c + 1]),
                     r=[pk, tag + "dwb"], w=[tag + "yT%d" % c])
            for tt in range(4):
                tok = tb * 512 + tt * 128
                b = tt % 2
                p = kb.ps[2 + b]
                pk = "ps%d" % (2 + b)
                for c in range(4):
                    S.op("pe", lambda e, c=c, tt=tt, p=p: e.transpose(
                        out=p[:, c * 128:(c + 1) * 128], in_=yT[c][:, tt * 128:(tt + 1) * 128], identity=kb.identf[:]),
                        r=[tag + "yT%d" % c, "identf"], w=[pk])
                S.op("dve", lambda e, b=b, p=p: e.bn_stats(out=st6[b][:], in_=p[:, 0:512]), r=[pk], w=[tag + "st%d" % b])
                S.op("dve", lambda e, b=b: e.bn_aggr(out=mv[b][:], in_=st6[b][:]), r=[tag + "st%d" % b], w=[tag + "mv%d" % b])
                S.op("dve", lambda e, b=b: e.tensor_scalar(out=mv[b][:, 1:2], in0=mv[b][:, 1:2], scalar1=EPS,
                                                           scalar2=None, op0=ALU.add),
                     r=[tag + "mv%d" % b], w=[tag + "mv%d" % b])
                S.op("act", lambda e, b=b: e.activation(out=mv[b][:, 1:2], in_=mv[b][:, 1:2], func=AF.Sqrt),
                     r=[tag + "mv%d" % b], w=[tag + "mv%d" % b])
                S.op("dve", lambda e, b=b: e.reciprocal(out=mv[b][:, 1:2], in_=mv[b][:, 1:2]),
                     r=[tag + "mv%d" % b], w=[tag + "mv%d" % b])
                S.op("dve", lambda e, b=b, p=p: e.tensor_scalar(out=z[b][:], in0=p[:, 0:512], scalar1=mv[b][:, 0:1],
                                                                scalar2=mv[b][:, 1:2], op0=ALU.subtract, op1=ALU.mult),
                     r=[pk, tag + "mv%d" % b], w=[tag + "z%d" % b])
                pz = kb.ps[4 + b][:].bitcast(BF16)
                pzk = "ps%d" % (4 + b)
                for c in range(4):
                    S.op("pe", lambda e, c=c, b=b, pz=pz: e.transpose(
                        out=pz[:, c * 128:(c + 1) * 128], in_=z[b][:, c * 128:(c + 1) * 128], identity=kb.ident[:]),
                        r=[tag + "z%d" % b, "ident"], w=[pzk])
                for c in range(4):
                    S.op("act", lambda e, c=c, pz=pz, tok=tok: e.activation(
                        out=catT[:, c, tok:tok + 128], in_=pz[:, c * 128:(c + 1) * 128], func=AF.Silu,
                        scale=lnw[:, c:c + 1], bias=lnb[:, c:c + 1]),
                        r=[pzk, tag + "lnw", tag + "lnb"], w=[tag + "cat%d_%d" % (c, tok)])
        return S.flush()


NA_EDGE_ROWS = [0, 1, 2, 3, 28, 29, 30, 31]


def na_variant(q):
    if q < 4:
        return 0, 6, "x"
    if q >= 28:
        return 14, 6, "x"
    if q % 2 == 0:
        return q // 2, 4, "e"
    return (q - 1) // 2, 5, "o"


def na_bias_tables(rpb, j):
    NEG = -30000.0
    R0 = 32 * j
    ccol = np.arange(64)
    cs = np.clip(ccol - 8, 0, 48)

    def table(q, r_glob):
        c0, nch, kind = na_variant(q)
        start = int(np.clip(r_glob - 4, 0, 120))
        t = np.full((128, 8, nch, 64), NEG, np.float32)
        for i in range(nch):
            for rr2 in range(2):
                gr = R0 - 4 + 2 * (c0 + i) + rr2
                if not (start <= gr < start + 8):
                    continue
                ro = gr - r_glob + 7
                for kcol in range(64):
                    valid = (kcol >= cs) & (kcol < cs + 16)
                    co = kcol - ccol + 15
                    vals = rpb[:, ro, np.clip(co, 0, 30)]
                    t[rr2 * 64 + kcol, :, i, :] = np.where(valid[None, :], vals, NEG)
        return t

    be, bo = table(8, R0 + 8), table(9, R0 + 9)
    bx = np.stack([table(q, R0 + q) for q in NA_EDGE_ROWS], 0)
    return be, bo, bx


def na_phase(kb, tag, catT, qT_d, kT1_d, va1_d, be_d, bo_d, bx_d):
    nc, S = kb.nc, kb.S
    with contextlib.ExitStack() as st:
        sb = lambda n, s, d: st.enter_context(nc.sbuf_tensor(tag + n, s, d))
        qT = sb("qT", [128, 4, NLAT], BF16)
        kTh = sb("kTh", [128, 4, 2560], BF16)
        kTc = sb("kTc", [128, 4, 256], BF16)
        Vh = sb("Vh", [128, 20, 1024], BF16)
        Vc = sb("Vc", [128, 2, 1024], BF16)
        be = sb("be", [128, 8, 4, 64], F32)
        bo = sb("bo", [128, 8, 5, 64], F32)
        bx = [sb("bx%d" % i, [128, 8, 6, 64], F32) for i in range(2)]
        sbf = [sb("sbf%d" % i, [128, 384], F32) for i in range(2)]
        pT = [sb("pT%d" % i, [128, 512], BF16) for i in range(2)]
        rinv = sb("rinv", [128, 512], F32)
        S.dma("sp", qT[:], qT_d[:, :, 0:NLAT], w=[tag + "qT"])
        S.dma("sp", kTh[:, :, 0:256], kT1_d[:, :, 2304:2560], w=[tag + "kTh_a"])
        S.dma("sp", kTh[:, :, 256:2560], kT1_d[:, :, 0:2304], w=[tag + "kTh_b"])
        S.dma("sp", kTc[:], kT1_d[:, :, NLH:NACTTOK], w=[tag + "kTc"])
        vv = va1_d.rearrange("(c p) e -> p c e", p=128)
        S.dma("act", Vh[:, 0:2, :], vv[:, 18:20, :], w=[tag + "Vh0a"])
        S.dma("act", Vh[:, 2:10, :], vv[:, 0:8, :], w=[tag + "Vh0b"])
        S.dma("act", Vh[:, 10:20, :], vv[:, 8:18, :], w=[tag + "Vh1"])
        S.dma("act", Vc[:], vv[:, 20:22, :], w=[tag + "Vc"])
        S.dma("sp", be[:], be_d, w=[tag + "be"])
        S.dma("sp", bo[:], bo_d, w=[tag + "bo"])
        nedge = 0
        unit = 0
        for q in range(32):
            c0, nch, kind = na_variant(q)
            if kind == "x":
                bt, btk = bx[nedge % 2], tag + "bx%d" % (nedge % 2)
                S.dma("sp", bt[:], bx_d[NA_EDGE_ROWS.index(q)], w=[btk])
                nedge += 1
            elif kind == "e":
                bt, btk = be, tag + "be"
            else:
                bt, btk = bo, tag + "bo"
            po = kb.ps[3 + q % 2]
            pok = "ps%d" % (3 + q % 2)
            nw = nch * 64

            def emit_QK(h, q=q, c0=c0, nch=nch):
                ps_ = kb.ps[h % 3]
                pb, hc = (h % 2) * 64, h // 2
                for i in range(nch + 2):
                    if i < nch:
                        lhsT = kTh[pb:pb + 64, hc, (c0 + i) * 128:(c0 + i + 1) * 128]
                        rk = tag + "kTh_b"
                    else:
                        lhsT = kTc[pb:pb + 64, hc, (i - nch) * 128:(i - nch + 1) * 128]
                        rk = tag + "kTc"
                    S.op("pe", lambda e, lhsT=lhsT, i=i, ps_=ps_: e.matmul(
                        ps_[:, i * 64:(i + 1) * 64], lhsT=lhsT, rhs=qT[pb:pb + 64, hc, q * 64:(q + 1) * 64],
                        start=True, stop=True), r=[rk, tag + "kTh_a", tag + "qT"], w=["ps%d" % (h % 3)])

            def emit_soft(h, nw=nw, nch=nch, bt=bt, btk=btk):
                ps_ = kb.ps[h % 3]
                psk = "ps%d" % (h % 3)
                b = h % 2
                S.op("dve", lambda e: e.scalar_tensor_tensor(
                    out=sbf[b][:, 0:nw], in0=ps_[:, 0:nw], scalar=0.125,
                    in1=bt[:, h, :, :].rearrange("p c q -> p (c q)"), op0=ALU.mult, op1=ALU.add),
                    r=[psk, btk], w=[tag + "sbf%d" % b])
                S.op("act", lambda e: e.activation(out=pT[b][:, 0:nw], in_=sbf[b][:, 0:nw], func=AF.Exp),
                     r=[tag + "sbf%d" % b], w=[tag + "pTa%d" % b])
                S.op("act", lambda e: e.activation(out=pT[b][:, nw:nw + 128], in_=ps_[:, nw:nw + 128], func=AF.Exp,
                                                   scale=0.125), r=[psk], w=[tag + "pTb%d" % b])

            def emit_PV(h, c0=c0, nch=nch, po=po, pok=pok):
                b = h % 2
                for i in range(nch + 2):
                    if i < nch:
                        lhsT = Vh[:, c0 + i, h * 128:(h + 1) * 128]
                        rk = tag + "Vh1"
                    else:
                        lhsT = Vc[:, i - nch, h * 128:(h + 1) * 128]
                        rk = tag + "Vc"
                    S.op("pe", lambda e, lhsT=lhsT, i=i: e.matmul(
                        po[:, h * 64:(h + 1) * 64], lhsT=lhsT, rhs=pT[b][:, i * 64:(i + 1) * 64],
                        start=(i == 0), stop=(i == nch + 1)),
                        r=[rk, tag + "Vh0a", tag + "Vh0b", tag + "pTa%d" % b, tag + "pTb%d" % b], w=[pok])

            emit_QK(0)
            for h in range(8):
                emit_soft(h)
                if h + 1 < 8:
                    emit_QK(h + 1)
                emit_PV(h)
            S.op("dve", lambda e, po=po: e.reciprocal(out=rinv[64:128, :], in_=po[64:128, 0:512]), r=[pok], w=[tag + "rinv"])
            for ev in range(2):
                o_v = po[0:64, 0:512].rearrange("p (hp e d) -> p hp e d", hp=4, e=2)[:, :, ev, :]
                r_v = rinv[64:128, :].rearrange("p (hp e d) -> p hp e d", hp=4, e=2)[:, :, ev, :]
                S.op("dve", lambda e, o_v=o_v, r_v=r_v, ev=ev, q=q: e.tensor_tensor(
                    out=catT[ev * 64:(ev + 1) * 64, 4:8, q * 64:(q + 1) * 64], in0=o_v, in1=r_v, op=ALU.mult),
                    r=[pok, tag + "rinv"], w=[tag + "cat%d_%d" % (q, ev)])
        return S.flush()


def final_phase(kb, tag, x_in, out_d, fn_d, nt):
    nc, S = kb.nc, kb.S
    with contextlib.ExitStack() as st:
        sb = lambda n, s, d: st.enter_context(nc.sbuf_tensor(tag + n, s, d))
        fn = sb("fn", [128, D], F32)
        xt = [sb("xt%d" % i, [128, D], F32) for i in range(4)]
        junk = sb("junk", [128, D], BF16)
        ss = [sb("ss%d" % i, [128, 1], F32) for i in range(4)]
        S.dma("sp", fn[:], fn_d, w=[tag + "fn"])
        for t in range(nt):
            b = t % 4
            xk = tag + "x%d" % b
            S.dma("sp", xt[b][:], x_in[t * 128:(t + 1) * 128, :], w=[xk])
            S.op("act", lambda e, b=b: e.activation(out=junk[:], in_=xt[b][:], func=AF.Square, accum_out=ss[b][:, 0:1]),
                 r=[xk], w=[tag + "ss%d" % b])
            rstd_ops(S, ss[b], 1, [tag + "ss%d" % b], D)
            S.op("dve", lambda e, b=b: e.scalar_tensor_tensor(out=xt[b][:], in0=xt[b][:], scalar=ss[b][:, 0:1], in1=fn[:],
                                                              op0=ALU.mult, op1=ALU.mult),
                 r=[xk, tag + "ss%d" % b, tag + "fn"], w=[xk])
            S.dma("sp", out_d[t * 128:(t + 1) * 128, :], xt[b][:], r=[xk], w=[tag + "o%d" % t])
        return S.flush()


def modfull_phase(kb, csT_d, wmod_d, bmc_d, bmr_d, modc_d, modr_d):
    nc, S = kb.nc, kb.S
    with contextlib.ExitStack() as st:
        sb = lambda n, s_, d: st.enter_context(nc.sbuf_tensor("mf_" + n, s_, d))
        s2 = sb("s2", [128, 8, 2], F32)
        srep = sb("srep", [128, 8, 2, 128], F32)
        NWV = 4
        wv = [sb("wv%d" % i, [128, 8, D], F32) for i in range(NWV)]
        modc = sb("modc", [128, 2, 9, 2, 8], F32)
        bmc = sb("bmc", [128, 2, 9, 8], F32)
        bmr = [sb("bmr%d" % i, [128, D], F32) for i in range(2)]
        rowt = [sb("rowt%d" % i, [128, 2, D], F32) for i in range(2)]
        S.dma("sp", s2[:], csT_d, w=["mf_s2"])
        S.dma("sp", bmc[:], bmc_d, w=["mf_bmc"])
        S.op("act", lambda e: e.activation(out=s2[:], in_=s2[:], func=AF.Silu), r=["mf_s2"], w=["mf_s2"])
        S.op("dve", lambda e: e.tensor_copy(out=srep[:], in_=s2[:].unsqueeze(3).to_broadcast([128, 8, 2, 128])),
             r=["mf_s2"], w=["mf_srep"])
        n = 0
        gi = 0
        issued = 0
        for l in range(2):
            wl = wmod_d[l].rearrange("(kc p) f -> p kc f", p=128)
            for v in range(9):
                b = n % NWV
                wk = "mf_wv%d" % b
                while issued < min(18, n + NWV):
                    li, vi, bi_ = issued // 9, issued % 9, issued % NWV
                    wl2 = wmod_d[li].rearrange("(kc p) f -> p kc f", p=128)
                    S.dma("sp", wv[bi_][:, 0:4, :], wl2[:, 0:4, vi * D:(vi + 1) * D], w=["mf_wv%da" % bi_])
                    S.dma("act", wv[bi_][:, 4:8, :], wl2[:, 4:8, vi * D:(vi + 1) * D], w=["mf_wv%db" % bi_])
                    issued += 1
                pc = kb.ps[n % 2]
                pck = "ps%d" % (n % 2)
                for c in range(8):
                    for kc in range(8):
                        S.op("pe", lambda e, c=c, kc=kc, b=b, pc=pc: e.matmul(
                            pc[:, c * 2:(c + 1) * 2], lhsT=wv[b][:, kc, c * 128:(c + 1) * 128], rhs=s2[:, kc, :],
                            start=(kc == 0), stop=(kc == 7)), r=[wk + "a", wk + "b", "mf_s2"], w=[pck])
                S.op("dve", lambda e, l=l, v=v, pc=pc: e.tensor_tensor(
                    out=modc[:, l, v, :, :], in0=pc[:, 0:16].rearrange("p (c s) -> p s c", s=2),
                    in1=bmc[:, l, v, :].unsqueeze(1).to_broadcast([128, 2, 8]), op=ALU.add),
                    r=[pck, "mf_bmc"], w=["mf_modc"])
                if v in (2, 5, 8):
                    g = (2, 5, 8).index(v)
                    rb = gi % 2
                    gi += 1
                    S.dma("sp", bmr[rb][:], bmr_d[:, l, g, :], w=["mf_bmr%d" % rb])
                    for strm in range(2):
                        for hf in range(2):
                            pr = kb.ps[2 + (strm * 2 + hf) % 4]
                            prk = "ps%d" % (2 + (strm * 2 + hf) % 4)
                            for kc in range(8):
                                S.op("pe", lambda e, kc=kc, strm=strm, hf=hf, b=b, pr=pr: e.matmul(
                                    pr[:, 0:512], lhsT=srep[:, kc, strm, :], rhs=wv[b][:, kc, hf * 512:(hf + 1) * 512],
                                    start=(kc == 0), stop=(kc == 7)), r=[wk + "a", wk + "b", "mf_srep"], w=[prk])
                            S.op("dve", lambda e, strm=strm, hf=hf, rb=rb, pr=pr: e.tensor_tensor(
                                out=rowt[rb][:, strm, hf * 512:(hf + 1) * 512], in0=pr[:, 0:512],
                                in1=bmr[rb][:, hf * 512:(hf + 1) * 512], op=ALU.add),
                                r=[prk, "mf_bmr%d" % rb], w=["mf_rowt%d_%d_%d" % (rb, strm, hf)])
                    S.dma("sp", modr_d[:, l, g, :, :], rowt[rb][:],
                          r=["mf_rowt%d_%d_%d" % (rb, a_, b_) for a_ in range(2) for b_ in range(2)], w=["mf_ro%d_%d" % (l, g)])
                n += 1
        S.dma("sp", modc_d, modc[:], r=["mf_modc"], w=["mf_co"])
        return S.flush()


NCORES = 8
_PROGS = {}


def build_fused():
    nc = bass.Bass("TRN2", target_bir_lowering=False)
    I = lambda n, s, dt=F32: nc.dram_tensor(n, list(s), dt, kind="ExternalInput").ap()
    O = lambda n, s, dt=F32: nc.dram_tensor(n, list(s), dt, kind="ExternalOutput").ap()
    T = lambda n, s, dt=F32: nc.dram_tensor(n, list(s), dt, kind="Internal").ap()
    x = I("x", [NTOKALL, D])
    csT = I("csT", [128, 8, 2]); wmod = I("wmod", [2, D, 9 * D]); bmc = I("bmc", [128, 2, 9, 8]); bmr = I("bmr", [128, 2, 3, D])
    fw = {}
    for i in range(1, 5):
        fw[i] = (I("f%d_wg" % i, [D, DFF]), I("f%d_wu" % i, [D, DFF]), I("f%d_wd" % i, [DFF, D]))
    win0 = I("win0", [D, 1536]); wout0 = I("wout0", [D, D]); win1 = I("win1", [D, 2560]); wout1 = I("wout1", [D, D])
    gains = I("gains", [128, 1024]); cos = I("cos", [8192, 32]); sin = I("sin", [8192, 32])
    tabs = {n: I("t_" + n, s, BF16) for n, s in FTAB_SHAPES.items()}
    dww = I("dww", [128, 4, 31]); dwb = I("dwb", [128, 4]); lnw = I("lnw", [128, 4]); lnb = I("lnb", [128, 4])
    be = I("be", [128, 8, 4, 64]); bo = I("bo", [128, 8, 5, 64]); bx = I("bx", [8, 128, 8, 6, 64])
    msk = I("msk", [128, 2]); fn = I("fn", [128, D])
    out = O("out", [NLAT, D])
    modc = T("modc", [128, 2, 9, 2, 8]); modr = T("modr", [128, 2, 3, 2, D])
    x1 = T("x1", [NTOKALL, D]); qT = T("qT", [64, 12, NTOKALL], BF16); kT = T("kT", [64, 4, NTOKALL], BF16)
    va = T("va", [NTOKALL, 512], BF16); f = T("f", [NTOKALL, 256], BF16)
    x2 = T("x2", [NACTTOK, D]); x3 = T("x3", [NACTTOK, D]); x4 = T("x4", [NACTTOK, D])
    uT = T("uT", [128, 4, NLH], BF16); qT1 = T("qT1", [128, 4, NLH], BF16)
    kT1 = T("kT1", [128, 4, NACTTOK], BF16); va1 = T("va1", [NACTTOK, 1024], BF16)
    x5 = T("x5", [NLAT, D]); x6 = T("x6", [NLAT, D])
    fwb = {}
    bg = []
    for i in range(2, 5):
        fwb[i] = (T("f%d_wgb" % i, [D, DFF], BF16), T("f%d_wub" % i, [D, DFF], BF16), T("f%d_wdb" % i, [DFF, D], BF16))
        for k in range(2):
            for c0 in range(0, DFF, 704):
                bg.append((fwb[i][k][:, c0:c0 + 704], fw[i][k][:, c0:c0 + 704]))
        for r0 in range(0, DFF, 1408):
            bg.append((fwb[i][2][r0:r0 + 1408, :], fw[i][2][r0:r0 + 1408, :]))
    with contextlib.ExitStack() as st:
        kb = KB(nc, st)
        modfull_phase(kb, csT, wmod, bmc, bmr, modc, modr)
        mc = lambda l, v: modc[:, l, v]
        mr = lambda l, g: modr[:, l, g]
        ffn_phase(kb, "f1", x, x1, fw[1][0], fw[1][1], fw[1][2], mc(0, 0), mc(0, 1), mr(0, 0), TILES_ALL, bg=bg)
        inproj0_phase(kb, "p1", x1, win0, mc(0, 3), mc(0, 4), gains, cos, sin, qT, kT, va, f, TILES_ALL, needq=NEEDQ0)
        with contextlib.ExitStack() as st2:
            catT = st2.enter_context(nc.sbuf_tensor("catT", [128, 8, NACTTOK], BF16))
            fourier_phase(kb, "fo", catT, f[0:8192, :], f[8192:NTOKALL, :], tabs)
            attn0_phase(kb, "at", catT, qT, kT, va)
            wout_phase(kb, "wo", catT, x1, x2, wout0, mr(0, 1), TILES_ACT_FROM_ALL)
        ffn_phase(kb, "f2", x2, x3, fwb[2][0], fwb[2][1], fwb[2][2], mc(0, 6), mc(0, 7), mr(0, 2), TILES_ACT)
        ffn_phase(kb, "f3", x3, x4, fwb[3][0], fwb[3][1], fwb[3][2], mc(1, 0), mc(1, 1), mr(1, 0), TILES_ACT)
        inproj1_phase(kb, "p2", x4, win1, mc(1, 3), mc(1, 4), uT, qT1, kT1, va1, TILES_ACT)
        with contextlib.ExitStack() as st2:
            catT = st2.enter_context(nc.sbuf_tensor("catT1", [128, 8, NLAT], BF16))
            conv_phase(kb, "cv", catT, uT, msk, dww, dwb, lnw, lnb)
            na_phase(kb, "na", catT, qT1, kT1, va1, be, bo, bx)
            wout_phase(kb, "w1", catT, x4, x5, wout1, mr(1, 1), TILES_OWN)
        ffn_phase(kb, "f4", x5, x6, fwb[4][0], fwb[4][1], fwb[4][2], mc(1, 6), mc(1, 7), mr(1, 2), TILES_OWN)
        final_phase(kb, "fi", x6, out, fn, NT_LAT)
    return nc


def _c(a, dt=np.float32):
    return np.ascontiguousarray(a, dtype=dt)


def kernel(x, c, ctx, c_ctx, w_mod, b_mod, ffn_w_gate, ffn_w_up, ffn_w_down,
           ab_w_in, ab_w_out, ab_q_norm, ab_k_norm,
           cd_w_in, cd_w_out, cd_dw_w, cd_dw_b, cd_ln_w, cd_ln_b, cd_rpb, final_norm):
    f32 = np.float32
    A = lambda a: np.asarray(a, f32)
    x = A(x); ctx = A(ctx); c = A(c); c_ctx = A(c_ctx); w_mod = _c(w_mod); b_mod = A(b_mod)
    ffn_w_gate = A(ffn_w_gate); ffn_w_up = A(ffn_w_up); ffn_w_down = A(ffn_w_down)
    common = dict(wmod=w_mod,
                  bmc=_c(b_mod.reshape(2, 9, 8, 128).transpose(3, 0, 1, 2)),
                  bmr=_c(np.broadcast_to(b_mod.reshape(2, 9, D)[:, [2, 5, 8], :][None], (128, 2, 3, D))),
                  win0=_c(A(ab_w_in)[0]), wout0=_c(A(ab_w_out)[0]), win1=_c(A(cd_w_in)[0]), wout1=_c(A(cd_w_out)[0]),
                  gains=_c(np.broadcast_to(np.concatenate([np.tile(A(ab_q_norm)[0], 12), np.tile(A(ab_k_norm)[0], 4)])[None],
                                           (128, 1024))),
                  dww=_c(A(cd_dw_w)[0].T.reshape(4, 128, 31).transpose(1, 0, 2)),
                  dwb=_c(A(cd_dw_b)[0].reshape(4, 128).T), lnw=_c(A(cd_ln_w)[0].reshape(4, 128).T),
                  lnb=_c(A(cd_ln_b)[0].reshape(4, 128).T),
                  fn=_c(np.broadcast_to(A(final_norm)[None], (128, D))))
    k = 1
    for l in range(2):
        for half in range(2):
            common["f%d_wg" % k] = _c(ffn_w_gate[l, half]); common["f%d_wu" % k] = _c(ffn_w_up[l, half])
            common["f%d_wd" % k] = _c(ffn_w_down[l, half])
            k += 1
    inv = 10000.0 ** (-np.arange(16, dtype=np.float64) / 16.0)
    rpb = A(cd_rpb)[0]
    maps = []
    for i in range(NCORES):
        b, j = i // 4, i % 4
        m = dict(common)
        m["x"] = _c(np.concatenate([np.roll(x[b], -2048 * j, axis=0), ctx[b]], 0))
        m["csT"] = _c(np.stack([c[b], c_ctx], 0).reshape(2, 8, 128).transpose(2, 1, 0))
        t = (np.arange(8192) + 2048 * j) % 8192
        ang = np.concatenate([(t // 64)[:, None] * inv, (t % 64)[:, None] * inv], -1)
        m["cos"] = _c(np.cos(ang)); m["sin"] = _c(np.sin(ang))
        for n, a in fourier_tables(j).items():
            m["t_" + n] = a
        be, bo, bx = na_bias_tables(rpb, j)
        m["be"], m["bo"], m["bx"] = be, bo, bx
        m["msk"] = _c(np.broadcast_to(np.array([0.0 if j == 0 else 1.0, 0.0 if j == 3 else 1.0], f32)[None], (128, 2)))
        maps.append(m)
    if "fused" not in _PROGS:
        _PROGS["fused"] = build_fused()
    res = run_bass_kernel_spmd(_PROGS["fused"], maps, core_ids=list(range(NCORES)))
    out = np.empty((2, 8192, D), f32)
    for i in range(NCORES):
        b, j = i // 4, i % 4
        out[b, 2048 * j:2048 * (j + 1)] = res.results[i]["out"]
    return out
```

```python
import contextlib
import numpy as np
import concourse.bass as bass
import concourse.mybir as mybir
from concourse.bass_utils import run_bass_kernel_spmd

F32 = mybir.dt.float32
BF16 = mybir.dt.bfloat16
I32 = mybir.dt.int32
AF = mybir.ActivationFunctionType
ALU = mybir.AluOpType
AX = mybir.AxisListType

ENGS = ("pe", "act", "dve", "pool", "sp")
DMA_RING = 12


class Sched:
    def __init__(self, nc, st):
        self.nc = nc
        self.ops = []
        self.start = 0
        self.last_w = {}
        self.readers = {}
        self.ring_n = {e: 0 for e in ENGS}
        self.known = {e: {} for e in ENGS}
        self.cnt = {e: 0 for e in ENGS}
        self.esem = {e: st.enter_context(nc.semaphore("s_" + e)) for e in ENGS}
        self.dsem = {}
        for e in ("sp", "act", "pool"):
            for s in range(DMA_RING):
                self.dsem[(e, s)] = st.enter_context(nc.semaphore("d_%s_%d" % (e, s)))
        self.barrier = set()

    def op(self, eng, fn, r=(), w=(), dma=False):
        pr = [k for k in r if isinstance(k, str) and len(k) == 3 and k.startswith("ps")]
        if pr:
            r = [k for k in r if k not in pr]
            w = list(w) + pr
        deps = set()
        for k in r:
            if k in self.last_w:
                deps.add(self.last_w[k])
        for k in w:
            if k in self.last_w:
                deps.add(self.last_w[k])
            for rd in self.readers.get(k, {}).values():
                deps.add(rd)
        idx = len(self.ops)
        self.ops.append(dict(eng=eng, fn=fn, deps=deps, dma=dma, inc=False))
        for k in w:
            self.last_w[k] = idx
            self.readers[k] = {}
        for k in r:
            d = self.readers.setdefault(k, {})
            d[("dma", idx) if dma else eng] = idx
        return idx

    def dma(self, eng, out, in_, r=(), w=(), **kw):
        return self.op(eng, lambda e: e.dma_start(out=out, in_=in_, **kw), r=r, w=w, dma=True)

    def flush(self):
        nc = self.nc
        ops = self.ops
        start = self.start
        dma_ops = [i for i in range(start, len(ops)) if ops[i]["dma"]]
        if dma_ops:
            idx = len(ops)
            ops.append(dict(eng="sp", fn=None, deps=set(dma_ops), dma=False, inc=False))
        end = len(ops)
        first_of = {}
        last_of = {}
        for i in range(start, end):
            E = ops[i]["eng"]
            first_of.setdefault(E, i)
            if ops[i]["fn"] is not None and not ops[i]["dma"]:
                last_of[E] = i
        for E, i in first_of.items():
            ops[i]["deps"] |= self.barrier
        for E, i in last_of.items():
            ops[i]["inc"] = True

        def resolve(d):
            p = ops[d]
            if d >= start or p["inc"]:
                return d
            j = d + 1
            while not (ops[j]["eng"] == p["eng"] and ops[j]["inc"]):
                j += 1
            return j

        for i in range(start, end):
            o = ops[i]
            E = o["eng"]
            waits_c = {}
            waits_d = {}
            for d in o["deps"]:
                if d == i:
                    continue
                p = ops[d]
                if p["dma"]:
                    key = (p["eng"], p["slot"])
                    waits_d[key] = max(waits_d.get(key, 0), p["val"])
                else:
                    if p["fn"] is None:
                        for dd in p["deps"]:
                            pp = ops[dd]
                            if pp["dma"]:
                                key = (pp["eng"], pp["slot"])
                                waits_d[key] = max(waits_d.get(key, 0), pp["val"])
                        continue
                    if p["eng"] == E and E == "pe":
                        continue
                    d = resolve(d)
                    waits_c[p["eng"]] = max(waits_c.get(p["eng"], -1), d)
            if o["dma"]:
                n = self.ring_n[E]
                self.ring_n[E] += 1
                o["slot"] = n % DMA_RING
                o["val"] = 16 * (n // DMA_RING + 1)
                if n >= DMA_RING:
                    key = (E, o["slot"])
                    waits_d[key] = max(waits_d.get(key, 0), o["val"] - 16)
            wc = []
            for pe_, d in waits_c.items():
                if self.known[E].get(pe_, -1) >= d:
                    continue
                self.known[E][pe_] = d
                ops[d]["inc"] = True
                wc.append(d)
            wd = []
            for key, v in waits_d.items():
                if self.known[E].get(key, 0) >= v:
                    continue
                self.known[E][key] = v
                wd.append((key, v))
            o["wc"] = wc
            o["wd"] = wd
        for i in range(start, end):
            o = ops[i]
            if o["inc"] and "cval" not in o:
                self.cnt[o["eng"]] += 1
                o["cval"] = self.cnt[o["eng"]]
        esem, dsem = self.esem, self.dsem
        with nc.Block() as block:
            def body(E):
                def f(eng):
                    for i in range(start, end):
                        o = ops[i]
                        if o["eng"] != E:
                            continue
                        for d in o["wc"]:
                            p = ops[d]
                            eng.wait_ge(esem[p["eng"]], p["cval"])
                        for key, v in o["wd"]:
                            eng.wait_ge(dsem[key], v)
                        if o["fn"] is None:
                            continue
                        ins = o["fn"](eng)
                        if o["dma"]:
                            ins.then_inc(dsem[(E, o["slot"])], 16)
                        elif o["inc"]:
                            ins.then_inc(esem[E], 1)
                return f

            block.tensor(body("pe"))
            block.scalar(body("act"))
            block.vector(body("dve"))
            block.gpsimd(body("pool"))
            block.sync(body("sp"))
        self.barrier = set(last_of.values())
        if dma_ops:
            self.barrier.add(idx)
        self.start = end
        for i in range(start, end):
            ops[i]["fn"] = ops[i]["fn"] is not None and True or None
        self.last_w = {}
        self.readers = {}
        return dict(n_ops=end - start, cnt=dict(self.cnt))


NT_LAT = 16
NT_CTX = 2
NT = NT_LAT + NT_CTX
NTOK = NT * 128
NLAT = NT_LAT * 128
NALL = 66
NTOKALL = NALL * 128
NACT = 22
NACTTOK = NACT * 128
NLH = 20 * 128
TILES_ALL = [(t, t, 0) for t in range(64)] + [(64, 64, 1), (65, 65, 1)]
ACT_SRC = list(range(18)) + [62, 63, 64, 65]
TILES_ACT_FROM_ALL = [(ACT_SRC[a], a, 0 if a < 20 else 1) for a in range(NACT)]
TILES_ACT = [(a, a, 0 if a < 20 else 1) for a in range(NACT)]
TILES_OWN = [(a, a, 0) for a in range(NT_LAT)]


NEEDQ0 = set(range(18)) | {62, 63, 64, 65}


def make_groups(tiles, key=None):
    groups, cur = [], []
    for t in tiles:
        if cur and (len(cur) == 4 or cur[-1][2] != t[2] or cur[-1][0] + 1 != t[0] or cur[-1][1] + 1 != t[1]
                    or (key is not None and key(cur[-1]) != key(t))):
            groups.append(cur)
            cur = []
        cur.append(t)
    if cur:
        groups.append(cur)
    return groups
D = 1024
DFF = 2816
NF = 22
EPS = 1e-6


class KB:
    def __init__(self, nc, st):
        self.nc = nc
        self.S = Sched(nc, st)
        self.ps = [st.enter_context(nc.psum_tensor("ps%d" % i, [128, 512], F32)) for i in range(8)]
        self.ident = st.enter_context(nc.sbuf_tensor("ident", [128, 128], BF16))
        self.identf = st.enter_context(nc.sbuf_tensor("identf", [128, 128], F32))
        self.ones_f = st.enter_context(nc.sbuf_tensor("ones_f", [128, 128], F32))
        S = self.S
        for t, k in ((self.ident, "ident"), (self.identf, "identf")):
            S.op("pool", lambda e, t=t: e.memset(t[:], 0.0), w=[k])
            S.op("pool", lambda e, t=t: e.affine_select(out=t[:], in_=t[:], pattern=[[-1, 128]],
                                                        compare_op=ALU.not_equal, fill=1.0, base=0,
                                                        channel_multiplier=1), r=[k], w=[k])
        S.op("pool", lambda e: e.memset(self.ones_f[:], 1.0), w=["ones_f"])
        self.dram_n = 0

    def dram(self, name, shape, dt, kind="Internal"):
        return self.nc.dram_tensor(name, list(shape), dt, kind=kind).ap()


def group_list():
    import os
    g = [(4 * i, 4, 0) for i in range(NT_LAT // 4)]
    g.append((NT_LAT, NT_CTX, 1))
    ng = int(os.environ.get("NGROUPS", "99"))
    return g[:ng]


def rstd_ops(S, ss, nt, keys, mean_div):
    S.op("dve", lambda e: e.tensor_scalar(out=ss[:, 0:nt], in0=ss[:, 0:nt], scalar1=1.0 / mean_div, scalar2=EPS,
                                          op0=ALU.mult, op1=ALU.add), r=keys, w=keys)
    S.op("act", lambda e: e.activation(out=ss[:, 0:nt], in_=ss[:, 0:nt], func=AF.Sqrt), r=keys, w=keys)
    S.op("dve", lambda e: e.reciprocal(out=ss[:, 0:nt], in_=ss[:, 0:nt]), r=keys, w=keys)


def norm_group(kb, tag, xts, xkeys, ss, junk, xn, hT, sc1, sh, strm, psb_i, hkey, ktag=None):
    S = kb.S
    import os
    NP = int(os.environ.get("NORM_PARTS", "15"))
    nt = len(xts)
    ktag = tag if ktag is None else ktag
    sskey = ktag + "ss"
    for t in range(nt if NP & 1 else 0):
        S.op("act", lambda e, t=t: e.activation(out=junk[:], in_=xts[t][:], func=AF.Square,
                                                accum_out=ss[:, t:t + 1]), r=[xkeys[t]], w=[sskey + str(t)])
    allss = [sskey + str(t) for t in range(nt)]
    if NP & 1:
        rstd_ops(S, ss, nt, allss, D)
    psbs = [kb.ps[i][:].bitcast(BF16) for i in psb_i]
    pkeys = ["ps%d" % i for i in psb_i]
    for t in range(nt if NP & 2 else 0):
        b = t % 2
        S.op("act", lambda e, t=t, b=b: e.activation(out=xn[b][:], in_=xts[t][:], func=AF.Copy,
                                                     scale=ss[:, t:t + 1]),
             r=[xkeys[t]] + allss, w=[tag + "xn%d" % b])
        for kc in range(8 if NP & 4 else 0):
            psb, pkey = psbs[kc // 4], pkeys[kc // 4]
            S.op("pe", lambda e, kc=kc, b=b, psb=psb: e.transpose(out=psb[:, kc * 128:(kc + 1) * 128],
                                                         in_=xn[b][:, kc * 128:(kc + 1) * 128],
                                                         identity=kb.ident[:]),
                 r=[tag + "xn%d" % b, "ident"], w=[pkey])
        for kc in range(8 if NP & 8 else 0):
            eng = "dve" if kc >= 4 else "act"
            psb, pkey = psbs[kc // 4], pkeys[kc // 4]
            if eng == "dve":
                S.op("dve", lambda e, kc=kc, t=t, psb=psb: e.tensor_scalar(
                    out=hT[:, kc, t * 128:(t + 1) * 128], in0=psb[:, kc * 128:(kc + 1) * 128],
                    scalar1=sc1[:, strm, kc:kc + 1], scalar2=sh[:, strm, kc:kc + 1],
                    op0=ALU.mult, op1=ALU.add), r=[pkey, tag + "modc"], w=[hkey + "_%d_%d" % (t, kc)])
            else:
                S.op("act", lambda e, kc=kc, t=t, psb=psb: e.activation(
                    out=hT[:, kc, t * 128:(t + 1) * 128], in_=psb[:, kc * 128:(kc + 1) * 128],
                    func=AF.Identity, scale=sc1[:, strm, kc:kc + 1], bias=sh[:, strm, kc:kc + 1]),
                    r=[pkey, tag + "modc"], w=[hkey + "_%d_%d" % (t, kc)])
    return [hkey + "_%d_%d" % (t, kc) for t in range(nt) for kc in range(8)]


def load_modc(kb, st, tag, sh_d, sc_d):
    nc, S = kb.nc, kb.S
    sh = st.enter_context(nc.sbuf_tensor(tag + "sh", [128, 2, 8], F32))
    sc1 = st.enter_context(nc.sbuf_tensor(tag + "sc1", [128, 2, 8], F32))
    S.dma("sp", sh[:], sh_d, w=[tag + "modc_a"])
    S.dma("sp", sc1[:], sc_d, w=[tag + "modc_b"])
    S.op("dve", lambda e: e.tensor_scalar(out=sc1[:], in0=sc1[:], scalar1=1.0, scalar2=None, op0=ALU.add),
         r=[tag + "modc_a", tag + "modc_b"], w=[tag + "modc"])
    return sh, sc1


def ffn_phase(kb, tag, x_in, x_out, wg_d, wu_d, wd_d, sh_d, sc_d, g_d, tiles, dbg=99, bg=()):
    nc, S = kb.nc, kb.S
    with contextlib.ExitStack() as st:
        sb = lambda n, s, d: st.enter_context(nc.sbuf_tensor(tag + n, s, d))
        wg = sb("wg", [128, 8, DFF], BF16)
        wu = sb("wu", [128, 8, DFF], BF16)
        wd = sb("wd", [128, NF, D], BF16)
        G = sb("G", [128, 2, D], F32)
        NXB = 5
        xt = [sb("xt%d" % i, [128, D], F32) for i in range(NXB)]
        xn = [sb("xn%d" % i, [128, D], BF16) for i in range(2)]
        junk = sb("junk", [128, D], BF16)
        ss = sb("ss", [128, 4], F32)
        hT = sb("hT", [128, 8, 512], BF16)
        aT = sb("aT", [128, NF, 512], BF16)
        sg = [sb("sg%d" % i, [128, 512], F32) for i in range(2)]
        tmp = [sb("tmp%d" % i, [128, 512], F32) for i in range(2)]
        sh, sc1 = load_modc(kb, st, tag, sh_d, sc_d)
        S.dma("sp", G[:], g_d, w=[tag + "G0"])
        S.op("pool", lambda e: e.tensor_scalar(out=G[:], in0=G[:], scalar1=0.5, scalar2=0.0, op0=ALU.mult, op1=ALU.add),
             r=[tag + "G0"], w=[tag + "G"])
        wg_v = wg_d.rearrange("(kc p) f -> p kc f", p=128)
        wu_v = wu_d.rearrange("(kc p) f -> p kc f", p=128)
        wd_v = wd_d.rearrange("(f p) d -> p f d", p=128)
        nblk = (DFF + 511) // 512
        pre = wg_d.dtype == BF16
        qs = ("sp", "act") if pre else ("pool", "pool")
        for b in range(nblk if dbg >= -1 else 0):
            c0, c1 = b * 512, min(DFF, (b + 1) * 512)
            S.dma(qs[0], wg[:, :, c0:c1], wg_v[:, :, c0:c1], w=[tag + "wg%d" % b])
            S.dma(qs[1], wu[:, :, c0:c1], wu_v[:, :, c0:c1], w=[tag + "wu%d" % b])
        WDG = 4
        for b in range((NF + WDG - 1) // WDG if dbg >= -2 else 0):
            f0, f1 = b * WDG, min(NF, (b + 1) * WDG)
            S.dma(qs[b % 2], wd[:, f0:f1, :], wd_v[:, f0:f1, :], w=[tag + "wd%d" % b])
        bg = list(bg)
        groups = make_groups(tiles)
        xkey = lambda n: tag + "x%d" % (n % NXB)

        def load_x(n):
            S.dma("sp", xt[n % NXB][:], x_in[tiles[n][0] * 128:(tiles[n][0] + 1) * 128, :], w=[xkey(n)])

        loaded = 0
        n0 = 0
        for gi, grp in enumerate(groups):
            nt, strm = len(grp), grp[0][2]
            while loaded < min(len(tiles), n0 + NXB):
                load_x(loaded)
                loaded += 1
            ntok = nt * 128
            xts = [xt[(n0 + t) % NXB] for t in range(nt)]
            xkeys = [xkey(n0 + t) for t in range(nt)]
            dsts = [g_[1] for g_ in grp]
            n0 += nt
            for _ in range(3):
                if bg:
                    dst_, src_ = bg.pop(0)
                    S.dma("pool", dst_, src_, w=[tag + "bg%d" % len(bg)])
            if dbg >= 1:
                hkeys = norm_group(kb, tag, xts, xkeys, ss, junk, xn, hT, sc1, sh, strm, (0, 7), tag + "hT")
            for f in range(NF if dbg >= 2 else 0):
                pg, pu = kb.ps[1 + f % 2], kb.ps[3 + f % 2]
                kg, ku = "ps%d" % (1 + f % 2), "ps%d" % (3 + f % 2)
                blk = (f * 128) // 512
                for kc in range(8):
                    S.op("pe", lambda e, kc=kc, f=f, pg=pg, ntok=ntok: e.matmul(
                        pg[:, 0:ntok], lhsT=wg[:, kc, f * 128:(f + 1) * 128], rhs=hT[:, kc, 0:ntok],
                        start=(kc == 0), stop=(kc == 7)),
                        r=[tag + "wg%d" % blk] + [tag + "hT_%d_%d" % (t, kc) for t in range(nt)], w=[kg])
                for kc in range(8):
                    S.op("pe", lambda e, kc=kc, f=f, pu=pu, ntok=ntok: e.matmul(
                        pu[:, 0:ntok], lhsT=wu[:, kc, f * 128:(f + 1) * 128], rhs=hT[:, kc, 0:ntok],
                        start=(kc == 0), stop=(kc == 7)),
                        r=[tag + "wu%d" % blk] + [tag + "hT_%d_%d" % (t, kc) for t in range(nt)], w=[ku])
                S.op("act", lambda e, f=f, pg=pg, ntok=ntok: e.activation(out=sg[f % 2][:, 0:ntok], in_=pg[:, 0:ntok],
                                                               func=AF.Silu), r=[kg], w=[tag + "sg%d" % (f % 2)])
                S.op("dve", lambda e, f=f, pu=pu, ntok=ntok: e.tensor_tensor(out=aT[:, f, 0:ntok], in0=pu[:, 0:ntok],
                                                                  in1=sg[f % 2][:, 0:ntok], op=ALU.mult),
                     r=[ku, tag + "sg%d" % (f % 2)], w=[tag + "aT%d" % f])
            for t in range(nt):
                for dh in range(2 if dbg >= 3 else 0):
                    py = kb.ps[5 + dh]
                    ky = "ps%d" % (5 + dh)
                    for f in range(NF):
                        S.op("pe", lambda e, f=f, t=t, dh=dh, py=py: e.matmul(
                            py[:, 0:512], lhsT=aT[:, f, t * 128:(t + 1) * 128], rhs=wd[:, f, dh * 512:(dh + 1) * 512],
                            start=(f == 0), stop=(f == NF - 1)),
                            r=[tag + "aT%d" % f, tag + "wd%d" % (f // WDG)], w=[ky])
                    S.op("dve", lambda e, dh=dh, py=py, strm=strm: e.tensor_tensor(
                        out=tmp[dh][:], in0=py[:, 0:512], in1=G[:, strm, dh * 512:(dh + 1) * 512], op=ALU.mult),
                        r=[ky, tag + "G"], w=[tag + "tmp%d" % dh])
                    S.op("dve", lambda e, dh=dh, t=t, xts=xts: e.tensor_tensor(
                        out=xts[t][:, dh * 512:(dh + 1) * 512], in0=xts[t][:, dh * 512:(dh + 1) * 512],
                        in1=tmp[dh][:], op=ALU.add),
                        r=[tag + "tmp%d" % dh, xkeys[t]], w=[xkeys[t]])
                S.dma("sp", x_out[dsts[t] * 128:(dsts[t] + 1) * 128, :], xts[t][:], r=[xkeys[t]], w=[tag + "xo%d" % dsts[t]])
        while bg:
            dst_, src_ = bg.pop(0)
            S.dma("pool", dst_, src_, w=[tag + "bg%d" % len(bg)])
        return S.flush()


def mod_phase(kb, csT_d, wm_d, bm_d, mod_d):
    nc, S = kb.nc, kb.S
    NCOL = 1152
    with contextlib.ExitStack() as st:
        sb = lambda n, s, d: st.enter_context(nc.sbuf_tensor("m_" + n, s, d))
        cs = sb("cs", [128, 8, 3], F32)
        w = [sb("w%d" % l, [128, 8, NCOL], F32) for l in range(2)]
        bm = sb("bm", [3, 2, NCOL], F32)
        res = sb("res", [3, 2, NCOL], F32)
        S.dma("sp", cs[:], csT_d, w=["m_cs"])
        S.dma("sp", bm[:], bm_d, w=["m_bm"])
        for l in range(2):
            S.dma("sp" if l == 0 else "act", w[l][:], wm_d[l].rearrange("(kc p) f -> p kc f", p=128), w=["m_w%d" % l])
        S.op("act", lambda e: e.activation(out=cs[:], in_=cs[:], func=AF.Silu), r=["m_cs"], w=["m_cs"])
        i = 0
        for l in range(2):
            for c0 in range(0, NCOL, 512):
                c1 = min(NCOL, c0 + 512)
                p = kb.ps[i % 8]
                pk = "ps%d" % (i % 8)
                i += 1
                for kc in range(8):
                    S.op("pe", lambda e, kc=kc, l=l, c0=c0, c1=c1, p=p: e.matmul(
                        p[0:3, 0:c1 - c0], lhsT=cs[:, kc, :], rhs=w[l][:, kc, c0:c1], start=(kc == 0), stop=(kc == 7)),
                        r=["m_cs", "m_w%d" % l], w=[pk])
                S.op("dve", lambda e, l=l, c0=c0, c1=c1, p=p: e.tensor_tensor(
                    out=res[:, l, c0:c1], in0=p[0:3, 0:c1 - c0], in1=bm[:, l, c0:c1], op=ALU.add),
                    r=[pk, "m_bm"], w=["m_res"])
        S.dma("sp", mod_d, res[:], r=["m_res"], w=["m_out"])
        return S.flush()


def inproj0_phase(kb, tag, x_d, win_d, sh_d, sc_d, gains_d, cos_d, sin_d, qT_d, kT_d, va_d, f_d, tiles, needq=None):
    nc, S = kb.nc, kb.S
    with contextlib.ExitStack() as st:
        sb = lambda n, s_, d: st.enter_context(nc.sbuf_tensor(tag + n, s_, d))
        win = sb("win", [128, 8, 1536], BF16)
        gains = sb("gains", [128, 1024], F32)
        NROPE = cos_d.shape[0] // 128
        cosb = sb("cos", [128, NROPE, 32], F32)
        sinb = sb("sin", [128, NROPE, 32], F32)
        NXB = 6
        xt = [sb("xt%d" % i, [128, D], F32) for i in range(NXB)]
        xn = [sb("xn%d" % i, [128, D], BF16) for i in range(2)]
        junk = sb("junk", [128, D], BF16)
        ss2 = [sb("ss%d" % i, [128, 4], F32) for i in range(2)]
        hT2 = [sb("hT%d" % i, [128, 8, 512], BF16) for i in range(2)]
        qkg2 = [sb("qkg%d" % i, [128, 4, 1024], F32) for i in range(2)]
        sqg = sb("sqg", [128, 4, 1024], F32)
        ssq = sb("ssq", [128, 4, 16], F32)
        sq_flat = sqg[:].rearrange("p t c -> p (t c)")
        ta = [sq_flat[:, 0:2048].rearrange("p (t h j) -> p t h j", t=4, h=16),
              sq_flat[:, 2048:4096].rearrange("p (t h j) -> p t h j", t=4, h=16),
              sb("ta2", [128, 4, 16, 32], F32), sb("ta3", [128, 4, 16, 32], F32)]
        qkr = sb("qkr", [128, 4, 1024], BF16)
        qkT = [sb("qkT%d" % i, [64, 16, 512], BF16) for i in range(2)]
        vab = [sb("vab%d" % i, [128, 4, 128], BF16) for i in range(2)]
        fb_ = [sb("fb%d" % i, [128, 256], BF16) for i in range(2)]
        for i in range(2):
            S.op("pool", lambda e, i=i: e.memset(vab[i][:], 1.0), w=[tag + "vab%d" % i])
        sh, sc1 = load_modc(kb, st, tag, sh_d, sc_d)
        S.dma("sp", gains[:], gains_d, w=[tag + "gains"])
        S.dma("sp", cosb[:], cos_d.rearrange("(t p) j -> p t j", p=128), w=[tag + "cos"])
        S.dma("sp", sinb[:], sin_d.rearrange("(t p) j -> p t j", p=128), w=[tag + "sin"])
        win_v = win_d.rearrange("(kc p) f -> p kc f", p=128)
        for b in range(3):
            S.dma("pool", win[:, :, b * 512:(b + 1) * 512], win_v[:, :, b * 512:(b + 1) * 512], w=[tag + "win%d" % b])
        xkey = lambda n: tag + "x%d" % (n % NXB)
        loaded = 0
        nq_of = (lambda t: True) if needq is None else (lambda t: t in needq)
        groups = make_groups(tiles, key=lambda t: nq_of(t[1]))
        n0 = 0
        ginfo = []
        for gi, grp in enumerate(groups):
            ginfo.append((n0, len(grp)))
            n0 += len(grp)

        def stage_norm(gi):
            nonlocal loaded
            grp = groups[gi]
            n0_, nt_ = ginfo[gi]
            while loaded < min(len(tiles), n0_ + NXB):
                S.dma("sp", xt[loaded % NXB][:], x_d[tiles[loaded][0] * 128:(tiles[loaded][0] + 1) * 128, :],
                      w=[xkey(loaded)])
                loaded += 1
            xts_ = [xt[(n0_ + t) % NXB] for t in range(nt_)]
            xkeys_ = [xkey(n0_ + t) for t in range(nt_)]
            norm_group(kb, tag, xts_, xkeys_, ss2[gi % 2], junk, xn, hT2[gi % 2], sc1, sh, grp[0][2], (0, 7),
                       tag + "hT%d" % (gi % 2), ktag=tag + "n%d" % (gi % 2))

        qkeys_of = {}

        def stage_proj(gi):
            grp = groups[gi]
            nt, strm, t0 = len(grp), grp[0][2], grp[0][1]
            nq = nq_of(t0)
            ntok = nt * 128
            hT = hT2[gi % 2]
            hkp = tag + "hT%d" % (gi % 2)
            qT_g = qkT[gi % 2]
            gk = tag + "qkT%d" % (gi % 2)
            H0 = 0 if nq else 12
            nh = 16 - H0
            C0 = H0 * 64
            qkg = qkg2[gi % 2]
            qkeys = []
            for t in range(nt):
                tt = t0 + t
                b = tt % 2
                bank0 = 1 + 3 * (t % 2)
                for c in (range(3) if nq else (1, 2)):
                    p = kb.ps[bank0 + c]
                    for kc in range(8):
                        S.op("pe", lambda e, kc=kc, c=c, t=t, p=p, hT=hT: e.matmul(
                            p[:, 0:512], lhsT=hT[:, kc, t * 128:(t + 1) * 128], rhs=win[:, kc, c * 512:(c + 1) * 512],
                            start=(kc == 0), stop=(kc == 7)),
                            r=[hkp + "_%d_%d" % (t, kc), tag + "win%d" % c], w=["ps%d" % (bank0 + c)])
                if nq:
                    S.op("act", lambda e, t=t, bank0=bank0: e.activation(out=qkg[:, t, 0:512], in_=kb.ps[bank0][:, 0:512],
                                                                         func=AF.Copy),
                         r=["ps%d" % bank0], w=[tag + "qkg%d_%da" % (gi % 2, t)])
                    qkeys.append(tag + "qkg%d_%da" % (gi % 2, t))
                S.op("dve", lambda e, t=t, bank0=bank0: e.tensor_copy(out=qkg[:, t, 512:1024], in_=kb.ps[bank0 + 1][:, 0:512]),
                     r=["ps%d" % (bank0 + 1)], w=[tag + "qkg%d_%db" % (gi % 2, t)])
                qkeys.append(tag + "qkg%d_%db" % (gi % 2, t))
                S.op("act", lambda e, b=b, bank0=bank0: e.activation(
                    out=vab[b][:, :, 0:64], in_=kb.ps[bank0 + 2][:, 0:256].rearrange("p (k d) -> p k d", d=64), func=AF.Copy),
                    r=["ps%d" % (bank0 + 2)], w=[tag + "vab%d" % b])
                S.op("act", lambda e, b=b, bank0=bank0: e.activation(out=fb_[b][:], in_=kb.ps[bank0 + 2][:, 256:512], func=AF.Copy),
                     r=["ps%d" % (bank0 + 2)], w=[tag + "fb%d" % b])
                S.dma("sp", va_d[tt * 128:(tt + 1) * 128, :], vab[b][:].rearrange("p k d -> p (k d)"),
                      r=[tag + "vab%d" % b], w=[tag + "vao%d" % tt])
                S.dma("sp", f_d[tt * 128:(tt + 1) * 128, :], fb_[b][:], r=[tag + "fb%d" % b], w=[tag + "fo%d" % tt])
            qkeys_of[gi] = qkeys

        def stage_chain(gi):
            grp = groups[gi]
            nt, strm, t0 = len(grp), grp[0][2], grp[0][1]
            nq = nq_of(t0)
            ntok = nt * 128
            hT = hT2[gi % 2]
            hkp = tag + "hT%d" % (gi % 2)
            qT_g = qkT[gi % 2]
            gk = tag + "qkT%d" % (gi % 2)
            H0 = 0 if nq else 12
            nh = 16 - H0
            C0 = H0 * 64
            qkg = qkg2[gi % 2]
            qkeys = qkeys_of[gi]
            Q = qkg[:, 0:nt, C0:1024]
            Q4 = Q.rearrange("p t (h d) -> p t h d", d=64)
            SQ4 = sqg[:, 0:nt, C0:1024].rearrange("p t (h d) -> p t h d", d=64)
            SS = ssq[:, 0:nt, H0:16]
            qk_ = tag + "Q"
            S.op("dve", lambda e, Q=Q, nt=nt, C0=C0: e.tensor_tensor(out=sqg[:, 0:nt, C0:1024], in0=Q, in1=Q, op=ALU.mult),
                 r=qkeys, w=[tag + "sqg"])
            S.op("dve", lambda e, SQ4=SQ4, SS=SS: e.tensor_reduce(out=SS, in_=SQ4, axis=AX.X, op=ALU.add),
                 r=[tag + "sqg"], w=[tag + "ssq"])
            S.op("dve", lambda e, SS=SS: e.tensor_scalar(out=SS, in0=SS, scalar1=1.0 / 64, scalar2=EPS, op0=ALU.mult, op1=ALU.add),
                 r=[tag + "ssq"], w=[tag + "ssq"])
            S.op("act", lambda e, SS=SS: e.activation(out=SS, in_=SS, func=AF.Sqrt), r=[tag + "ssq"], w=[tag + "ssq"])
            S.op("dve", lambda e, SS=SS: e.reciprocal(out=SS, in_=SS), r=[tag + "ssq"], w=[tag + "ssq"])
            S.op("dve", lambda e, Q4=Q4, SS=SS, nt=nt, nh=nh: e.tensor_tensor(
                out=Q4, in0=Q4, in1=SS.unsqueeze(3).to_broadcast([128, nt, nh, 64]), op=ALU.mult),
                r=qkeys + [tag + "ssq"], w=[qk_])
            S.op("dve", lambda e, Q=Q, nt=nt, nh=nh, C0=C0: e.tensor_tensor(
                out=Q, in0=Q, in1=gains[:, C0:1024].unsqueeze(1).to_broadcast([128, nt, nh * 64]), op=ALU.mult),
                r=[qk_, tag + "gains"], w=[qk_])
            R4 = qkr[:, 0:nt, C0:1024].rearrange("p t (h d) -> p t h d", d=64)
            rk = tag + "qkr"
            if strm == 0:
                X1, X2 = Q4[:, :, :, 0:32], Q4[:, :, :, 32:64]
                cb = cosb[:, t0:t0 + nt, :].unsqueeze(2).to_broadcast([128, nt, nh, 32])
                sb_ = sinb[:, t0:t0 + nt, :].unsqueeze(2).to_broadcast([128, nt, nh, 32])
                tav = [ta[i][:, 0:nt, H0:16, :] for i in range(4)]
                tk = [tag + "sqg", tag + "sqg", tag + "ta2", tag + "ta3"]
                S.op("dve", lambda e, X1=X1, cb=cb, tav=tav: e.tensor_tensor(out=tav[0], in0=X1, in1=cb, op=ALU.mult),
                     r=[qk_, tag + "cos"], w=[tk[0]])
                S.op("dve", lambda e, X2=X2, sb_=sb_, tav=tav: e.tensor_tensor(out=tav[1], in0=X2, in1=sb_, op=ALU.mult),
                     r=[qk_, tag + "sin"], w=[tk[1]])
                S.op("pool", lambda e, X2=X2, cb=cb, tav=tav: e.tensor_tensor(out=tav[2], in0=X2, in1=cb, op=ALU.mult),
                     r=[qk_, tag + "cos"], w=[tk[2]])
                S.op("pool", lambda e, X1=X1, sb_=sb_, tav=tav: e.tensor_tensor(out=tav[3], in0=X1, in1=sb_, op=ALU.mult),
                     r=[qk_, tag + "sin"], w=[tk[3]])
                S.op("dve", lambda e, R4=R4, tav=tav: e.tensor_tensor(out=R4[:, :, :, 0:32], in0=tav[0], in1=tav[1],
                                                                      op=ALU.subtract), r=[tk[0], tk[1]], w=[rk + "a"])
                S.op("pool", lambda e, R4=R4, tav=tav: e.tensor_tensor(out=R4[:, :, :, 32:64], in0=tav[2], in1=tav[3],
                                                                       op=ALU.add), r=[tk[2], tk[3]], w=[rk + "b"])
            else:
                S.op("dve", lambda e, Q=Q, nt=nt, C0=C0: e.tensor_copy(out=qkr[:, 0:nt, C0:1024], in_=Q),
                     r=[qk_], w=[rk + "a", rk + "b"])
            gks = []
            for t in range(nt):
                for hb in ((0, 1) if nq else (1,)):
                    h_lo = max(H0, hb * 8)
                    nhh = (hb + 1) * 8 - h_lo
                    bi = 1 + 3 * (t % 2) + hb
                    pT = kb.ps[bi][:].bitcast(BF16)
                    pk = "ps%d" % bi
                    for hh in range(nhh):
                        h = h_lo + hh
                        S.op("pe", lambda e, h=h, hh=hh, pT=pT, t=t: e.transpose(
                            out=pT[0:64, hh * 128:(hh + 1) * 128], in_=qkr[:, t, h * 64:(h + 1) * 64],
                            identity=kb.ident[:]), r=[rk + "a", rk + "b", "ident"], w=[pk])
                    src = pT[0:64, 0:nhh * 128].rearrange("p (h t) -> p h t", h=nhh)
                    dst = qT_g[:, h_lo:h_lo + nhh, t * 128:(t + 1) * 128]
                    k_ = gk + "_%d_%d" % (t, hb)
                    gks.append(k_)
                    if hb == 0:
                        S.op("act", lambda e, src=src, dst=dst: e.activation(out=dst, in_=src, func=AF.Copy), r=[pk], w=[k_])
                    else:
                        S.op("dve", lambda e, src=src, dst=dst: e.tensor_copy(out=dst, in_=src), r=[pk], w=[k_])
            tok0 = t0 * 128
            if nq:
                S.dma("sp", qT_d[:, :, tok0:tok0 + ntok], qT_g[:, 0:12, 0:ntok], r=gks, w=[tag + "qo%d" % gi])
            S.dma("sp", kT_d[:, :, tok0:tok0 + ntok], qT_g[:, 12:16, 0:ntok], r=gks, w=[tag + "ko%d" % gi])

        stage_norm(0)
        stage_proj(0)
        for gi in range(len(groups)):
            if gi + 1 < len(groups):
                stage_norm(gi + 1)
                stage_proj(gi + 1)
            stage_chain(gi)
        return S.flush()


def fourier_phase(kb, tag, catT, f_all_d, fc_d, tabs, nb2=20):
    nc, S = kb.nc, kb.S
    with contextlib.ExitStack() as st:
        sb = lambda n, s, d: st.enter_context(nc.sbuf_tensor(tag + n, s, d))
        xs = sb("xs", [128, 64, 256], BF16)
        A = [sb("Are", [128, 128, 128], BF16), sb("Aim", [128, 128, 128], BF16)]
        c128 = sb("c128", [128, 128], BF16)
        ns128 = sb("ns128", [128, 128], BF16)
        tw = {n: sb(n, [128, 128, 2 * nb2], BF16) for n in ("twc", "tws", "twns")}
        dd = {n: sb(n, [128, 2, 256], BF16) for n in ("dc", "ds")}
        O = [sb("Ore", [128, 128, 2, nb2], BF16), sb("Oim", [128, 128, 2, nb2], BF16)]
        S.dma("sp", xs[:].rearrange("p k f -> p (k f)"), f_all_d.rearrange("(a b) f -> a (b f)", b=64), w=[tag + "xs"])
        S.dma("sp", c128[:], tabs["c128"], w=[tag + "c128"])
        S.dma("sp", ns128[:], tabs["ns128"], w=[tag + "ns128"])
        for n in tw:
            S.dma("act", tw[n][:], tabs[n], w=[tag + n])
        for n in dd:
            S.dma("act", dd[n][:], tabs[n], w=[tag + n])
        tabA = [(c128, tag + "c128"), (ns128, tag + "ns128")]
        for fb in range(32):
            for ri in range(2):
                p = kb.ps[ri * 2 + fb % 2]
                pk = "ps%d" % (ri * 2 + fb % 2)
                for i in range(4):
                    fp = fb * 4 + i
                    for f2 in range(2):
                        lhsT = xs[:, :, 2 * fp + f2]
                        S.op("pe", lambda e, lhsT=lhsT, p=p, i=i, ri=ri, f2=f2: e.matmul(
                            p[f2 * 64:(f2 + 1) * 64, i * 128:(i + 1) * 128], lhsT=lhsT, rhs=tabA[ri][0][:],
                            start=True, stop=True),
                            r=[tag + "xs", tabA[ri][1]], w=[pk])
                dst = A[ri][:, fb * 4:(fb + 1) * 4, :]
                src = p[:, 0:512].rearrange("p (a n) -> p a n", a=4)
                if ri == 0:
                    S.op("act", lambda e, dst=dst, src=src: e.activation(out=dst, in_=src, func=AF.Copy),
                         r=[pk], w=[tag + "A%d_%d" % (ri, fb)])
                else:
                    S.op("dve", lambda e, dst=dst, src=src: e.tensor_copy(out=dst, in_=src),
                         r=[pk], w=[tag + "A%d_%d" % (ri, fb)])
        Akeys = [[tag + "A%d_%d" % (ri, fb) for fb in range(32)] for ri in range(2)]
        W2 = 2 * nb2
        NPB = 512 // W2
        nbanks = (128 + NPB - 1) // NPB
        for nb in range(nbanks):
            n1s = list(range(nb * NPB, min(128, (nb + 1) * NPB)))
            for ri in range(2):
                p = kb.ps[4 + ri * 2 + nb % 2]
                pk = "ps%d" % (4 + ri * 2 + nb % 2)
                for i, n1 in enumerate(n1s):
                    if ri == 0:
                        terms = [(A[0], "twc", 0), (A[1], "tws", 1)]
                    else:
                        terms = [(A[1], "twc", 1), (A[0], "twns", 0)]
                    for ti, (At, tn, ai) in enumerate(terms):
                        S.op("pe", lambda e, At=At, tn=tn, n1=n1, p=p, i=i, ti=ti: e.matmul(
                            p[:, i * W2:(i + 1) * W2], lhsT=At[:, :, n1], rhs=tw[tn][:, n1, :],
                            start=(ti == 0), stop=(ti == 1)),
                            r=Akeys[ai] + [tag + tn], w=[pk])
                dst = O[ri][:, n1s[0]:n1s[-1] + 1, :, :]
                src = p[:, 0:len(n1s) * W2].rearrange("p (a f n) -> p a f n", a=len(n1s), f=2)
                if ri == 0:
                    S.op("act", lambda e, dst=dst, src=src: e.activation(out=dst, in_=src, func=AF.Copy),
                         r=[pk], w=[tag + "O%d_%d" % (ri, nb)])
                else:
                    S.op("dve", lambda e, dst=dst, src=src: e.tensor_copy(out=dst, in_=src),
                         r=[pk], w=[tag + "O%d_%d" % (ri, nb)])
        Okeys = [[tag + "O%d_%d" % (ri, nb) for nb in range(nbanks)] for ri in range(2)]
        for ch in range(2):
            for nb in range(nb2 // 4):
                p = kb.ps[(ch * 5 + nb) % 4]
                pk = "ps%d" % ((ch * 5 + nb) % 4)
                k = 0
                for f2 in range(2):
                    for ri, dn in ((0, "dc"), (1, "ds")):
                        rhs = O[ri][:, :, f2, nb * 4:(nb + 1) * 4].rearrange("p n a -> p a n")
                        S.op("pe", lambda e, rhs=rhs, dn=dn, f2=f2, ch=ch, p=p, k=k: e.matmul(
                            p[:, 0:512], lhsT=dd[dn][:, f2, ch * 128:(ch + 1) * 128], rhs=rhs,
                            start=(k == 0), stop=(k == 3)),
                            r=Okeys[ri] + [tag + dn], w=[pk])
                        k += 1
                dst = catT[:, 6 + ch, nb * 512:(nb + 1) * 512]
                if nb % 2 == 0:
                    S.op("act", lambda e, dst=dst, p=p: e.activation(out=dst, in_=p[:, 0:512], func=AF.Copy),
                         r=[pk], w=[tag + "cat%d_%d" % (ch, nb)])
                else:
                    S.op("dve", lambda e, dst=dst, p=p: e.tensor_copy(out=dst, in_=p[:, 0:512]),
                         r=[pk], w=[tag + "cat%d_%d" % (ch, nb)])
        fcs = sb("fcs", [128, 2, 256], BF16)
        c256 = sb("c256", [128, 2, 256], BF16)
        s256 = sb("s256", [128, 2, 256], BF16)
        dcf = sb("dcf", [128, 128], BF16)
        ndsf = sb("ndsf", [128, 128], BF16)
        Z = [[sb("Z%d_%d" % (a, b), [128, 256], BF16) for b in range(2)] for a in range(2)]
        S.dma("sp", fcs[:], fc_d.rearrange("(kc p) f -> p kc f", p=128), w=[tag + "fcs"])
        S.dma("sp", c256[:], tabs["c256"], w=[tag + "c256"])
        S.dma("sp", s256[:], tabs["s256"], w=[tag + "s256"])
        S.dma("sp", dcf[:], tabs["dcf"], w=[tag + "dcf"])
        S.dma("sp", ndsf[:], tabs["ndsf"], w=[tag + "ndsf"])
        for fch in range(2):
            for ti, (tb, tk) in enumerate(((c256, "c256"), (s256, "s256"))):
                p = kb.ps[4 + fch * 2 + ti]
                pk = "ps%d" % (4 + fch * 2 + ti)
                for kc in range(2):
                    S.op("pe", lambda e, fch=fch, tb=tb, kc=kc, p=p: e.matmul(
                        p[:, 0:256], lhsT=fcs[:, kc, fch * 128:(fch + 1) * 128], rhs=tb[:, kc, :],
                        start=(kc == 0), stop=(kc == 1)), r=[tag + "fcs", tag + tk], w=[pk])
                S.op("dve" if ti else "act",
                     (lambda e, fch=fch, ti=ti, p=p: e.tensor_copy(out=Z[fch][ti][:], in_=p[:, 0:256])) if ti else
                     (lambda e, fch=fch, ti=ti, p=p: e.activation(out=Z[fch][ti][:], in_=p[:, 0:256], func=AF.Copy)),
                     r=[pk], w=[tag + "Z%d_%d" % (fch, ti)])
        for ch in range(2):
            p = kb.ps[ch]
            pk = "ps%d" % ch
            S.op("pe", lambda e, ch=ch, p=p: e.matmul(p[:, 0:256], lhsT=dcf[:], rhs=Z[ch][0][:], start=True, stop=False),
                 r=[tag + "dcf", tag + "Z%d_0" % ch], w=[pk])
            S.op("pe", lambda e, ch=ch, p=p: e.matmul(p[:, 0:256], lhsT=ndsf[:], rhs=Z[ch][1][:], start=False, stop=True),
                 r=[tag + "ndsf", tag + "Z%d_1" % ch], w=[pk])
            S.op("act", lambda e, ch=ch, p=p: e.activation(out=catT[:, 6 + ch, nb2 * 128:nb2 * 128 + 256], in_=p[:, 0:256], func=AF.Copy),
                 r=[pk], w=[tag + "catc%d" % ch])
        return S.flush()


def fourier_tables(j):
    import ml_dtypes
    bf = lambda a: np.ascontiguousarray(a.astype(np.float32)).astype(ml_dtypes.bfloat16)
    k1 = np.arange(128)[:, None]
    n1 = np.arange(128)[None, :]
    T = {}
    T["c128"] = bf(np.cos(2 * np.pi * k1 * n1 / 128))
    T["ns128"] = bf(-np.sin(2 * np.pi * k1 * n1 / 128))
    k2 = np.arange(64)
    NB2 = np.array(list(range(18)) + [62, 63])
    nb2 = len(NB2)
    n = 128 * NB2[None, None, :] + np.arange(128)[None, :, None]
    th = 2 * np.pi * ((n * k2[:, None, None]) % 8192) / 8192.0 \
        + (np.pi / 2) * ((j * (n + k2[:, None, None])) % 4)
    twc = np.zeros((2, 64, 128, 2, nb2))
    tws = np.zeros((2, 64, 128, 2, nb2))
    for f2 in range(2):
        twc[f2, :, :, f2, :] = np.cos(th)
        tws[f2, :, :, f2, :] = np.sin(th)
    T["twc"] = bf(twc.reshape(128, 128, 2 * nb2))
    T["tws"] = bf(tws.reshape(128, 128, 2 * nb2))
    T["twns"] = bf(-tws.reshape(128, 128, 2 * nb2))
    sc = 1.0 / np.sqrt(8192.0 * 64.0)
    dc = np.zeros((4, 32, 2, 4, 64))
    ds = np.zeros((4, 32, 2, 4, 64))
    m = np.arange(64)[None, :]
    for f2 in range(2):
        c = (2 * np.arange(32) + f2)[:, None]
        for g in range(4):
            dc[g, :, f2, g, :] = np.cos(2 * np.pi * m * c / 64) * sc
            ds[g, :, f2, g, :] = np.sin(2 * np.pi * m * c / 64) * sc
    T["dc"] = bf(dc.reshape(128, 2, 256))
    T["ds"] = bf(ds.reshape(128, 2, 256))
    kk = np.arange(256)[:, None]
    nn = np.arange(256)[None, :]
    c256 = np.cos(2 * np.pi * kk * nn / 256).reshape(2, 128, 256).transpose(1, 0, 2)
    s256 = np.sin(2 * np.pi * kk * nn / 256).reshape(2, 128, 256).transpose(1, 0, 2)
    T["c256"] = bf(c256)
    T["s256"] = bf(s256)
    scc = 1.0 / np.sqrt(256.0 * 64.0)
    dcf = np.zeros((2, 64, 2, 64))
    dsf = np.zeros((2, 64, 2, 64))
    cc = np.arange(64)[:, None]
    for g in range(2):
        dcf[g, :, g, :] = np.cos(2 * np.pi * m * cc / 64) * scc
        dsf[g, :, g, :] = np.sin(2 * np.pi * m * cc / 64) * scc
    T["dcf"] = bf(dcf.reshape(128, 128))
    T["ndsf"] = bf(-dsf.reshape(128, 128))
    return T


FTAB_SHAPES = dict(c128=[128, 128], ns128=[128, 128], twc=[128, 128, 40], tws=[128, 128, 40], twns=[128, 128, 40],
                   dc=[128, 2, 256], ds=[128, 2, 256], c256=[128, 2, 256], s256=[128, 2, 256],
                   dcf=[128, 128], ndsf=[128, 128])


NKC = 66


def attn0_phase(kb, tag, catT, qT_d, kT_all_d, v_all_d):
    nc, S = kb.nc, kb.S
    with contextlib.ExitStack() as st:
        sb = lambda n, s, d: st.enter_context(nc.sbuf_tensor(tag + n, s, d))
        V = sb("V", [128, NKC, 512], BF16)
        kT = [sb("kT%d" % i, [128, NKC * 128], BF16) for i in range(2)]
        qTs = [sb("qT%d" % i, [128, 3, NACTTOK], BF16) for i in range(2)]
        for i in range(2):
            S.op("pool", lambda e, i=i: e.memset(kT[i][64:128, :], 0.0), w=[tag + "kTz%d" % i])
            S.op("pool", lambda e, i=i: e.memset(qTs[i][64:128, :, :], 0.0), w=[tag + "qTz%d" % i])
        pT = [sb("pT%d" % i, [128, 512], BF16) for i in range(3)]
        rinv = [sb("rinv%d" % i, [128, 512], F32) for i in range(2)]
        v_v = v_all_d.rearrange("(c p) e -> p c e", p=128)
        for i in range(3):
            S.dma("sp", V[:, i * 22:(i + 1) * 22, :], v_v[:, i * 22:(i + 1) * 22, :], w=[tag + "V%d" % i])
        blk = 0
        pending = None
        for kvh in range(4):
            kb_, kk = kT[kvh % 2], tag + "kT%d" % (kvh % 2)
            qb_, qk_ = qTs[kvh % 2], tag + "qT%d" % (kvh % 2)
            S.dma("sp", kb_[0:64, :], kT_all_d[:, kvh, :], w=[kk])
            S.dma("act", qb_[0:64, :, 0:2304], qT_d[:, 3 * kvh:3 * kvh + 3, 0:2304], w=[qk_ + "a"])
            S.dma("act", qb_[0:64, :, 2304:NACTTOK], qT_d[:, 3 * kvh:3 * kvh + 3, 7936:NTOKALL], w=[qk_ + "b"])
            for g in range(3):
                h = 3 * kvh + g
                for qb in range(6):
                    if qb < 5:
                        q0, nq, chunks = qb * 512, 512, list(range(NKC))
                    else:
                        q0, nq, chunks = NLH, 256, [64, 65]
                    po = kb.ps[3 + blk % 2]
                    pok = "ps%d" % (3 + blk % 2)
                    ri = rinv[blk % 2]
                    rik = tag + "rinv%d" % (blk % 2)
                    blk += 1

                    def emit_S(c, kb_=kb_, qb_=qb_, g=g, q0=q0, nq=nq, kk=kk, qk_=qk_, kvh=kvh):
                        S.op("pe", lambda e: e.matmul(kb.ps[c % 3][:, 0:nq], lhsT=kb_[:, c * 128:(c + 1) * 128],
                                                      rhs=qb_[:, g, q0:q0 + nq], start=True, stop=True),
                             r=[kk, qk_ + "a", qk_ + "b", tag + "kTz%d" % (kvh % 2), tag + "qTz%d" % (kvh % 2)],
                             w=["ps%d" % (c % 3)])

                    def emit_E(c, nq=nq):
                        S.op("act", lambda e: e.activation(out=pT[c % 3][:, 0:nq], in_=kb.ps[c % 3][:, 0:nq],
                                                           func=AF.Exp, scale=0.125),
                             r=["ps%d" % (c % 3)], w=[tag + "pT%d" % (c % 3)])

                    def emit_PV(c, first, last, po=po, pok=pok, kvh=kvh, nq=nq):
                        S.op("pe", lambda e: e.matmul(po[:, 0:nq], lhsT=V[:, c, kvh * 128:(kvh + 1) * 128],
                                                      rhs=pT[c % 3][:, 0:nq], start=first, stop=last),
                             r=[tag + "V%d" % (c // 22), tag + "pT%d" % (c % 3)], w=[pok])

                    emit_S(chunks[0])
                    if len(chunks) > 1:
                        emit_S(chunks[1])
                    for i, c in enumerate(chunks):
                        emit_E(c)
                        if i + 2 < len(chunks):
                            emit_S(chunks[i + 2])
                        emit_PV(c, i == 0, i == len(chunks) - 1)
                        if i == 1 and pending is not None:
                            pending()
                            pending = None

                    def fin(po=po, pok=pok, ri=ri, rik=rik, h=h, q0=q0, nq=nq):
                        S.op("dve", lambda e: e.reciprocal(out=ri[64:128, 0:nq], in_=po[64:128, 0:nq]), r=[pok], w=[rik])
                        pb = (h % 2) * 64
                        S.op("dve", lambda e: e.tensor_tensor(out=catT[pb:pb + 64, h // 2, q0:q0 + nq],
                                                              in0=po[0:64, 0:nq], in1=ri[64:128, 0:nq], op=ALU.mult),
                             r=[pok, rik], w=[tag + "cat_%d_%d" % (h, q0)])
                    pending = fin
        if pending is not None:
            pending()
        return S.flush()


def wout_phase(kb, tag, catT, x_in, x_out, wout_d, g_d, tiles):
    nc, S = kb.nc, kb.S
    with contextlib.ExitStack() as st:
        sb = lambda n, s, d: st.enter_context(nc.sbuf_tensor(tag + n, s, d))
        wo = sb("wo", [128, 8, D], BF16)
        G = sb("G", [128, 2, D], F32)
        NXB = 4
        xt = [sb("xt%d" % i, [128, D], F32) for i in range(NXB)]
        tmp = [sb("tmp%d" % i, [128, 512], F32) for i in range(2)]
        S.dma("sp", G[:], g_d, w=[tag + "G"])
        wv = wout_d.rearrange("(kc p) f -> p kc f", p=128)
        for b in range(2):
            S.dma("pool", wo[:, :, b * 512:(b + 1) * 512], wv[:, :, b * 512:(b + 1) * 512], w=[tag + "wo%d" % b])
        for n, (src, t, strm) in enumerate(tiles):
            xk = tag + "x%d" % (n % NXB)
            xb = xt[n % NXB]
            S.dma("sp", xb[:], x_in[src * 128:(src + 1) * 128, :], w=[xk])
            for dh in range(2):
                p = kb.ps[(2 * t + dh) % 4]
                pk = "ps%d" % ((2 * t + dh) % 4)
                for kc in range(8):
                    S.op("pe", lambda e, kc=kc, t=t, dh=dh, p=p: e.matmul(
                        p[:, 0:512], lhsT=catT[:, kc, t * 128:(t + 1) * 128], rhs=wo[:, kc, dh * 512:(dh + 1) * 512],
                        start=(kc == 0), stop=(kc == 7)), r=[tag + "wo%d" % dh], w=[pk])
                S.op("dve", lambda e, dh=dh, p=p, strm=strm: e.tensor_tensor(
                    out=tmp[dh][:], in0=p[:, 0:512], in1=G[:, strm, dh * 512:(dh + 1) * 512], op=ALU.mult),
                    r=[pk, tag + "G"], w=[tag + "tmp%d" % dh])
                S.op("pool", lambda e, dh=dh, xb=xb: e.tensor_tensor(
                    out=xb[:, dh * 512:(dh + 1) * 512], in0=xb[:, dh * 512:(dh + 1) * 512], in1=tmp[dh][:], op=ALU.add),
                    r=[tag + "tmp%d" % dh, xk], w=[xk])
            S.dma("sp", x_out[t * 128:(t + 1) * 128, :], xb[:], r=[xk], w=[tag + "xo%d" % t])
        return S.flush()


def inproj1_phase(kb, tag, x_d, win_d, sh_d, sc_d, uT_d, qT_d, kT_d, va_d, tiles):
    nc, S = kb.nc, kb.S
    with contextlib.ExitStack() as st:
        sb = lambda n, s, d: st.enter_context(nc.sbuf_tensor(tag + n, s, d))
        win = sb("win", [128, 8, 2560], BF16)
        NXB = 8
        xt = [sb("xt%d" % i, [128, D], F32) for i in range(NXB)]
        xn = [sb("xn%d" % i, [128, D], BF16) for i in range(2)]
        junk = sb("junk", [128, D], BF16)
        ss = sb("ss", [128, 4], F32)
        hT = sb("hT", [128, 8, 512], BF16)
        sg = [sb("sg%d" % i, [128, 512], F32) for i in range(2)]
        uT = [sb("uT%d" % i, [128, 4, 512], BF16) for i in range(2)]
        qT = [sb("qT%d" % i, [128, 4, 512], BF16) for i in range(2)]
        kT = [sb("kT%d" % i, [128, 4, 512], BF16) for i in range(2)]
        vab = [sb("vab%d" % i, [128, 8, 128], BF16) for i in range(2)]
        for i in range(2):
            S.op("pool", lambda e, i=i: e.memset(vab[i][:], 1.0), w=[tag + "vab%d" % i])
        sh, sc1 = load_modc(kb, st, tag, sh_d, sc_d)
        win_v = win_d.rearrange("(kc p) f -> p kc f", p=128)
        for b in range(5):
            S.dma("pool", win[:, :, b * 512:(b + 1) * 512], win_v[:, :, b * 512:(b + 1) * 512], w=[tag + "win%d" % b])
        xkey = lambda n: tag + "x%d" % (n % NXB)
        loaded = 0
        groups = make_groups(tiles)
        bank = 0
        n0 = 0
        for gi, grp in enumerate(groups):
            nt, strm, t0 = len(grp), grp[0][2], grp[0][1]
            while loaded < min(len(tiles), n0 + NXB):
                S.dma("sp", xt[loaded % NXB][:], x_d[tiles[loaded][0] * 128:(tiles[loaded][0] + 1) * 128, :],
                      w=[xkey(loaded)])
                loaded += 1
            ntok = nt * 128
            tok0 = t0 * 128
            xts = [xt[(n0 + t) % NXB] for t in range(nt)]
            xkeys = [xkey(n0 + t) for t in range(nt)]
            n0 += nt
            norm_group(kb, tag, xts, xkeys, ss, junk, xn, hT, sc1, sh, strm, (0, 7), tag + "hT")
            hk = [tag + "hT_%d_%d" % (t, kc) for t in range(nt) for kc in range(8)]
            gb2 = gi % 2

            def fm(col0, ntok=ntok):
                nonlocal bank
                bi = 1 + bank % 6
                bank += 1
                p = kb.ps[bi]
                for kc in range(8):
                    S.op("pe", lambda e, kc=kc, p=p: e.matmul(
                        p[:, 0:ntok], lhsT=win[:, kc, col0:col0 + 128], rhs=hT[:, kc, 0:ntok],
                        start=(kc == 0), stop=(kc == 7)), r=hk + [tag + "win%d" % (col0 // 512)], w=["ps%d" % bi])
                return p, "ps%d" % bi

            if strm == 0:
                for c in range(4):
                    pa, pak = fm(c * 128)
                    pb, pbk = fm(512 + c * 128)
                    S.op("act", lambda e, pb=pb, c=c, ntok=ntok: e.activation(out=sg[c % 2][:, 0:ntok], in_=pb[:, 0:ntok],
                                                                              func=AF.Sigmoid),
                         r=[pbk], w=[tag + "sg%d" % (c % 2)])
                    S.op("dve", lambda e, pa=pa, c=c, ntok=ntok, gb2=gb2: e.tensor_tensor(
                        out=uT[gb2][:, c, 0:ntok], in0=pa[:, 0:ntok], in1=sg[c % 2][:, 0:ntok], op=ALU.mult),
                        r=[pak, tag + "sg%d" % (c % 2)], w=[tag + "uT%d_%d" % (gb2, c)])
                S.dma("sp", uT_d[:, :, tok0:tok0 + ntok], uT[gb2][:, :, 0:ntok],
                      r=[tag + "uT%d_%d" % (gb2, c) for c in range(4)], w=[tag + "uo%d" % gi])
                for c in range(4):
                    pq, pqk = fm(1024 + c * 128)
                    S.op("act", lambda e, pq=pq, c=c, ntok=ntok, gb2=gb2: e.activation(
                        out=qT[gb2][:, c, 0:ntok], in_=pq[:, 0:ntok], func=AF.Copy),
                        r=[pqk], w=[tag + "qT%d_%d" % (gb2, c)])
                S.dma("sp", qT_d[:, :, tok0:tok0 + ntok], qT[gb2][:, :, 0:ntok],
                      r=[tag + "qT%d_%d" % (gb2, c) for c in range(4)], w=[tag + "qo%d" % gi])
            for c in range(4):
                pk_, pkk = fm(1536 + c * 128)
                S.op("dve", lambda e, pk_=pk_, c=c, ntok=ntok, gb2=gb2: e.tensor_copy(
                    out=kT[gb2][:, c, 0:ntok], in_=pk_[:, 0:ntok]), r=[pkk], w=[tag + "kT%d_%d" % (gb2, c)])
            S.dma("sp", kT_d[:, :, tok0:tok0 + ntok], kT[gb2][:, :, 0:ntok],
                  r=[tag + "kT%d_%d" % (gb2, c) for c in range(4)], w=[tag + "ko%d" % gi])
            for t in range(nt):
                tt = t0 + t
                b = tt % 2
                bi = 1 + bank % 6
                bank += 1
                p = kb.ps[bi]
                for kc in range(8):
                    S.op("pe", lambda e, kc=kc, t=t, p=p: e.matmul(
                        p[:, 0:512], lhsT=hT[:, kc, t * 128:(t + 1) * 128], rhs=win[:, kc, 2048:2560],
                        start=(kc == 0), stop=(kc == 7)), r=hk + [tag + "win4"], w=["ps%d" % bi])
                S.op("act", lambda e, b=b, p=p: e.activation(
                    out=vab[b][:, :, 0:64], in_=p[:, 0:512].rearrange("p (k d) -> p k d", d=64), func=AF.Copy),
                    r=["ps%d" % bi], w=[tag + "vab%d" % b])
                S.dma("sp", va_d[tt * 128:(tt + 1) * 128, :], vab[b][:].rearrange("p k d -> p (k d)"),
                      r=[tag + "vab%d" % b], w=[tag + "vao%d" % tt])
        return S.flush()


def conv_phase(kb, tag, catT, uT_d, msk_d, dww_d, dwb_d, lnw_d, lnb_d):
    nc, S = kb.nc, kb.S
    with contextlib.ExitStack() as st:
        sb = lambda n, s, d: st.enter_context(nc.sbuf_tensor(tag + n, s, d))
        uT = sb("uT", [128, 4, NLAT + 30], BF16)
        dwd = sb("dwd", [128, 4, 31, 128], BF16)
        dww = sb("dww", [128, 4, 31], F32)
        dwb = sb("dwb", [128, 4], F32)
        lnw = sb("lnw", [128, 4], F32)
        lnb = sb("lnb", [128, 4], F32)
        yT = [sb("yT%d" % c, [128, 512], F32) for c in range(4)]
        st6 = [sb("st6_%d" % i, [128, 6], F32) for i in range(2)]
        mv = [sb("mv%d" % i, [128, 2], F32) for i in range(2)]
        z = [sb("z%d" % i, [128, 512], BF16) for i in range(2)]
        msk = sb("msk", [128, 2], F32)
        S.dma("sp", msk[:], msk_d, w=[tag + "msk"])
        S.dma("sp", uT[:, :, 0:15], uT_d[:, :, NLH - 15:NLH], w=[tag + "uTa"])
        S.dma("sp", uT[:, :, 15:NLAT + 30], uT_d[:, :, 0:NLAT + 15], w=[tag + "uTb"])
        S.op("dve", lambda e: e.tensor_scalar(out=uT[:, :, 0:15], in0=uT[:, :, 0:15], scalar1=msk[:, 0:1], scalar2=None,
                                              op0=ALU.mult), r=[tag + "uTa", tag + "msk"], w=[tag + "uTa"])
        S.op("dve", lambda e: e.tensor_scalar(out=uT[:, :, NLAT + 15:NLAT + 30], in0=uT[:, :, NLAT + 15:NLAT + 30],
                                              scalar1=msk[:, 1:2], scalar2=None, op0=ALU.mult),
             r=[tag + "uTb", tag + "msk"], w=[tag + "uTb"])
        S.dma("sp", dww[:], dww_d, w=[tag + "dww"])
        S.dma("sp", dwb[:], dwb_d, w=[tag + "dwb"])
        S.dma("sp", lnw[:], lnw_d, w=[tag + "lnw"])
        S.dma("sp", lnb[:], lnb_d, w=[tag + "lnb"])
        for c in range(4):
            for j in range(31):
                eng = "dve" if (c * 31 + j) % 2 else "pool"
                S.op(eng, lambda e, c=c, j=j: e.tensor_scalar(out=dwd[:, c, j, :], in0=kb.ident[:],
                                                              scalar1=dww[:, c, j:j + 1], scalar2=0.0,
                                                              op0=ALU.mult, op1=ALU.add),
                     r=["ident", tag + "dww"], w=[tag + "dwd%d_%d" % (c, j)])
        for tb in range(NLAT // 512):
            for c in range(4):
                p = kb.ps[c % 2]
                pk = "ps%d" % (c % 2)
                for j in range(31):
                    S.op("pe", lambda e, c=c, j=j, tb=tb, p=p: e.matmul(
                        p[:, 0:512], lhsT=dwd[:, c, j, :], rhs=uT[:, c, tb * 512 + j:tb * 512 + j + 512],
                        start=(j == 0), stop=(j == 30)), r=[tag + "uTa", tag + "uTb", tag + "dwd%d_%d" % (c, j)], w=[pk])
                S.op("act", lambda e, c=c, p=p: e.activation(out=yT[c][:], in_=p[:, 0:512], func=AF.Identity,
                                                             bias=dwb[:, c:c + 1]),
                     r=[pk, tag + "dwb"], w=[tag + "yT%d" % c])
            for tt in range(4):
                tok = tb * 512 + tt * 128
                b = tt % 2
                p = kb.ps[2 + b]
                pk = "ps%d" % (2 + b)
                for c in range(4):
                    S.op("pe", lambda e, c=c, tt=tt, p=p: e.transpose(
                        out=p[:, c * 128:(c + 1) * 128], in_=yT[c][:, tt * 128:(tt + 1) * 128], identity=kb.identf[:]),
                        r=[tag + "yT%d" % c, "identf"], w=[pk])
                S.op("dve", lambda e, b=b, p=p: e.bn_stats(out=st6[b][:], in_=p[:, 0:512]), r=[pk], w=[tag + "st%d" % b])
                S.op("dve", lambda e, b=b: e.bn_aggr(out=mv[b][:], in_=st6[b][:]), r=[tag + "st%d" % b], w=[tag + "mv%d" % b])
                S.op("dve", lambda e, b=b: e.tensor_scalar(out=mv[b][:, 1:2], in0=mv[b][:, 1:2], scalar1=EPS,
                                                           scalar2=None, op0=ALU.add),
                     r=[tag + "mv%d" % b], w=[tag + "mv%d" % b])
                S.op("act", lambda e, b=b: e.activation(out=mv[b][:, 1:2], in_=mv[b][:, 1:2], func=AF.Sqrt),
                     r=[tag + "mv%d" % b], w=[tag + "mv%d" % b])
                S.op("dve", lambda e, b=b: e.reciprocal(out=mv[b][:, 1:2], in_=mv[b][:, 1:2]),
                     r=[tag + "mv%d" % b], w=[tag + "mv%d" % b])
                S.op("dve", lambda e, b=b, p=p: e.tensor_scalar(out=z[b][:], in0=p[:, 0:512], scalar1=mv[b][:, 0:1],
                                                                scalar2=mv[b][:, 1:2], op0=ALU.subtract, op1=ALU.mult),
                     r=[pk, tag + "mv%d" % b], w=[tag + "z%d" % b])
                pz = kb.ps[4 + b][:].bitcast(BF16)
                pzk = "ps%d" % (4 + b)
                for c in range(4):
                    S.op("pe", lambda e, c=c, b=b, pz=pz: e.transpose(
                        out=pz[:, c * 128:(c + 1) * 128], in_=z[b][:, c * 128:(c + 1) * 128], identity=kb.ident[:]),
                        r=[tag + "z%d" % b, "ident"], w=[pzk])
                for c in range(4):
                    S.op("act", lambda e, c=c, pz=pz, tok=tok: e.activation(
                        out=catT[:, c, tok:tok + 128], in_=pz[:, c * 128:(c + 1) * 128], func=AF.Silu,
                        scale=lnw[:, c:c + 1], bias=lnb[:, c:c + 1]),
                        r=[pzk, tag + "lnw", tag + "lnb"], w=[tag + "cat%d_%d" % (c, tok)])
        return S.flush()


NA_EDGE_ROWS = [0, 1, 2, 3, 28, 29, 30, 31]


def na_variant(q):
    if q < 4:
        return 0, 6, "x"
    if q >= 28:
        return 14, 6, "x"
    if q % 2 == 0:
        return q // 2, 4, "e"
    return (q - 1) // 2, 5, "o"


def na_bias_tables(rpb, j):
    NEG = -30000.0
    R0 = 32 * j
    ccol = np.arange(64)
    cs = np.clip(ccol - 8, 0, 48)

    def table(q, r_glob):
        c0, nch, kind = na_variant(q)
        start = int(np.clip(r_glob - 4, 0, 120))
        t = np.full((128, 8, nch, 64), NEG, np.float32)
        for i in range(nch):
            for rr2 in range(2):
                gr = R0 - 4 + 2 * (c0 + i) + rr2
                if not (start <= gr < start + 8):
                    continue
                ro = gr - r_glob + 7
                for kcol in range(64):
                    valid = (kcol >= cs) & (kcol < cs + 16)
                    co = kcol - ccol + 15
                    vals = rpb[:, ro, np.clip(co, 0, 30)]
                    t[rr2 * 64 + kcol, :, i, :] = np.where(valid[None, :], vals, NEG)
        return t

    be, bo = table(8, R0 + 8), table(9, R0 + 9)
    bx = np.stack([table(q, R0 + q) for q in NA_EDGE_ROWS], 0)
    return be, bo, bx


def na_phase(kb, tag, catT, qT_d, kT1_d, va1_d, be_d, bo_d, bx_d):
    nc, S = kb.nc, kb.S
    with contextlib.ExitStack() as st:
        sb = lambda n, s, d: st.enter_context(nc.sbuf_tensor(tag + n, s, d))
        qT = sb("qT", [128, 4, NLAT], BF16)
        kTh = sb("kTh", [128, 4, 2560], BF16)
        kTc = sb("kTc", [128, 4, 256], BF16)
        Vh = sb("Vh", [128, 20, 1024], BF16)
        Vc = sb("Vc", [128, 2, 1024], BF16)
        be = sb("be", [128, 8, 4, 64], F32)
        bo = sb("bo", [128, 8, 5, 64], F32)
        bx = [sb("bx%d" % i, [128, 8, 6, 64], F32) for i in range(2)]
        sbf = [sb("sbf%d" % i, [128, 384], F32) for i in range(2)]
        pT = [sb("pT%d" % i, [128, 512], BF16) for i in range(2)]
        rinv = sb("rinv", [128, 512], F32)
        S.dma("sp", qT[:], qT_d[:, :, 0:NLAT], w=[tag + "qT"])
        S.dma("sp", kTh[:, :, 0:256], kT1_d[:, :, 2304:2560], w=[tag + "kTh_a"])
        S.dma("sp", kTh[:, :, 256:2560], kT1_d[:, :, 0:2304], w=[tag + "kTh_b"])
        S.dma("sp", kTc[:], kT1_d[:, :, NLH:NACTTOK], w=[tag + "kTc"])
        vv = va1_d.rearrange("(c p) e -> p c e", p=128)
        S.dma("act", Vh[:, 0:2, :], vv[:, 18:20, :], w=[tag + "Vh0a"])
        S.dma("act", Vh[:, 2:10, :], vv[:, 0:8, :], w=[tag + "Vh0b"])
        S.dma("act", Vh[:, 10:20, :], vv[:, 8:18, :], w=[tag + "Vh1"])
        S.dma("act", Vc[:], vv[:, 20:22, :], w=[tag + "Vc"])
        S.dma("sp", be[:], be_d, w=[tag + "be"])
        S.dma("sp", bo[:], bo_d, w=[tag + "bo"])
        nedge = 0
        unit = 0
        for q in range(32):
            c0, nch, kind = na_variant(q)
            if kind == "x":
                bt, btk = bx[nedge % 2], tag + "bx%d" % (nedge % 2)
                S.dma("sp", bt[:], bx_d[NA_EDGE_ROWS.index(q)], w=[btk])
                nedge += 1
            elif kind == "e":
                bt, btk = be, tag + "be"
            else:
                bt, btk = bo, tag + "bo"
            po = kb.ps[3 + q % 2]
            pok = "ps%d" % (3 + q % 2)
            nw = nch * 64

            def emit_QK(h, q=q, c0=c0, nch=nch):
                ps_ = kb.ps[h % 3]
                pb, hc = (h % 2) * 64, h // 2
                for i in range(nch + 2):
                    if i < nch:
                        lhsT = kTh[pb:pb + 64, hc, (c0 + i) * 128:(c0 + i + 1) * 128]
                        rk = tag + "kTh_b"
                    else:
                        lhsT = kTc[pb:pb + 64, hc, (i - nch) * 128:(i - nch + 1) * 128]
                        rk = tag + "kTc"
                    S.op("pe", lambda e, lhsT=lhsT, i=i, ps_=ps_: e.matmul(
                        ps_[:, i * 64:(i + 1) * 64], lhsT=lhsT, rhs=qT[pb:pb + 64, hc, q * 64:(q + 1) * 64],
                        start=True, stop=True), r=[rk, tag + "kTh_a", tag + "qT"], w=["ps%d" % (h % 3)])

            def emit_soft(h, nw=nw, nch=nch, bt=bt, btk=btk):
                ps_ = kb.ps[h % 3]
                psk = "ps%d" % (h % 3)
                b = h % 2
                S.op("dve", lambda e: e.scalar_tensor_tensor(
                    out=sbf[b][:, 0:nw], in0=ps_[:, 0:nw], scalar=0.125,
                    in1=bt[:, h, :, :].rearrange("p c q -> p (c q)"), op0=ALU.mult, op1=ALU.add),
                    r=[psk, btk], w=[tag + "sbf%d" % b])
                S.op("act", lambda e: e.activation(out=pT[b][:, 0:nw], in_=sbf[b][:, 0:nw], func=AF.Exp),
                     r=[tag + "sbf%d" % b], w=[tag + "pTa%d" % b])
                S.op("act", lambda e: e.activation(out=pT[b][:, nw:nw + 128], in_=ps_[:, nw:nw + 128], func=AF.Exp,
                                                   scale=0.125), r=[psk], w=[tag + "pTb%d" % b])

            def emit_PV(h, c0=c0, nch=nch, po=po, pok=pok):
                b = h % 2
                for i in range(nch + 2):
                    if i < nch:
                        lhsT = Vh[:, c0 + i, h * 128:(h + 1) * 128]
                        rk = tag + "Vh1"
                    else:
                        lhsT = Vc[:, i - nch, h * 128:(h + 1) * 128]
                        rk = tag + "Vc"
                    S.op("pe", lambda e, lhsT=lhsT, i=i: e.matmul(
                        po[:, h * 64:(h + 1) * 64], lhsT=lhsT, rhs=pT[b][:, i * 64:(i + 1) * 64],
                        start=(i == 0), stop=(i == nch + 1)),
                        r=[rk, tag + "Vh0a", tag + "Vh0b", tag + "pTa%d" % b, tag + "pTb%d" % b], w=[pok])

            emit_QK(0)
            for h in range(8):
                emit_soft(h)
                if h + 1 < 8:
                    emit_QK(h + 1)
                emit_PV(h)
            S.op("dve", lambda e, po=po: e.reciprocal(out=rinv[64:128, :], in_=po[64:128, 0:512]), r=[pok], w=[tag + "rinv"])
            for ev in range(2):
                o_v = po[0:64, 0:512].rearrange("p (hp e d) -> p hp e d", hp=4, e=2)[:, :, ev, :]
                r_v = rinv[64:128, :].rearrange("p (hp e d) -> p hp e d", hp=4, e=2)[:, :, ev, :]
                S.op("dve", lambda e, o_v=o_v, r_v=r_v, ev=ev, q=q: e.tensor_tensor(
                    out=catT[ev * 64:(ev + 1) * 64, 4:8, q * 64:(q + 1) * 64], in0=o_v, in1=r_v, op=ALU.mult),
                    r=[pok, tag + "rinv"], w=[tag + "cat%d_%d" % (q, ev)])
        return S.flush()


def final_phase(kb, tag, x_in, out_d, fn_d, nt):
    nc, S = kb.nc, kb.S
    with contextlib.ExitStack() as st:
        sb = lambda n, s, d: st.enter_context(nc.sbuf_tensor(tag + n, s, d))
        fn = sb("fn", [128, D], F32)
        xt = [sb("xt%d" % i, [128, D], F32) for i in range(4)]
        junk = sb("junk", [128, D], BF16)
        ss = [sb("ss%d" % i, [128, 1], F32) for i in range(4)]
        S.dma("sp", fn[:], fn_d, w=[tag + "fn"])
        for t in range(nt):
            b = t % 4
            xk = tag + "x%d" % b
            S.dma("sp", xt[b][:], x_in[t * 128:(t + 1) * 128, :], w=[xk])
            S.op("act", lambda e, b=b: e.activation(out=junk[:], in_=xt[b][:], func=AF.Square, accum_out=ss[b][:, 0:1]),
                 r=[xk], w=[tag + "ss%d" % b])
            rstd_ops(S, ss[b], 1, [tag + "ss%d" % b], D)
            S.op("dve", lambda e, b=b: e.scalar_tensor_tensor(out=xt[b][:], in0=xt[b][:], scalar=ss[b][:, 0:1], in1=fn[:],
                                                              op0=ALU.mult, op1=ALU.mult),
                 r=[xk, tag + "ss%d" % b, tag + "fn"], w=[xk])
            S.dma("sp", out_d[t * 128:(t + 1) * 128, :], xt[b][:], r=[xk], w=[tag + "o%d" % t])
        return S.flush()


def modfull_phase(kb, csT_d, wmod_d, bmc_d, bmr_d, modc_d, modr_d):
    nc, S = kb.nc, kb.S
    with contextlib.ExitStack() as st:
        sb = lambda n, s_, d: st.enter_context(nc.sbuf_tensor("mf_" + n, s_, d))
        s2 = sb("s2", [128, 8, 2], F32)
        srep = sb("srep", [128, 8, 2, 128], F32)
        NWV = 4
        wv = [sb("wv%d" % i, [128, 8, D], F32) for i in range(NWV)]
        modc = sb("modc", [128, 2, 9, 2, 8], F32)
        bmc = sb("bmc", [128, 2, 9, 8], F32)
        bmr = [sb("bmr%d" % i, [128, D], F32) for i in range(2)]
        rowt = [sb("rowt%d" % i, [128, 2, D], F32) for i in range(2)]
        S.dma("sp", s2[:], csT_d, w=["mf_s2"])
        S.dma("sp", bmc[:], bmc_d, w=["mf_bmc"])
        S.op("act", lambda e: e.activation(out=s2[:], in_=s2[:], func=AF.Silu), r=["mf_s2"], w=["mf_s2"])
        S.op("dve", lambda e: e.tensor_copy(out=srep[:], in_=s2[:].unsqueeze(3).to_broadcast([128, 8, 2, 128])),
             r=["mf_s2"], w=["mf_srep"])
        n = 0
        gi = 0
        issued = 0
        for l in range(2):
            wl = wmod_d[l].rearrange("(kc p) f -> p kc f", p=128)
            for v in range(9):
                b = n % NWV
                wk = "mf_wv%d" % b
                while issued < min(18, n + NWV):
                    li, vi, bi_ = issued // 9, issued % 9, issued % NWV
                    wl2 = wmod_d[li].rearrange("(kc p) f -> p kc f", p=128)
                    S.dma("sp", wv[bi_][:, 0:4, :], wl2[:, 0:4, vi * D:(vi + 1) * D], w=["mf_wv%da" % bi_])
                    S.dma("act", wv[bi_][:, 4:8, :], wl2[:, 4:8, vi * D:(vi + 1) * D], w=["mf_wv%db" % bi_])
                    issued += 1
                pc = kb.ps[n % 2]
                pck = "ps%d" % (n % 2)
                for c in range(8):
                    for kc in range(8):
                        S.op("pe", lambda e, c=c, kc=kc, b=b, pc=pc: e.matmul(
                            pc[:, c * 2:(c + 1) * 2], lhsT=wv[b][:, kc, c * 128:(c + 1) * 128], rhs=s2[:, kc, :],
                            start=(kc == 0), stop=(kc == 7)), r=[wk + "a", wk + "b", "mf_s2"], w=[pck])
                S.op("dve", lambda e, l=l, v=v, pc=pc: e.tensor_tensor(
                    out=modc[:, l, v, :, :], in0=pc[:, 0:16].rearrange("p (c s) -> p s c", s=2),
                    in1=bmc[:, l, v, :].unsqueeze(1).to_broadcast([128, 2, 8]), op=ALU.add),
                    r=[pck, "mf_bmc"], w=["mf_modc"])
                if v in (2, 5, 8):
                    g = (2, 5, 8).index(v)
                    rb = gi % 2
                    gi += 1
                    S.dma("sp", bmr[rb][:], bmr_d[:, l, g, :], w=["mf_bmr%d" % rb])
                    for strm in range(2):
                        for hf in range(2):
                            pr = kb.ps[2 + (strm * 2 + hf) % 4]
                            prk = "ps%d" % (2 + (strm * 2 + hf) % 4)
                            for kc in range(8):
                                S.op("pe", lambda e, kc=kc, strm=strm, hf=hf, b=b, pr=pr: e.matmul(
                                    pr[:, 0:512], lhsT=srep[:, kc, strm, :], rhs=wv[b][:, kc, hf * 512:(hf + 1) * 512],
                                    start=(kc == 0), stop=(kc == 7)), r=[wk + "a", wk + "b", "mf_srep"], w=[prk])
                            S.op("dve", lambda e, strm=strm, hf=hf, rb=rb, pr=pr: e.tensor_tensor(
                                out=rowt[rb][:, strm, hf * 512:(hf + 1) * 512], in0=pr[:, 0:512],
                                in1=bmr[rb][:, hf * 512:(hf + 1) * 512], op=ALU.add),
                                r=[prk, "mf_bmr%d" % rb], w=["mf_rowt%d_%d_%d" % (rb, strm, hf)])
                    S.dma("sp", modr_d[:, l, g, :, :], rowt[rb][:],
                          r=["mf_rowt%d_%d_%d" % (rb, a_, b_) for a_ in range(2) for b_ in range(2)], w=["mf_ro%d_%d" % (l, g)])
                n += 1
        S.dma("sp", modc_d, modc[:], r=["mf_modc"], w=["mf_co"])
        return S.flush()


NCORES = 8
_PROGS = {}


def build_fused():
    nc = bass.Bass("TRN2", target_bir_lowering=False)
    I = lambda n, s, dt=F32: nc.dram_tensor(n, list(s), dt, kind="ExternalInput").ap()
    O = lambda n, s, dt=F32: nc.dram_tensor(n, list(s), dt, kind="ExternalOutput").ap()
    T = lambda n, s, dt=F32: nc.dram_tensor(n, list(s), dt, kind="Internal").ap()
    x = I("x", [NTOKALL, D])
    csT = I("csT", [128, 8, 2]); wmod = I("wmod", [2, D, 9 * D]); bmc = I("bmc", [128, 2, 9, 8]); bmr = I("bmr", [128, 2, 3, D])
    fw = {}
    for i in range(1, 5):
        fw[i] = (I("f%d_wg" % i, [D, DFF]), I("f%d_wu" % i, [D, DFF]), I("f%d_wd" % i, [DFF, D]))
    win0 = I("win0", [D, 1536]); wout0 = I("wout0", [D, D]); win1 = I("win1", [D, 2560]); wout1 = I("wout1", [D, D])
    gains = I("gains", [128, 1024]); cos = I("cos", [8192, 32]); sin = I("sin", [8192, 32])
    tabs = {n: I("t_" + n, s, BF16) for n, s in FTAB_SHAPES.items()}
    dww = I("dww", [128, 4, 31]); dwb = I("dwb", [128, 4]); lnw = I("lnw", [128, 4]); lnb = I("lnb", [128, 4])
    be = I("be", [128, 8, 4, 64]); bo = I("bo", [128, 8, 5, 64]); bx = I("bx", [8, 128, 8, 6, 64])
    msk = I("msk", [128, 2]); fn = I("fn", [128, D])
    out = O("out", [NLAT, D])
    modc = T("modc", [128, 2, 9, 2, 8]); modr = T("modr", [128, 2, 3, 2, D])
    x1 = T("x1", [NTOKALL, D]); qT = T("qT", [64, 12, NTOKALL], BF16); kT = T("kT", [64, 4, NTOKALL], BF16)
    va = T("va", [NTOKALL, 512], BF16); f = T("f", [NTOKALL, 256], BF16)
    x2 = T("x2", [NACTTOK, D]); x3 = T("x3", [NACTTOK, D]); x4 = T("x4", [NACTTOK, D])
    uT = T("uT", [128, 4, NLH], BF16); qT1 = T("qT1", [128, 4, NLH], BF16)
    kT1 = T("kT1", [128, 4, NACTTOK], BF16); va1 = T("va1", [NACTTOK, 1024], BF16)
    x5 = T("x5", [NLAT, D]); x6 = T("x6", [NLAT, D])
    fwb = {}
    bg = []
    for i in range(2, 5):
        fwb[i] = (T("f%d_wgb" % i, [D, DFF], BF16), T("f%d_wub" % i, [D, DFF], BF16), T("f%d_wdb" % i, [DFF, D], BF16))
        for k in range(2):
            for c0 in range(0, DFF, 704):
                bg.append((fwb[i][k][:, c0:c0 + 704], fw[i][k][:, c0:c0 + 704]))
        for r0 in range(0, DFF, 1408):
            bg.append((fwb[i][2][r0:r0 + 1408, :], fw[i][2][r0:r0 + 1408, :]))
    with contextlib.ExitStack() as st:
        kb = KB(nc, st)
        modfull_phase(kb, csT, wmod, bmc, bmr, modc, modr)
        mc = lambda l, v: modc[:, l, v]
        mr = lambda l, g: modr[:, l, g]
        ffn_phase(kb, "f1", x, x1, fw[1][0], fw[1][1], fw[1][2], mc(0, 0), mc(0, 1), mr(0, 0), TILES_ALL, bg=bg)
        inproj0_phase(kb, "p1", x1, win0, mc(0, 3), mc(0, 4), gains, cos, sin, qT, kT, va, f, TILES_ALL, needq=NEEDQ0)
        with contextlib.ExitStack() as st2:
            catT = st2.enter_context(nc.sbuf_tensor("catT", [128, 8, NACTTOK], BF16))
            fourier_phase(kb, "fo", catT, f[0:8192, :], f[8192:NTOKALL, :], tabs)
            attn0_phase(kb, "at", catT, qT, kT, va)
            wout_phase(kb, "wo", catT, x1, x2, wout0, mr(0, 1), TILES_ACT_FROM_ALL)
        ffn_phase(kb, "f2", x2, x3, fwb[2][0], fwb[2][1], fwb[2][2], mc(0, 6), mc(0, 7), mr(0, 2), TILES_ACT)
        ffn_phase(kb, "f3", x3, x4, fwb[3][0], fwb[3][1], fwb[3][2], mc(1, 0), mc(1, 1), mr(1, 0), TILES_ACT)
        inproj1_phase(kb, "p2", x4, win1, mc(1, 3), mc(1, 4), uT, qT1, kT1, va1, TILES_ACT)
        with contextlib.ExitStack() as st2:
            catT = st2.enter_context(nc.sbuf_tensor("catT1", [128, 8, NLAT], BF16))
            conv_phase(kb, "cv", catT, uT, msk, dww, dwb, lnw, lnb)
            na_phase(kb, "na", catT, qT1, kT1, va1, be, bo, bx)
            wout_phase(kb, "w1", catT, x4, x5, wout1, mr(1, 1), TILES_OWN)
        ffn_phase(kb, "f4", x5, x6, fwb[4][0], fwb[4][1], fwb[4][2], mc(1, 6), mc(1, 7), mr(1, 2), TILES_OWN)
        final_phase(kb, "fi", x6, out, fn, NT_LAT)
    return nc


def _c(a, dt=np.float32):
    return np.ascontiguousarray(a, dtype=dt)


def kernel(x, c, ctx, c_ctx, w_mod, b_mod, ffn_w_gate, ffn_w_up, ffn_w_down,
           ab_w_in, ab_w_out, ab_q_norm, ab_k_norm,
           cd_w_in, cd_w_out, cd_dw_w, cd_dw_b, cd_ln_w, cd_ln_b, cd_rpb, final_norm):
    f32 = np.float32
    A = lambda a: np.asarray(a, f32)
    x = A(x); ctx = A(ctx); c = A(c); c_ctx = A(c_ctx); w_mod = _c(w_mod); b_mod = A(b_mod)
    ffn_w_gate = A(ffn_w_gate); ffn_w_up = A(ffn_w_up); ffn_w_down = A(ffn_w_down)
    common = dict(wmod=w_mod,
                  bmc=_c(b_mod.reshape(2, 9, 8, 128).transpose(3, 0, 1, 2)),
                  bmr=_c(np.broadcast_to(b_mod.reshape(2, 9, D)[:, [2, 5, 8], :][None], (128, 2, 3, D))),
                  win0=_c(A(ab_w_in)[0]), wout0=_c(A(ab_w_out)[0]), win1=_c(A(cd_w_in)[0]), wout1=_c(A(cd_w_out)[0]),
                  gains=_c(np.broadcast_to(np.concatenate([np.tile(A(ab_q_norm)[0], 12), np.tile(A(ab_k_norm)[0], 4)])[None],
                                           (128, 1024))),
                  dww=_c(A(cd_dw_w)[0].T.reshape(4, 128, 31).transpose(1, 0, 2)),
                  dwb=_c(A(cd_dw_b)[0].reshape(4, 128).T), lnw=_c(A(cd_ln_w)[0].reshape(4, 128).T),
                  lnb=_c(A(cd_ln_b)[0].reshape(4, 128).T),
                  fn=_c(np.broadcast_to(A(final_norm)[None], (128, D))))
    k = 1
    for l in range(2):
        for half in range(2):
            common["f%d_wg" % k] = _c(ffn_w_gate[l, half]); common["f%d_wu" % k] = _c(ffn_w_up[l, half])
            common["f%d_wd" % k] = _c(ffn_w_down[l, half])
            k += 1
    inv = 10000.0 ** (-np.arange(16, dtype=np.float64) / 16.0)
    rpb = A(cd_rpb)[0]
    maps = []
    for i in range(NCORES):
        b, j = i // 4, i % 4
        m = dict(common)
        m["x"] = _c(np.concatenate([np.roll(x[b], -2048 * j, axis=0), ctx[b]], 0))
        m["csT"] = _c(np.stack([c[b], c_ctx], 0).reshape(2, 8, 128).transpose(2, 1, 0))
        t = (np.arange(8192) + 2048 * j) % 8192
        ang = np.concatenate([(t // 64)[:, None] * inv, (t % 64)[:, None] * inv], -1)
        m["cos"] = _c(np.cos(ang)); m["sin"] = _c(np.sin(ang))
        for n, a in fourier_tables(j).items():
            m["t_" + n] = a
        be, bo, bx = na_bias_tables(rpb, j)
        m["be"], m["bo"], m["bx"] = be, bo, bx
        m["msk"] = _c(np.broadcast_to(np.array([0.0 if j == 0 else 1.0, 0.0 if j == 3 else 1.0], f32)[None], (128, 2)))
        maps.append(m)
    if "fused" not in _PROGS:
        _PROGS["fused"] = build_fused()
    res = run_bass_kernel_spmd(_PROGS["fused"], maps, core_ids=list(range(NCORES)))
    out = np.empty((2, 8192, D), f32)
    for i in range(NCORES):
        b, j = i // 4, i % 4
        out[b, 2048 * j:2048 * (j + 1)] = res.results[i]["out"]
    return out
```

```python
import contextlib
import numpy as np
import concourse.bass as bass
import concourse.mybir as mybir
from concourse.bass_utils import run_bass_kernel_spmd

F32 = mybir.dt.float32
BF16 = mybir.dt.bfloat16
I32 = mybir.dt.int32
AF = mybir.ActivationFunctionType
ALU = mybir.AluOpType
AX = mybir.AxisListType

ENGS = ("pe", "act", "dve", "pool", "sp")
DMA_RING = 12


class Sched:
    def __init__(self, nc, st):
        self.nc = nc
        self.ops = []
        self.start = 0
        self.last_w = {}
        self.readers = {}
        self.ring_n = {e: 0 for e in ENGS}
        self.known = {e: {} for e in ENGS}
        self.cnt = {e: 0 for e in ENGS}
        self.esem = {e: st.enter_context(nc.semaphore("s_" + e)) for e in ENGS}
        self.dsem = {}
        for e in ("sp", "act", "pool"):
            for s in range(DMA_RING):
                self.dsem[(e, s)] = st.enter_context(nc.semaphore("d_%s_%d" % (e, s)))
        self.barrier = set()

    def op(self, eng, fn, r=(), w=(), dma=False):
        pr = [k for k in r if isinstance(k, str) and len(k) == 3 and k.startswith("ps")]
        if pr:
            r = [k for k in r if k not in pr]
            w = list(w) + pr
        deps = set()
        for k in r:
            if k in self.last_w:
                deps.add(self.last_w[k])
        for k in w:
            if k in self.last_w:
                deps.add(self.last_w[k])
            for rd in self.readers.get(k, {}).values():
                deps.add(rd)
        idx = len(self.ops)
        self.ops.append(dict(eng=eng, fn=fn, deps=deps, dma=dma, inc=False))
        for k in w:
            self.last_w[k] = idx
            self.readers[k] = {}
        for k in r:
            d = self.readers.setdefault(k, {})
            d[("dma", idx) if dma else eng] = idx
        return idx

    def dma(self, eng, out, in_, r=(), w=(), **kw):
        return self.op(eng, lambda e: e.dma_start(out=out, in_=in_, **kw), r=r, w=w, dma=True)

    def flush(self):
        nc = self.nc
        ops = self.ops
        start = self.start
        dma_ops = [i for i in range(start, len(ops)) if ops[i]["dma"]]
        if dma_ops:
            idx = len(ops)
            ops.append(dict(eng="sp", fn=None, deps=set(dma_ops), dma=False, inc=False))
        end = len(ops)
        first_of = {}
        last_of = {}
        for i in range(start, end):
            E = ops[i]["eng"]
            first_of.setdefault(E, i)
            if ops[i]["fn"] is not None and not ops[i]["dma"]:
                last_of[E] = i
        for E, i in first_of.items():
            ops[i]["deps"] |= self.barrier
        for E, i in last_of.items():
            ops[i]["inc"] = True

        def resolve(d):
            p = ops[d]
            if d >= start or p["inc"]:
                return d
            j = d + 1
            while not (ops[j]["eng"] == p["eng"] and ops[j]["inc"]):
                j += 1
            return j

        for i in range(start, end):
            o = ops[i]
            E = o["eng"]
            waits_c = {}
            waits_d = {}
            for d in o["deps"]:
                if d == i:
                    continue
                p = ops[d]
                if p["dma"]:
                    key = (p["eng"], p["slot"])
                    waits_d[key] = max(waits_d.get(key, 0), p["val"])
                else:
                    if p["fn"] is None:
                        for dd in p["deps"]:
                            pp = ops[dd]
                            if pp["dma"]:
                                key = (pp["eng"], pp["slot"])
                                waits_d[key] = max(waits_d.get(key, 0), pp["val"])
                        continue
                    if p["eng"] == E and E == "pe":
                        continue
                    d = resolve(d)
                    waits_c[p["eng"]] = max(waits_c.get(p["eng"], -1), d)
            if o["dma"]:
                n = self.ring_n[E]
                self.ring_n[E] += 1
                o["slot"] = n % DMA_RING
                o["val"] = 16 * (n // DMA_RING + 1)
                if n >= DMA_RING:
                    key = (E, o["slot"])
                    waits_d[key] = max(waits_d.get(key, 0), o["val"] - 16)
            wc = []
            for pe_, d in waits_c.items():
                if self.known[E].get(pe_, -1) >= d:
                    continue
                self.known[E][pe_] = d
                ops[d]["inc"] = True
                wc.append(d)
            wd = []
            for key, v in waits_d.items():
                if self.known[E].get(key, 0) >= v:
                    continue
                self.known[E][key] = v
                wd.append((key, v))
            o["wc"] = wc
            o["wd"] = wd
        for i in range(start, end):
            o = ops[i]
            if o["inc"] and "cval" not in o:
                self.cnt[o["eng"]] += 1
                o["cval"] = self.cnt[o["eng"]]
        esem, dsem = self.esem, self.dsem
        with nc.Block() as block:
            def body(E):
                def f(eng):
                    for i in range(start, end):
                        o = ops[i]
                        if o["eng"] != E:
                            continue
                        for d in o["wc"]:
                            p = ops[d]
                            eng.wait_ge(esem[p["eng"]], p["cval"])
                        for key, v in o["wd"]:
                            eng.wait_ge(dsem[key], v)
                        if o["fn"] is None:
                            continue
                        ins = o["fn"](eng)
                        if o["dma"]:
                            ins.then_inc(dsem[(E, o["slot"])], 16)
                        elif o["inc"]:
                            ins.then_inc(esem[E], 1)
                return f

            block.tensor(body("pe"))
            block.scalar(body("act"))
            block.vector(body("dve"))
            block.gpsimd(body("pool"))
            block.sync(body("sp"))
        self.barrier = set(last_of.values())
        if dma_ops:
            self.barrier.add(idx)
        self.start = end
        for i in range(start, end):
            ops[i]["fn"] = ops[i]["fn"] is not None and True or None
        self.last_w = {}
        self.readers = {}
        return dict(n_ops=end - start, cnt=dict(self.cnt))


NT_LAT = 16
NT_CTX = 2
NT = NT_LAT + NT_CTX
NTOK = NT * 128
NLAT = NT_LAT * 128
NALL = 66
NTOKALL = NALL * 128
NACT = 22
NACTTOK = NACT * 128
NLH = 20 * 128
TILES_ALL = [(t, t, 0) for t in range(64)] + [(64, 64, 1), (65, 65, 1)]
ACT_SRC = list(range(18)) + [62, 63, 64, 65]
TILES_ACT_FROM_ALL = [(ACT_SRC[a], a, 0 if a < 20 else 1) for a in range(NACT)]
TILES_ACT = [(a, a, 0 if a < 20 else 1) for a in range(NACT)]
TILES_OWN = [(a, a, 0) for a in range(NT_LAT)]


NEEDQ0 = set(range(18)) | {62, 63, 64, 65}


def make_groups(tiles, key=None):
    groups, cur = [], []
    for t in tiles:
        if cur and (len(cur) == 4 or cur[-1][2] != t[2] or cur[-1][0] + 1 != t[0] or cur[-1][1] + 1 != t[1]
                    or (key is not None and key(cur[-1]) != key(t))):
            groups.append(cur)
            cur = []
        cur.append(t)
    if cur:
        groups.append(cur)
    return groups
D = 1024
DFF = 2816
NF = 22
EPS = 1e-6


class KB:
    def __init__(self, nc, st):
        self.nc = nc
        self.S = Sched(nc, st)
        self.ps = [st.enter_context(nc.psum_tensor("ps%d" % i, [128, 512], F32)) for i in range(8)]
        self.ident = st.enter_context(nc.sbuf_tensor("ident", [128, 128], BF16))
        self.identf = st.enter_context(nc.sbuf_tensor("identf", [128, 128], F32))
        self.ones_f = st.enter_context(nc.sbuf_tensor("ones_f", [128, 128], F32))
        S = self.S
        for t, k in ((self.ident, "ident"), (self.identf, "identf")):
            S.op("pool", lambda e, t=t: e.memset(t[:], 0.0), w=[k])
            S.op("pool", lambda e, t=t: e.affine_select(out=t[:], in_=t[:], pattern=[[-1, 128]],
                                                        compare_op=ALU.not_equal, fill=1.0, base=0,
                                                        channel_multiplier=1), r=[k], w=[k])
        S.op("pool", lambda e: e.memset(self.ones_f[:], 1.0), w=["ones_f"])
        self.dram_n = 0

    def dram(self, name, shape, dt, kind="Internal"):
        return self.nc.dram_tensor(name, list(shape), dt, kind=kind).ap()


def group_list():
    import os
    g = [(4 * i, 4, 0) for i in range(NT_LAT // 4)]
    g.append((NT_LAT, NT_CTX, 1))
    ng = int(os.environ.get("NGROUPS", "99"))
    return g[:ng]


def rstd_ops(S, ss, nt, keys, mean_div):
    S.op("dve", lambda e: e.tensor_scalar(out=ss[:, 0:nt], in0=ss[:, 0:nt], scalar1=1.0 / mean_div, scalar2=EPS,
                                          op0=ALU.mult, op1=ALU.add), r=keys, w=keys)
    S.op("act", lambda e: e.activation(out=ss[:, 0:nt], in_=ss[:, 0:nt], func=AF.Sqrt), r=keys, w=keys)
    S.op("dve", lambda e: e.reciprocal(out=ss[:, 0:nt], in_=ss[:, 0:nt]), r=keys, w=keys)


def norm_group(kb, tag, xts, xkeys, ss, junk, xn, hT, sc1, sh, strm, psb_i, hkey, ktag=None):
    S = kb.S
    import os
    NP = int(os.environ.get("NORM_PARTS", "15"))
    nt = len(xts)
    ktag = tag if ktag is None else ktag
    sskey = ktag + "ss"
    for t in range(nt if NP & 1 else 0):
        S.op("act", lambda e, t=t: e.activation(out=junk[:], in_=xts[t][:], func=AF.Square,
                                                accum_out=ss[:, t:t + 1]), r=[xkeys[t]], w=[sskey + str(t)])
    allss = [sskey + str(t) for t in range(nt)]
    if NP & 1:
        rstd_ops(S, ss, nt, allss, D)
    psbs = [kb.ps[i][:].bitcast(BF16) for i in psb_i]
    pkeys = ["ps%d" % i for i in psb_i]
    for t in range(nt if NP & 2 else 0):
        b = t % 2
        S.op("act", lambda e, t=t, b=b: e.activation(out=xn[b][:], in_=xts[t][:], func=AF.Copy,
                                                     scale=ss[:, t:t + 1]),
             r=[xkeys[t]] + allss, w=[tag + "xn%d" % b])
        for kc in range(8 if NP & 4 else 0):
            psb, pkey = psbs[kc // 4], pkeys[kc // 4]
            S.op("pe", lambda e, kc=kc, b=b, psb=psb: e.transpose(out=psb[:, kc * 128:(kc + 1) * 128],
                                                         in_=xn[b][:, kc * 128:(kc + 1) * 128],
                                                         identity=kb.ident[:]),
                 r=[tag + "xn%d" % b, "ident"], w=[pkey])
        for kc in range(8 if NP & 8 else 0):
            eng = "dve" if kc >= 4 else "act"
            psb, pkey = psbs[kc // 4], pkeys[kc // 4]
            if eng == "dve":
                S.op("dve", lambda e, kc=kc, t=t, psb=psb: e.tensor_scalar(
                    out=hT[:, kc, t * 128:(t + 1) * 128], in0=psb[:, kc * 128:(kc + 1) * 128],
                    scalar1=sc1[:, strm, kc:kc + 1], scalar2=sh[:, strm, kc:kc + 1],
                    op0=ALU.mult, op1=ALU.add), r=[pkey, tag + "modc"], w=[hkey + "_%d_%d" % (t, kc)])
            else:
                S.op("act", lambda e, kc=kc, t=t, psb=psb: e.activation(
                    out=hT[:, kc, t * 128:(t + 1) * 128], in_=psb[:, kc * 128:(kc + 1) * 128],
                    func=AF.Identity, scale=sc1[:, strm, kc:kc + 1], bias=sh[:, strm, kc:kc + 1]),
                    r=[pkey, tag + "modc"], w=[hkey + "_%d_%d" % (t, kc)])
    return [hkey + "_%d_%d" % (t, kc) for t in range(nt) for kc in range(8)]


def load_modc(kb, st, tag, sh_d, sc_d):
    nc, S = kb.nc, kb.S
    sh = st.enter_context(nc.sbuf_tensor(tag + "sh", [128, 2, 8], F32))
    sc1 = st.enter_context(nc.sbuf_tensor(tag + "sc1", [128, 2, 8], F32))
    S.dma("sp", sh[:], sh_d, w=[tag + "modc_a"])
    S.dma("sp", sc1[:], sc_d, w=[tag + "modc_b"])
    S.op("dve", lambda e: e.tensor_scalar(out=sc1[:], in0=sc1[:], scalar1=1.0, scalar2=None, op0=ALU.add),
         r=[tag + "modc_a", tag + "modc_b"], w=[tag + "modc"])
    return sh, sc1


def ffn_phase(kb, tag, x_in, x_out, wg_d, wu_d, wd_d, sh_d, sc_d, g_d, tiles, dbg=99, bg=()):
    nc, S = kb.nc, kb.S
    with contextlib.ExitStack() as st:
        sb = lambda n, s, d: st.enter_context(nc.sbuf_tensor(tag + n, s, d))
        wg = sb("wg", [128, 8, DFF], BF16)
        wu = sb("wu", [128, 8, DFF], BF16)
        wd = sb("wd", [128, NF, D], BF16)
        G = sb("G", [128, 2, D], F32)
        NXB = 5
        xt = [sb("xt%d" % i, [128, D], F32) for i in range(NXB)]
        xn = [sb("xn%d" % i, [128, D], BF16) for i in range(2)]
        junk = sb("junk", [128, D], BF16)
        ss = sb("ss", [128, 4], F32)
        hT = sb("hT", [128, 8, 512], BF16)
        aT = sb("aT", [128, NF, 512], BF16)
        sg = [sb("sg%d" % i, [128, 512], F32) for i in range(2)]
        tmp = [sb("tmp%d" % i, [128, 512], F32) for i in range(2)]
        sh, sc1 = load_modc(kb, st, tag, sh_d, sc_d)
        S.dma("sp", G[:], g_d, w=[tag + "G0"])
        S.op("pool", lambda e: e.tensor_scalar(out=G[:], in0=G[:], scalar1=0.5, scalar2=0.0, op0=ALU.mult, op1=ALU.add),
             r=[tag + "G0"], w=[tag + "G"])
        wg_v = wg_d.rearrange("(kc p) f -> p kc f", p=128)
        wu_v = wu_d.rearrange("(kc p) f -> p kc f", p=128)
        wd_v = wd_d.rearrange("(f p) d -> p f d", p=128)
        nblk = (DFF + 511) // 512
        pre = wg_d.dtype == BF16
        qs = ("sp", "act") if pre else ("pool", "pool")
        for b in range(nblk if dbg >= -1 else 0):
            c0, c1 = b * 512, min(DFF, (b + 1) * 512)
            S.dma(qs[0], wg[:, :, c0:c1], wg_v[:, :, c0:c1], w=[tag + "wg%d" % b])
            S.dma(qs[1], wu[:, :, c0:c1], wu_v[:, :, c0:c1], w=[tag + "wu%d" % b])
        WDG = 4
        for b in range((NF + WDG - 1) // WDG if dbg >= -2 else 0):
            f0, f1 = b * WDG, min(NF, (b + 1) * WDG)
            S.dma(qs[b % 2], wd[:, f0:f1, :], wd_v[:, f0:f1, :], w=[tag + "wd%d" % b])
        bg = list(bg)
        groups = make_groups(tiles)
        xkey = lambda n: tag + "x%d" % (n % NXB)

        def load_x(n):
            S.dma("sp", xt[n % NXB][:], x_in[tiles[n][0] * 128:(tiles[n][0] + 1) * 128, :], w=[xkey(n)])

        loaded = 0
        n0 = 0
        for gi, grp in enumerate(groups):
            nt, strm = len(grp), grp[0][2]
            while loaded < min(len(tiles), n0 + NXB):
                load_x(loaded)
                loaded += 1
            ntok = nt * 128
            xts = [xt[(n0 + t) % NXB] for t in range(nt)]
            xkeys = [xkey(n0 + t) for t in range(nt)]
            dsts = [g_[1] for g_ in grp]
            n0 += nt
            for _ in range(3):
                if bg:
                    dst_, src_ = bg.pop(0)
                    S.dma("pool", dst_, src_, w=[tag + "bg%d" % len(bg)])
            if dbg >= 1:
                hkeys = norm_group(kb, tag, xts, xkeys, ss, junk, xn, hT, sc1, sh, strm, (0, 7), tag + "hT")
            for f in range(NF if dbg >= 2 else 0):
                pg, pu = kb.ps[1 + f % 2], kb.ps[3 + f % 2]
                kg, ku = "ps%d" % (1 + f % 2), "ps%d" % (3 + f % 2)
                blk = (f * 128) // 512
                for kc in range(8):
                    S.op("pe", lambda e, kc=kc, f=f, pg=pg, ntok=ntok: e.matmul(
                        pg[:, 0:ntok], lhsT=wg[:, kc, f * 128:(f + 1) * 128], rhs=hT[:, kc, 0:ntok],
                        start=(kc == 0), stop=(kc == 7)),
                        r=[tag + "wg%d" % blk] + [tag + "hT_%d_%d" % (t, kc) for t in range(nt)], w=[kg])
                for kc in range(8):
                    S.op("pe", lambda e, kc=kc, f=f, pu=pu, ntok=ntok: e.matmul(
                        pu[:, 0:ntok], lhsT=wu[:, kc, f * 128:(f + 1) * 128], rhs=hT[:, kc, 0:ntok],
                        start=(kc == 0), stop=(kc == 7)),
                        r=[tag + "wu%d" % blk] + [tag + "hT_%d_%d" % (t, kc) for t in range(nt)], w=[ku])
                S.op("act", lambda e, f=f, pg=pg, ntok=ntok: e.activation(out=sg[f % 2][:, 0:ntok], in_=pg[:, 0:ntok],
                                                               func=AF.Silu), r=[kg], w=[tag + "sg%d" % (f % 2)])
                S.op("dve", lambda e, f=f, pu=pu, ntok=ntok: e.tensor_tensor(out=aT[:, f, 0:ntok], in0=pu[:, 0:ntok],
                                                                  in1=sg[f % 2][:, 0:ntok], op=ALU.mult),
                     r=[ku, tag + "sg%d" % (f % 2)], w=[tag + "aT%d" % f])
            for t in range(nt):
                for dh in range(2 if dbg >= 3 else 0):
                    py = kb.ps[5 + dh]
                    ky = "ps%d" % (5 + dh)
                    for f in range(NF):
                        S.op("pe", lambda e, f=f, t=t, dh=dh, py=py: e.matmul(
                            py[:, 0:512], lhsT=aT[:, f, t * 128:(t + 1) * 128], rhs=wd[:, f, dh * 512:(dh + 1) * 512],
                            start=(f == 0), stop=(f == NF - 1)),
                            r=[tag + "aT%d" % f, tag + "wd%d" % (f // WDG)], w=[ky])
                    S.op("dve", lambda e, dh=dh, py=py, strm=strm: e.tensor_tensor(
                        out=tmp[dh][:], in0=py[:, 0:512], in1=G[:, strm, dh * 512:(dh + 1) * 512], op=ALU.mult),
                        r=[ky, tag + "G"], w=[tag + "tmp%d" % dh])
                    S.op("dve", lambda e, dh=dh, t=t, xts=xts: e.tensor_tensor(
                        out=xts[t][:, dh * 512:(dh + 1) * 512], in0=xts[t][:, dh * 512:(dh + 1) * 512],
                        in1=tmp[dh][:], op=ALU.add),
                        r=[tag + "tmp%d" % dh, xkeys[t]], w=[xkeys[t]])
                S.dma("sp", x_out[dsts[t] * 128:(dsts[t] + 1) * 128, :], xts[t][:], r=[xkeys[t]], w=[tag + "xo%d" % dsts[t]])
        while bg:
            dst_, src_ = bg.pop(0)
            S.dma("pool", dst_, src_, w=[tag + "bg%d" % len(bg)])
        return S.flush()


def mod_phase(kb, csT_d, wm_d, bm_d, mod_d):
    nc, S = kb.nc, kb.S
    NCOL = 1152
    with contextlib.ExitStack() as st:
        sb = lambda n, s, d: st.enter_context(nc.sbuf_tensor("m_" + n, s, d))
        cs = sb("cs", [128, 8, 3], F32)
        w = [sb("w%d" % l, [128, 8, NCOL], F32) for l in range(2)]
        bm = sb("bm", [3, 2, NCOL], F32)
        res = sb("res", [3, 2, NCOL], F32)
        S.dma("sp", cs[:], csT_d, w=["m_cs"])
        S.dma("sp", bm[:], bm_d, w=["m_bm"])
        for l in range(2):
            S.dma("sp" if l == 0 else "act", w[l][:], wm_d[l].rearrange("(kc p) f -> p kc f", p=128), w=["m_w%d" % l])
        S.op("act", lambda e: e.activation(out=cs[:], in_=cs[:], func=AF.Silu), r=["m_cs"], w=["m_cs"])
        i = 0
        for l in range(2):
            for c0 in range(0, NCOL, 512):
                c1 = min(NCOL, c0 + 512)
                p = kb.ps[i % 8]
                pk = "ps%d" % (i % 8)
                i += 1
                for kc in range(8):
                    S.op("pe", lambda e, kc=kc, l=l, c0=c0, c1=c1, p=p: e.matmul(
                        p[0:3, 0:c1 - c0], lhsT=cs[:, kc, :], rhs=w[l][:, kc, c0:c1], start=(kc == 0), stop=(kc == 7)),
                        r=["m_cs", "m_w%d" % l], w=[pk])
                S.op("dve", lambda e, l=l, c0=c0, c1=c1, p=p: e.tensor_tensor(
                    out=res[:, l, c0:c1], in0=p[0:3, 0:c1 - c0], in1=bm[:, l, c0:c1], op=ALU.add),
                    r=[pk, "m_bm"], w=["m_res"])
        S.dma("sp", mod_d, res[:], r=["m_res"], w=["m_out"])
        return S.flush()


def inproj0_phase(kb, tag, x_d, win_d, sh_d, sc_d, gains_d, cos_d, sin_d, qT_d, kT_d, va_d, f_d, tiles, needq=None):
    nc, S = kb.nc, kb.S
    with contextlib.ExitStack() as st:
        sb = lambda n, s_, d: st.enter_context(nc.sbuf_tensor(tag + n, s_, d))
        win = sb("win", [128, 8, 1536], BF16)
        gains = sb("gains", [128, 1024], F32)
        NROPE = cos_d.shape[0] // 128
        cosb = sb("cos", [128, NROPE, 32], F32)
        sinb = sb("sin", [128, NROPE, 32], F32)
        NXB = 6
        xt = [sb("xt%d" % i, [128, D], F32) for i in range(NXB)]
        xn = [sb("xn%d" % i, [128, D], BF16) for i in range(2)]
        junk = sb("junk", [128, D], BF16)
        ss2 = [sb("ss%d" % i, [128, 4], F32) for i in range(2)]
        hT2 = [sb("hT%d" % i, [128, 8, 512], BF16) for i in range(2)]
        qkg2 = [sb("qkg%d" % i, [128, 4, 1024], F32) for i in range(2)]
        sqg = sb("sqg", [128, 4, 1024], F32)
        ssq = sb("ssq", [128, 4, 16], F32)
        sq_flat = sqg[:].rearrange("p t c -> p (t c)")
        ta = [sq_flat[:, 0:2048].rearrange("p (t h j) -> p t h j", t=4, h=16),
              sq_flat[:, 2048:4096].rearrange("p (t h j) -> p t h j", t=4, h=16),
              sb("ta2", [128, 4, 16, 32], F32), sb("ta3", [128, 4, 16, 32], F32)]
        qkr = sb("qkr", [128, 4, 1024], BF16)
        qkT = [sb("qkT%d" % i, [64, 16, 512], BF16) for i in range(2)]
        vab = [sb("vab%d" % i, [128, 4, 128], BF16) for i in range(2)]
        fb_ = [sb("fb%d" % i, [128, 256], BF16) for i in range(2)]
        for i in range(2):
            S.op("pool", lambda e, i=i: e.memset(vab[i][:], 1.0), w=[tag + "vab%d" % i])
        sh, sc1 = load_modc(kb, st, tag, sh_d, sc_d)
        S.dma("sp", gains[:], gains_d, w=[tag + "gains"])
        S.dma("sp", cosb[:], cos_d.rearrange("(t p) j -> p t j", p=128), w=[tag + "cos"])
        S.dma("sp", sinb[:], sin_d.rearrange("(t p) j -> p t j", p=128), w=[tag + "sin"])
        win_v = win_d.rearrange("(kc p) f -> p kc f", p=128)
        for b in range(3):
            S.dma("pool", win[:, :, b * 512:(b + 1) * 512], win_v[:, :, b * 512:(b + 1) * 512], w=[tag + "win%d" % b])
        xkey = lambda n: tag + "x%d" % (n % NXB)
        loaded = 0
        nq_of = (lambda t: True) if needq is None else (lambda t: t in needq)
        groups = make_groups(tiles, key=lambda t: nq_of(t[1]))
        n0 = 0
        ginfo = []
        for gi, grp in enumerate(groups):
            ginfo.append((n0, len(grp)))
            n0 += len(grp)

        def stage_norm(gi):
            nonlocal loaded
            grp = groups[gi]
            n0_, nt_ = ginfo[gi]
            while loaded < min(len(tiles), n0_ + NXB):
                S.dma("sp", xt[loaded % NXB][:], x_d[tiles[loaded][0] * 128:(tiles[loaded][0] + 1) * 128, :],
                      w=[xkey(loaded)])
                loaded += 1
            xts_ = [xt[(n0_ + t) % NXB] for t in range(nt_)]
            xkeys_ = [xkey(n0_ + t) for t in range(nt_)]
            norm_group(kb, tag, xts_, xkeys_, ss2[gi % 2], junk, xn, hT2[gi % 2], sc1, sh, grp[0][2], (0, 7),
                       tag + "hT%d" % (gi % 2), ktag=tag + "n%d" % (gi % 2))

        qkeys_of = {}

        def stage_proj(gi):
            grp = groups[gi]
            nt, strm, t0 = len(grp), grp[0][2], grp[0][1]
            nq = nq_of(t0)
            ntok = nt * 128
            hT = hT2[gi % 2]
            hkp = tag + "hT%d" % (gi % 2)
            qT_g = qkT[gi % 2]
            gk = tag + "qkT%d" % (gi % 2)
            H0 = 0 if nq else 12
            nh = 16 - H0
            C0 = H0 * 64
            qkg = qkg2[gi % 2]
            qkeys = []
            for t in range(nt):
                tt = t0 + t
                b = tt % 2
                bank0 = 1 + 3 * (t % 2)
                for c in (range(3) if nq else (1, 2)):
                    p = kb.ps[bank0 + c]
                    for kc in range(8):
                        S.op("pe", lambda e, kc=kc, c=c, t=t, p=p, hT=hT: e.matmul(
                            p[:, 0:512], lhsT=hT[:, kc, t * 128:(t + 1) * 128], rhs=win[:, kc, c * 512:(c + 1) * 512],
                            start=(kc == 0), stop=(kc == 7)),
                            r=[hkp + "_%d_%d" % (t, kc), tag + "win%d" % c], w=["ps%d" % (bank0 + c)])
                if nq:
                    S.op("act", lambda e, t=t, bank0=bank0: e.activation(out=qkg[:, t, 0:512], in_=kb.ps[bank0][:, 0:512],
                                                                         func=AF.Copy),
                         r=["ps%d" % bank0], w=[tag + "qkg%d_%da" % (gi % 2, t)])
                    qkeys.append(tag + "qkg%d_%da" % (gi % 2, t))
                S.op("dve", lambda e, t=t, bank0=bank0: e.tensor_copy(out=qkg[:, t, 512:1024], in_=kb.ps[bank0 + 1][:, 0:512]),
                     r=["ps%d" % (bank0 + 1)], w=[tag + "qkg%d_%db" % (gi % 2, t)])
                qkeys.append(tag + "qkg%d_%db" % (gi % 2, t))
                S.op("act", lambda e, b=b, bank0=bank0: e.activation(
                    out=vab[b][:, :, 0:64], in_=kb.ps[bank0 + 2][:, 0:256].rearrange("p (k d) -> p k d", d=64), func=AF.Copy),
                    r=["ps%d" % (bank0 + 2)], w=[tag + "vab%d" % b])
                S.op("act", lambda e, b=b, bank0=bank0: e.activation(out=fb_[b][:], in_=kb.ps[bank0 + 2][:, 256:512], func=AF.Copy),
                     r=["ps%d" % (bank0 + 2)], w=[tag + "fb%d" % b])
                S.dma("sp", va_d[tt * 128:(tt + 1) * 128, :], vab[b][:].rearrange("p k d -> p (k d)"),
                      r=[tag + "vab%d" % b], w=[tag + "vao%d" % tt])
                S.dma("sp", f_d[tt * 128:(tt + 1) * 128, :], fb_[b][:], r=[tag + "fb%d" % b], w=[tag + "fo%d" % tt])
            qkeys_of[gi] = qkeys

        def stage_chain(gi):
            grp = groups[gi]
            nt, strm, t0 = len(grp), grp[0][2], grp[0][1]
            nq = nq_of(t0)
            ntok = nt * 128
            hT = hT2[gi % 2]
            hkp = tag + "hT%d" % (gi % 2)
            qT_g = qkT[gi % 2]
            gk = tag + "qkT%d" % (gi % 2)
            H0 = 0 if nq else 12
            nh = 16 - H0
            C0 = H0 * 64
            qkg = qkg2[gi % 2]
            qkeys = qkeys_of[gi]
            Q = qkg[:, 0:nt, C0:1024]
            Q4 = Q.rearrange("p t (h d) -> p t h d", d=64)
            SQ4 = sqg[:, 0:nt, C0:1024].rearrange("p t (h d) -> p t h d", d=64)
            SS = ssq[:, 0:nt, H0:16]
            qk_ = tag + "Q"
            S.op("dve", lambda e, Q=Q, nt=nt, C0=C0: e.tensor_tensor(out=sqg[:, 0:nt, C0:1024], in0=Q, in1=Q, op=ALU.mult),
                 r=qkeys, w=[tag + "sqg"])
            S.op("dve", lambda e, SQ4=SQ4, SS=SS: e.tensor_reduce(out=SS, in_=SQ4, axis=AX.X, op=ALU.add),
                 r=[tag + "sqg"], w=[tag + "ssq"])
            S.op("dve", lambda e, SS=SS: e.tensor_scalar(out=SS, in0=SS, scalar1=1.0 / 64, scalar2=EPS, op0=ALU.mult, op1=ALU.add),
                 r=[tag + "ssq"], w=[tag + "ssq"])
            S.op("act", lambda e, SS=SS: e.activation(out=SS, in_=SS, func=AF.Sqrt), r=[tag + "ssq"], w=[tag + "ssq"])
            S.op("dve", lambda e, SS=SS: e.reciprocal(out=SS, in_=SS), r=[tag + "ssq"], w=[tag + "ssq"])
            S.op("dve", lambda e, Q4=Q4, SS=SS, nt=nt, nh=nh: e.tensor_tensor(
                out=Q4, in0=Q4, in1=SS.unsqueeze(3).to_broadcast([128, nt, nh, 64]), op=ALU.mult),
                r=qkeys + [tag + "ssq"], w=[qk_])
            S.op("dve", lambda e, Q=Q, nt=nt, nh=nh, C0=C0: e.tensor_tensor(
                out=Q, in0=Q, in1=gains[:, C0:1024].unsqueeze(1).to_broadcast([128, nt, nh * 64]), op=ALU.mult),
                r=[qk_, tag + "gains"], w=[qk_])
            R4 = qkr[:, 0:nt, C0:1024].rearrange("p t (h d) -> p t h d", d=64)
            rk = tag + "qkr"
            if strm == 0:
                X1, X2 = Q4[:, :, :, 0:32], Q4[:, :, :, 32:64]
                cb = cosb[:, t0:t0 + nt, :].unsqueeze(2).to_broadcast([128, nt, nh, 32])
                sb_ = sinb[:, t0:t0 + nt, :].unsqueeze(2).to_broadcast([128, nt, nh, 32])
                tav = [ta[i][:, 0:nt, H0:16, :] for i in range(4)]
                tk = [tag + "sqg", tag + "sqg", tag + "ta2", tag + "ta3"]
                S.op("dve", lambda e, X1=X1, cb=cb, tav=tav: e.tensor_tensor(out=tav[0], in0=X1, in1=cb, op=ALU.mult),
                     r=[qk_, tag + "cos"], w=[tk[0]])
                S.op("dve", lambda e, X2=X2, sb_=sb_, tav=tav: e.tensor_tensor(out=tav[1], in0=X2, in1=sb_, op=ALU.mult),
                     r=[qk_, tag + "sin"], w=[tk[1]])
                S.op("pool", lambda e, X2=X2, cb=cb, tav=tav: e.tensor_tensor(out=tav[2], in0=X2, in1=cb, op=ALU.mult),
                     r=[qk_, tag + "cos"], w=[tk[2]])
                S.op("pool", lambda e, X1=X1, sb_=sb_, tav=tav: e.tensor_tensor(out=tav[3], in0=X1, in1=sb_, op=ALU.mult),
                     r=[qk_, tag + "sin"], w=[tk[3]])
                S.op("dve", lambda e, R4=R4, tav=tav: e.tensor_tensor(out=R4[:, :, :, 0:32], in0=tav[0], in1=tav[1],
                                                                      op=ALU.subtract), r=[tk[0], tk[1]], w=[rk + "a"])
                S.op("pool", lambda e, R4=R4, tav=tav: e.tensor_tensor(out=R4[:, :, :, 32:64], in0=tav[2], in1=tav[3],
                                                                       op=ALU.add), r=[tk[2], tk[3]], w=[rk + "b"])
            else:
                S.op("dve", lambda e, Q=Q, nt=nt, C0=C0: e.tensor_copy(out=qkr[:, 0:nt, C0:1024], in_=Q),
                     r=[qk_], w=[rk + "a", rk + "b"])
            gks = []
            for t in range(nt):
                for hb in ((0, 1) if nq else (1,)):
                    h_lo = max(H0, hb * 8)
                    nhh = (hb + 1) * 8 - h_lo
                    bi = 1 + 3 * (t % 2) + hb
                    pT = kb.ps[bi][:].bitcast(BF16)
                    pk = "ps%d" % bi
                    for hh in range(nhh):
                        h = h_lo + hh
                        S.op("pe", lambda e, h=h, hh=hh, pT=pT, t=t: e.transpose(
                            out=pT[0:64, hh * 128:(hh + 1) * 128], in_=qkr[:, t, h * 64:(h + 1) * 64],
                            identity=kb.ident[:]), r=[rk + "a", rk + "b", "ident"], w=[pk])
                    src = pT[0:64, 0:nhh * 128].rearrange("p (h t) -> p h t", h=nhh)
                    dst = qT_g[:, h_lo:h_lo + nhh, t * 128:(t + 1) * 128]
                    k_ = gk + "_%d_%d" % (t, hb)
                    gks.append(k_)
                    if hb == 0:
                        S.op("act", lambda e, src=src, dst=dst: e.activation(out=dst, in_=src, func=AF.Copy), r=[pk], w=[k_])
                    else:
                        S.op("dve", lambda e, src=src, dst=dst: e.tensor_copy(out=dst, in_=src), r=[pk], w=[k_])
            tok0 = t0 * 128
            if nq:
                S.dma("sp", qT_d[:, :, tok0:tok0 + ntok], qT_g[:, 0:12, 0:ntok], r=gks, w=[tag + "qo%d" % gi])
            S.dma("sp", kT_d[:, :, tok0:tok0 + ntok], qT_g[:, 12:16, 0:ntok], r=gks, w=[tag + "ko%d" % gi])

        stage_norm(0)
        stage_proj(0)
        for gi in range(len(groups)):
            if gi + 1 < len(groups):
                stage_norm(gi + 1)
                stage_proj(gi + 1)
            stage_chain(gi)
        return S.flush()


def fourier_phase(kb, tag, catT, f_all_d, fc_d, tabs, nb2=20):
    nc, S = kb.nc, kb.S
    with contextlib.ExitStack() as st:
        sb = lambda n, s, d: st.enter_context(nc.sbuf_tensor(tag + n, s, d))
        xs = sb("xs", [128, 64, 256], BF16)
        A = [sb("Are", [128, 128, 128], BF16), sb("Aim", [128, 128, 128], BF16)]
        c128 = sb("c128", [128, 128], BF16)
        ns128 = sb("ns128", [128, 128], BF16)
        tw = {n: sb(n, [128, 128, 2 * nb2], BF16) for n in ("twc", "tws", "twns")}
        dd = {n: sb(n, [128, 2, 256], BF16) for n in ("dc", "ds")}
        O = [sb("Ore", [128, 128, 2, nb2], BF16), sb("Oim", [128, 128, 2, nb2], BF16)]
        S.dma("sp", xs[:].rearrange("p k f -> p (k f)"), f_all_d.rearrange("(a b) f -> a (b f)", b=64), w=[tag + "xs"])
        S.dma("sp", c128[:], tabs["c128"], w=[tag + "c128"])
        S.dma("sp", ns128[:], tabs["ns128"], w=[tag + "ns128"])
        for n in tw:
            S.dma("act", tw[n][:], tabs[n], w=[tag + n])
        for n in dd:
            S.dma("act", dd[n][:], tabs[n], w=[tag + n])
        tabA = [(c128, tag + "c128"), (ns128, tag + "ns128")]
        for fb in range(32):
            for ri in range(2):
                p = kb.ps[ri * 2 + fb % 2]
                pk = "ps%d" % (ri * 2 + fb % 2)
                for i in range(4):
                    fp = fb * 4 + i
                    for f2 in range(2):
                        lhsT = xs[:, :, 2 * fp + f2]
                        S.op("pe", lambda e, lhsT=lhsT, p=p, i=i, ri=ri, f2=f2: e.matmul(
                            p[f2 * 64:(f2 + 1) * 64, i * 128:(i + 1) * 128], lhsT=lhsT, rhs=tabA[ri][0][:],
                            start=True, stop=True),
                            r=[tag + "xs", tabA[ri][1]], w=[pk])
                dst = A[ri][:, fb * 4:(fb + 1) * 4, :]
                src = p[:, 0:512].rearrange("p (a n) -> p a n", a=4)
                if ri == 0:
                    S.op("act", lambda e, dst=dst, src=src: e.activation(out=dst, in_=src, func=AF.Copy),
                         r=[pk], w=[tag + "A%d_%d" % (ri, fb)])
                else:
                    S.op("dve", lambda e, dst=dst, src=src: e.tensor_copy(out=dst, in_=src),
                         r=[pk], w=[tag + "A%d_%d" % (ri, fb)])
        Akeys = [[tag + "A%d_%d" % (ri, fb) for fb in range(32)] for ri in range(2)]
        W2 = 2 * nb2
        NPB = 512 // W2
        nbanks = (128 + NPB - 1) // NPB
        for nb in range(nbanks):
            n1s = list(range(nb * NPB, min(128, (nb + 1) * NPB)))
            for ri in range(2):
                p = kb.ps[4 + ri * 2 + nb % 2]
                pk = "ps%d" % (4 + ri * 2 + nb % 2)
                for i, n1 in enumerate(n1s):
                    if ri == 0:
                        terms = [(A[0], "twc", 0), (A[1], "tws", 1)]
                    else:
                        terms = [(A[1], "twc", 1), (A[0], "twns", 0)]
                    for ti, (At, tn, ai) in enumerate(terms):
                        S.op("pe", lambda e, At=At, tn=tn, n1=n1, p=p, i=i, ti=ti: e.matmul(
                            p[:, i * W2:(i + 1) * W2], lhsT=At[:, :, n1], rhs=tw[tn][:, n1, :],
                            start=(ti == 0), stop=(ti == 1)),
                            r=Akeys[ai] + [tag + tn], w=[pk])
                dst = O[ri][:, n1s[0]:n1s[-1] + 1, :, :]
                src = p[:, 0:len(n1s) * W2].rearrange("p (a f n) -> p a f n", a=len(n1s), f=2)
                if ri == 0:
                    S.op("act", lambda e, dst=dst, src=src: e.activation(out=dst, in_=src, func=AF.Copy),
                         r=[pk], w=[tag + "O%d_%d" % (ri, nb)])
                else:
                    S.op("dve", lambda e, dst=dst, src=src: e.tensor_copy(out=dst, in_=src),
                         r=[pk], w=[tag + "O%d_%d" % (ri, nb)])
        Okeys = [[tag + "O%d_%d" % (ri, nb) for nb in range(nbanks)] for ri in range(2)]
        for ch in range(2):
            for nb in range(nb2 // 4):
                p = kb.ps[(ch * 5 + nb) % 4]
                pk = "ps%d" % ((ch * 5 + nb) % 4)
                k = 0
                for f2 in range(2):
                    for ri, dn in ((0, "dc"), (1, "ds")):
                        rhs = O[ri][:, :, f2, nb * 4:(nb + 1) * 4].rearrange("p n a -> p a n")
                        S.op("pe", lambda e, rhs=rhs, dn=dn, f2=f2, ch=ch, p=p, k=k: e.matmul(
                            p[:, 0:512], lhsT=dd[dn][:, f2, ch * 128:(ch + 1) * 128], rhs=rhs,
                            start=(k == 0), stop=(k == 3)),
                            r=Okeys[ri] + [tag + dn], w=[pk])
                        k += 1
                dst = catT[:, 6 + ch, nb * 512:(nb + 1) * 512]
                if nb % 2 == 0:
                    S.op("act", lambda e, dst=dst, p=p: e.activation(out=dst, in_=p[:, 0:512], func=AF.Copy),
                         r=[pk], w=[tag + "cat%d_%d" % (ch, nb)])
                else:
                    S.op("dve", lambda e, dst=dst, p=p: e.tensor_copy(out=dst, in_=p[:, 0:512]),
                         r=[pk], w=[tag + "cat%d_%d" % (ch, nb)])
        fcs = sb("fcs", [128, 2, 256], BF16)
        c256 = sb("c256", [128, 2, 256], BF16)
        s256 = sb("s256", [128, 2, 256], BF16)
        dcf = sb("dcf", [128, 128], BF16)
        ndsf = sb("ndsf", [128, 128], BF16)
        Z = [[sb("Z%d_%d" % (a, b), [128, 256], BF16) for b in range(2)] for a in range(2)]
        S.dma("sp", fcs[:], fc_d.rearrange("(kc p) f -> p kc f", p=128), w=[tag + "fcs"])
        S.dma("sp", c256[:], tabs["c256"], w=[tag + "c256"])
        S.dma("sp", s256[:], tabs["s256"], w=[tag + "s256"])
        S.dma("sp", dcf[:], tabs["dcf"], w=[tag + "dcf"])
        S.dma("sp", ndsf[:], tabs["ndsf"], w=[tag + "ndsf"])
        for fch in range(2):
            for ti, (tb, tk) in enumerate(((c256, "c256"), (s256, "s256"))):
                p = kb.ps[4 + fch * 2 + ti]
                pk = "ps%d" % (4 + fch * 2 + ti)
                for kc in range(2):
                    S.op("pe", lambda e, fch=fch, tb=tb, kc=kc, p=p: e.matmul(
                        p[:, 0:256], lhsT=fcs[:, kc, fch * 128:(fch + 1) * 128], rhs=tb[:, kc, :],
                        start=(kc == 0), stop=(kc == 1)), r=[tag + "fcs", tag + tk], w=[pk])
                S.op("dve" if ti else "act",
                     (lambda e, fch=fch, ti=ti, p=p: e.tensor_copy(out=Z[fch][ti][:], in_=p[:, 0:256])) if ti else
                     (lambda e, fch=fch, ti=ti, p=p: e.activation(out=Z[fch][ti][:], in_=p[:, 0:256], func=AF.Copy)),
                     r=[pk], w=[tag + "Z%d_%d" % (fch, ti)])
        for ch in range(2):
            p = kb.ps[ch]
            pk = "ps%d" % ch
            S.op("pe", lambda e, ch=ch, p=p: e.matmul(p[:, 0:256], lhsT=dcf[:], rhs=Z[ch][0][:], start=True, stop=False),
                 r=[tag + "dcf", tag + "Z%d_0" % ch], w=[pk])
            S.op("pe", lambda e, ch=ch, p=p: e.matmul(p[:, 0:256], lhsT=ndsf[:], rhs=Z[ch][1][:], start=False, stop=True),
                 r=[tag + "ndsf", tag + "Z%d_1" % ch], w=[pk])
            S.op("act", lambda e, ch=ch, p=p: e.activation(out=catT[:, 6 + ch, nb2 * 128:nb2 * 128 + 256], in_=p[:, 0:256], func=AF.Copy),
                 r=[pk], w=[tag + "catc%d" % ch])
        return S.flush()


def fourier_tables(j):
    import ml_dtypes
    bf = lambda a: np.ascontiguousarray(a.astype(np.float32)).astype(ml_dtypes.bfloat16)
    k1 = np.arange(128)[:, None]
    n1 = np.arange(128)[None, :]
    T = {}
    T["c128"] = bf(np.cos(2 * np.pi * k1 * n1 / 128))
    T["ns128"] = bf(-np.sin(2 * np.pi * k1 * n1 / 128))
    k2 = np.arange(64)
    NB2 = np.array(list(range(18)) + [62, 63])
    nb2 = len(NB2)
    n = 128 * NB2[None, None, :] + np.arange(128)[None, :, None]
    th = 2 * np.pi * ((n * k2[:, None, None]) % 8192) / 8192.0 \
        + (np.pi / 2) * ((j * (n + k2[:, None, None])) % 4)
    twc = np.zeros((2, 64, 128, 2, nb2))
    tws = np.zeros((2, 64, 128, 2, nb2))
    for f2 in range(2):
        twc[f2, :, :, f2, :] = np.cos(th)
        tws[f2, :, :, f2, :] = np.sin(th)
    T["twc"] = bf(twc.reshape(128, 128, 2 * nb2))
    T["tws"] = bf(tws.reshape(128, 128, 2 * nb2))
    T["twns"] = bf(-tws.reshape(128, 128, 2 * nb2))
    sc = 1.0 / np.sqrt(8192.0 * 64.0)
    dc = np.zeros((4, 32, 2, 4, 64))
    ds = np.zeros((4, 32, 2, 4, 64))
    m = np.arange(64)[None, :]
    for f2 in range(2):
        c = (2 * np.arange(32) + f2)[:, None]
        for g in range(4):
            dc[g, :, f2, g, :] = np.cos(2 * np.pi * m * c / 64) * sc
            ds[g, :, f2, g, :] = np.sin(2 * np.pi * m * c / 64) * sc
    T["dc"] = bf(dc.reshape(128, 2, 256))
    T["ds"] = bf(ds.reshape(128, 2, 256))
    kk = np.arange(256)[:, None]
    nn = np.arange(256)[None, :]
    c256 = np.cos(2 * np.pi * kk * nn / 256).reshape(2, 128, 256).transpose(1, 0, 2)
    s256 = np.sin(2 * np.pi * kk * nn / 256).reshape(2, 128, 256).transpose(1, 0, 2)
    T["c256"] = bf(c256)
    T["s256"] = bf(s256)
    scc = 1.0 / np.sqrt(256.0 * 64.0)
    dcf = np.zeros((2, 64, 2, 64))
    dsf = np.zeros((2, 64, 2, 64))
    cc = np.arange(64)[:, None]
    for g in range(2):
        dcf[g, :, g, :] = np.cos(2 * np.pi * m * cc / 64) * scc
        dsf[g, :, g, :] = np.sin(2 * np.pi * m * cc / 64) * scc
    T["dcf"] = bf(dcf.reshape(128, 128))
    T["ndsf"] = bf(-dsf.reshape(128, 128))
    return T


FTAB_SHAPES = dict(c128=[128, 128], ns128=[128, 128], twc=[128, 128, 40], tws=[128, 128, 40], twns=[128, 128, 40],
                   dc=[128, 2, 256], ds=[128, 2, 256], c256=[128, 2, 256], s256=[128, 2, 256],
                   dcf=[128, 128], ndsf=[128, 128])


NKC = 66


def attn0_phase(kb, tag, catT, qT_d, kT_all_d, v_all_d):
    nc, S = kb.nc, kb.S
    with contextlib.ExitStack() as st:
        sb = lambda n, s, d: st.enter_context(nc.sbuf_tensor(tag + n, s, d))
        V = sb("V", [128, NKC, 512], BF16)
        kT = [sb("kT%d" % i, [128, NKC * 128], BF16) for i in range(2)]
        qTs = [sb("qT%d" % i, [128, 3, NACTTOK], BF16) for i in range(2)]
        for i in range(2):
            S.op("pool", lambda e, i=i: e.memset(kT[i][64:128, :], 0.0), w=[tag + "kTz%d" % i])
            S.op("pool", lambda e, i=i: e.memset(qTs[i][64:128, :, :], 0.0), w=[tag + "qTz%d" % i])
        pT = [sb("pT%d" % i, [128, 512], BF16) for i in range(3)]
        rinv = [sb("rinv%d" % i, [128, 512], F32) for i in range(2)]
        v_v = v_all_d.rearrange("(c p) e -> p c e", p=128)
        for i in range(3):
            S.dma("sp", V[:, i * 22:(i + 1) * 22, :], v_v[:, i * 22:(i + 1) * 22, :], w=[tag + "V%d" % i])
        blk = 0
        pending = None
        for kvh in range(4):
            kb_, kk = kT[kvh % 2], tag + "kT%d" % (kvh % 2)
            qb_, qk_ = qTs[kvh % 2], tag + "qT%d" % (kvh % 2)
            S.dma("sp", kb_[0:64, :], kT_all_d[:, kvh, :], w=[kk])
            S.dma("act", qb_[0:64, :, 0:2304], qT_d[:, 3 * kvh:3 * kvh + 3, 0:2304], w=[qk_ + "a"])
            S.dma("act", qb_[0:64, :, 2304:NACTTOK], qT_d[:, 3 * kvh:3 * kvh + 3, 7936:NTOKALL], w=[qk_ + "b"])
            for g in range(3):
                h = 3 * kvh + g
                for qb in range(6):
                    if qb < 5:
                        q0, nq, chunks = qb * 512, 512, list(range(NKC))
                    else:
                        q0, nq, chunks = NLH, 256, [64, 65]
                    po = kb.ps[3 + blk % 2]
                    pok = "ps%d" % (3 + blk % 2)
                    ri = rinv[blk % 2]
                    rik = tag + "rinv%d" % (blk % 2)
                    blk += 1

                    def emit_S(c, kb_=kb_, qb_=qb_, g=g, q0=q0, nq=nq, kk=kk, qk_=qk_, kvh=kvh):
                        S.op("pe", lambda e: e.matmul(kb.ps[c % 3][:, 0:nq], lhsT=kb_[:, c * 128:(c + 1) * 128],
                                                      rhs=qb_[:, g, q0:q0 + nq], start=True, stop=True),
                             r=[kk, qk_ + "a", qk_ + "b", tag + "kTz%d" % (kvh % 2), tag + "qTz%d" % (kvh % 2)],
                             w=["ps%d" % (c % 3)])

                    def emit_E(c, nq=nq):
                        S.op("act", lambda e: e.activation(out=pT[c % 3][:, 0:nq], in_=kb.ps[c % 3][:, 0:nq],
                                                           func=AF.Exp, scale=0.125),
                             r=["ps%d" % (c % 3)], w=[tag + "pT%d" % (c % 3)])

                    def emit_PV(c, first, last, po=po, pok=pok, kvh=kvh, nq=nq):
                        S.op("pe", lambda e: e.matmul(po[:, 0:nq], lhsT=V[:, c, kvh * 128:(kvh + 1) * 128],
                                                      rhs=pT[c % 3][:, 0:nq], start=first, stop=last),
                             r=[tag + "V%d" % (c // 22), tag + "pT%d" % (c % 3)], w=[pok])

                    emit_S(chunks[0])
                    if len(chunks) > 1:
                        emit_S(chunks[1])
                    for i, c in enumerate(chunks):
                        emit_E(c)
                        if i + 2 < len(chunks):
                            emit_S(chunks[i + 2])
                        emit_PV(c, i == 0, i == len(chunks) - 1)
                        if i == 1 and pending is not None:
                            pending()
                            pending = None

                    def fin(po=po, pok=pok, ri=ri, rik=rik, h=h, q0=q0, nq=nq):
                        S.op("dve", lambda e: e.reciprocal(out=ri[64:128, 0:nq], in_=po[64:128, 0:nq]), r=[pok], w=[rik])
                        pb = (h % 2) * 64
                        S.op("dve", lambda e: e.tensor_tensor(out=catT[pb:pb + 64, h // 2, q0:q0 + nq],
                                                              in0=po[0:64, 0:nq], in1=ri[64:128, 0:nq], op=ALU.mult),
                             r=[pok, rik], w=[tag + "cat_%d_%d" % (h, q0)])
                    pending = fin
        if pending is not None:
            pending()
        return S.flush()


def wout_phase(kb, tag, catT, x_in, x_out, wout_d, g_d, tiles):
    nc, S = kb.nc, kb.S
    with contextlib.ExitStack() as st:
        sb = lambda n, s, d: st.enter_context(nc.sbuf_tensor(tag + n, s, d))
        wo = sb("wo", [128, 8, D], BF16)
        G = sb("G", [128, 2, D], F32)
        NXB = 4
        xt = [sb("xt%d" % i, [128, D], F32) for i in range(NXB)]
        tmp = [sb("tmp%d" % i, [128, 512], F32) for i in range(2)]
        S.dma("sp", G[:], g_d, w=[tag + "G"])
        wv = wout_d.rearrange("(kc p) f -> p kc f", p=128)
        for b in range(2):
            S.dma("pool", wo[:, :, b * 512:(b + 1) * 512], wv[:, :, b * 512:(b + 1) * 512], w=[tag + "wo%d" % b])
        for n, (src, t, strm) in enumerate(tiles):
            xk = tag + "x%d" % (n % NXB)
            xb = xt[n % NXB]
            S.dma("sp", xb[:], x_in[src * 128:(src + 1) * 128, :], w=[xk])
            for dh in range(2):
                p = kb.ps[(2 * t + dh) % 4]
                pk = "ps%d" % ((2 * t + dh) % 4)
                for kc in range(8):
                    S.op("pe", lambda e, kc=kc, t=t, dh=dh, p=p: e.matmul(
                        p[:, 0:512], lhsT=catT[:, kc, t * 128:(t + 1) * 128], rhs=wo[:, kc, dh * 512:(dh + 1) * 512],
                        start=(kc == 0), stop=(kc == 7)), r=[tag + "wo%d" % dh], w=[pk])
                S.op("dve", lambda e, dh=dh, p=p, strm=strm: e.tensor_tensor(
                    out=tmp[dh][:], in0=p[:, 0:512], in1=G[:, strm, dh * 512:(dh + 1) * 512], op=ALU.mult),
                    r=[pk, tag + "G"], w=[tag + "tmp%d" % dh])
                S.op("pool", lambda e, dh=dh, xb=xb: e.tensor_tensor(
                    out=xb[:, dh * 512:(dh + 1) * 512], in0=xb[:, dh * 512:(dh + 1) * 512], in1=tmp[dh][:], op=ALU.add),
                    r=[tag + "tmp%d" % dh, xk], w=[xk])
            S.dma("sp", x_out[t * 128:(t + 1) * 128, :], xb[:], r=[xk], w=[tag + "xo%d" % t])
        return S.flush()


def inproj1_phase(kb, tag, x_d, win_d, sh_d, sc_d, uT_d, qT_d, kT_d, va_d, tiles):
    nc, S = kb.nc, kb.S
    with contextlib.ExitStack() as st:
        sb = lambda n, s, d: st.enter_context(nc.sbuf_tensor(tag + n, s, d))
        win = sb("win", [128, 8, 2560], BF16)
        NXB = 8
        xt = [sb("xt%d" % i, [128, D], F32) for i in range(NXB)]
        xn = [sb("xn%d" % i, [128, D], BF16) for i in range(2)]
        junk = sb("junk", [128, D], BF16)
        ss = sb("ss", [128, 4], F32)
        hT = sb("hT", [128, 8, 512], BF16)
        sg = [sb("sg%d" % i, [128, 512], F32) for i in range(2)]
        uT = [sb("uT%d" % i, [128, 4, 512], BF16) for i in range(2)]
        qT = [sb("qT%d" % i, [128, 4, 512], BF16) for i in range(2)]
        kT = [sb("kT%d" % i, [128, 4, 512], BF16) for i in range(2)]
        vab = [sb("vab%d" % i, [128, 8, 128], BF16) for i in range(2)]
        for i in range(2):
            S.op("pool", lambda e, i=i: e.memset(vab[i][:], 1.0), w=[tag + "vab%d" % i])
        sh, sc1 = load_modc(kb, st, tag, sh_d, sc_d)
        win_v = win_d.rearrange("(kc p) f -> p kc f", p=128)
        for b in range(5):
            S.dma("pool", win[:, :, b * 512:(b + 1) * 512], win_v[:, :, b * 512:(b + 1) * 512], w=[tag + "win%d" % b])
        xkey = lambda n: tag + "x%d" % (n % NXB)
        loaded = 0
        groups = make_groups(tiles)
        bank = 0
        n0 = 0
        for gi, grp in enumerate(groups):
            nt, strm, t0 = len(grp), grp[0][2], grp[0][1]
            while loaded < min(len(tiles), n0 + NXB):
                S.dma("sp", xt[loaded % NXB][:], x_d[tiles[loaded][0] * 128:(tiles[loaded][0] + 1) * 128, :],
                      w=[xkey(loaded)])
                loaded += 1
            ntok = nt * 128
            tok0 = t0 * 128
            xts = [xt[(n0 + t) % NXB] for t in range(nt)]
            xkeys = [xkey(n0 + t) for t in range(nt)]
            n0 += nt
            norm_group(kb, tag, xts, xkeys, ss, junk, xn, hT, sc1, sh, strm, (0, 7), tag + "hT")
            hk = [tag + "hT_%d_%d" % (t, kc) for t in range(nt) for kc in range(8)]
            gb2 = gi % 2

            def fm(col0, ntok=ntok):
                nonlocal bank
                bi = 1 + bank % 6
                bank += 1
                p = kb.ps[bi]
                for kc in range(8):
                    S.op("pe", lambda e, kc=kc, p=p: e.matmul(
                        p[:, 0:ntok], lhsT=win[:, kc, col0:col0 + 128], rhs=hT[:, kc, 0:ntok],
                        start=(kc == 0), stop=(kc == 7)), r=hk + [tag + "win%d" % (col0 // 512)], w=["ps%d" % bi])
                return p, "ps%d" % bi

            if strm == 0:
                for c in range(4):
                    pa, pak = fm(c * 128)
                    pb, pbk = fm(512 + c * 128)
                    S.op("act", lambda e, pb=pb, c=c, ntok=ntok: e.activation(out=sg[c % 2][:, 0:ntok], in_=pb[:, 0:ntok],
                                                                              func=AF.Sigmoid),
                         r=[pbk], w=[tag + "sg%d" % (c % 2)])
                    S.op("dve", lambda e, pa=pa, c=c, ntok=ntok, gb2=gb2: e.tensor_tensor(
                        out=uT[gb2][:, c, 0:ntok], in0=pa[:, 0:ntok], in1=sg[c % 2][:, 0:ntok], op=ALU.mult),
                        r=[pak, tag + "sg%d" % (c % 2)], w=[tag + "uT%d_%d" % (gb2, c)])
                S.dma("sp", uT_d[:, :, tok0:tok0 + ntok], uT[gb2][:, :, 0:ntok],
                      r=[tag + "uT%d_%d" % (gb2, c) for c in range(4)], w=[tag + "uo%d" % gi])
                for c in range(4):
                    pq, pqk = fm(1024 + c * 128)
                    S.op("act", lambda e, pq=pq, c=c, ntok=ntok, gb2=gb2: e.activation(
                        out=qT[gb2][:, c, 0:ntok], in_=pq[:, 0:ntok], func=AF.Copy),
                        r=[pqk], w=[tag + "qT%d_%d" % (gb2, c)])
                S.dma("sp", qT_d[:, :, tok0:tok0 + ntok], qT[gb2][:, :, 0:ntok],
                      r=[tag + "qT%d_%d" % (gb2, c) for c in range(4)], w=[tag + "qo%d" % gi])
            for c in range(4):
                pk_, pkk = fm(1536 + c * 128)
                S.op("dve", lambda e, pk_=pk_, c=c, ntok=ntok, gb2=gb2: e.tensor_copy(
                    out=kT[gb2][:, c, 0:ntok], in_=pk_[:, 0:ntok]), r=[pkk], w=[tag + "kT%d_%d" % (gb2, c)])
            S.dma("sp", kT_d[:, :, tok0:tok0 + ntok], kT[gb2][:, :, 0:ntok],
                  r=[tag + "kT%d_%d" % (gb2, c) for c in range(4)], w=[tag + "ko%d" % gi])
            for t in range(nt):
                tt = t0 + t
                b = tt % 2
                bi = 1 + bank % 6
                bank += 1
                p = kb.ps[bi]
                for kc in range(8):
                    S.op("pe", lambda e, kc=kc, t=t, p=p: e.matmul(
                        p[:, 0:512], lhsT=hT[:, kc, t * 128:(t + 1) * 128], rhs=win[:, kc, 2048:2560],
                        start=(kc == 0), stop=(kc == 7)), r=hk + [tag + "win4"], w=["ps%d" % bi])
                S.op("act", lambda e, b=b, p=p: e.activation(
                    out=vab[b][:, :, 0:64], in_=p[:, 0:512].rearrange("p (k d) -> p k d", d=64), func=AF.Copy),
                    r=["ps%d" % bi], w=[tag + "vab%d" % b])
                S.dma("sp", va_d[tt * 128:(tt + 1) * 128, :], vab[b][:].rearrange("p k d -> p (k d)"),
                      r=[tag + "vab%d" % b], w=[tag + "vao%d" % tt])
        return S.flush()


def conv_phase(kb, tag, catT, uT_d, msk_d, dww_d, dwb_d, lnw_d, lnb_d):
    nc, S = kb.nc, kb.S
    with contextlib.ExitStack() as st:
        sb = lambda n, s, d: st.enter_context(nc.sbuf_tensor(tag + n, s, d))
        uT = sb("uT", [128, 4, NLAT + 30], BF16)
        dwd = sb("dwd", [128, 4, 31, 128], BF16)
        dww = sb("dww", [128, 4, 31], F32)
        dwb = sb("dwb", [128, 4], F32)
        lnw = sb("lnw", [128, 4], F32)
        lnb = sb("lnb", [128, 4], F32)
        yT = [sb("yT%d" % c, [128, 512], F32) for c in range(4)]
        st6 = [sb("st6_%d" % i, [128, 6], F32) for i in range(4)]
        mv4 = sb("mv4", [128, 4, 2], F32)
        z = [sb("z%d" % i, [128, 512], BF16) for i in range(2)]
        msk = sb("msk", [128, 2], F32)
        S.dma("sp", msk[:], msk_d, w=[tag + "msk"])
        S.dma("sp", uT[:, :, 0:15], uT_d[:, :, NLH - 15:NLH], w=[tag + "uTa"])
        S.dma("sp", uT[:, :, 15:NLAT + 30], uT_d[:, :, 0:NLAT + 15], w=[tag + "uTb"])
        S.op("dve", lambda e: e.tensor_scalar(out=uT[:, :, 0:15], in0=uT[:, :, 0:15], scalar1=msk[:, 0:1], scalar2=None,
                                              op0=ALU.mult), r=[tag + "uTa", tag + "msk"], w=[tag + "uTa"])
        S.op("dve", lambda e: e.tensor_scalar(out=uT[:, :, NLAT + 15:NLAT + 30], in0=uT[:, :, NLAT + 15:NLAT + 30],
                                              scalar1=msk[:, 1:2], scalar2=None, op0=ALU.mult),
             r=[tag + "uTb", tag + "msk"], w=[tag + "uTb"])
        S.dma("sp", dww[:], dww_d, w=[tag + "dww"])
        S.dma("sp", dwb[:], dwb_d, w=[tag + "dwb"])
        S.dma("sp", lnw[:], lnw_d, w=[tag + "lnw"])
        S.dma("sp", lnb[:], lnb_d, w=[tag + "lnb"])
        for c in range(4):
            for j in range(31):
                eng = "dve" if (c * 31 + j) % 2 else "pool"
                S.op(eng, lambda e, c=c, j=j: e.tensor_scalar(out=dwd[:, c, j, :], in0=kb.ident[:],
                                                              scalar1=dww[:, c, j:j + 1], scalar2=0.0,
                                                              op0=ALU.mult, op1=ALU.add),
                     r=["ident", tag + "dww"], w=[tag + "dwd%d_%d" % (c, j)])
        for tb in range(NLAT // 512):
            for c in range(4):
                p = kb.ps[c % 2]
                pk = "ps%d" % (c % 2)
                for j in range(31):
                    S.op("pe", lambda e, c=c, j=j, tb=tb, p=p: e.matmul(
                        p[:, 0:512], lhsT=dwd[:, c, j, :], rhs=uT[:, c, tb * 512 + j:tb * 512 + j + 512],
                        start=(j == 0), stop=(j == 30)), r=[tag + "uTa", tag + "uTb", tag + "dwd%d_%d" % (c, j)], w=[pk])
                S.op("act", lambda e, c=c, p=p: e.activation(out=yT[c][:], in_=p[:, 0:512], func=AF.Identity,
                                                             bias=dwb[:, c:c + 1]),
                     r=[pk, tag + "dwb"], w=[tag + "yT%d" % c])
            for tt in range(4):
                pb_ = kb.ps[2 + tt]
                pbk = "ps%d" % (2 + tt)
                for c in range(4):
                    S.op("pe", lambda e, c=c, tt=tt, pb_=pb_: e.transpose(
                        out=pb_[:, c * 128:(c + 1) * 128], in_=yT[c][:, tt * 128:(tt + 1) * 128], identity=kb.identf[:]),
                        r=[tag + "yT%d" % c, "identf"], w=[pbk])
                S.op("dve", lambda e, tt=tt, pb_=pb_: e.bn_stats(out=st6[tt][:], in_=pb_[:, 0:512]), r=[pbk], w=[tag + "st%d" % tt])
                S.op("dve", lambda e, tt=tt: e.bn_aggr(out=mv4[:, tt, :], in_=st6[tt][:]), r=[tag + "st%d" % tt], w=[tag + "mv_%d" % tt])
            mvk = [tag + "mv_%d" % tt for tt in range(4)]
            S.op("dve", lambda e: e.tensor_scalar(out=mv4[:, :, 1], in0=mv4[:, :, 1], scalar1=EPS, scalar2=None, op0=ALU.add),
                 r=mvk, w=[tag + "rs"])
            S.op("act", lambda e: e.activation(out=mv4[:, :, 1], in_=mv4[:, :, 1], func=AF.Sqrt), r=[tag + "rs"], w=[tag + "rs"])
            S.op("dve", lambda e: e.reciprocal(out=mv4[:, :, 1], in_=mv4[:, :, 1]), r=[tag + "rs"], w=[tag + "rs"])
            for tt in range(4):
                tok = tb * 512 + tt * 128
                b = tt % 2
                pb_ = kb.ps[2 + tt]
                pbk = "ps%d" % (2 + tt)
                S.op("dve", lambda e, b=b, tt=tt, pb_=pb_: e.tensor_scalar(
                    out=z[b][:], in0=pb_[:, 0:512], scalar1=mv4[:, tt, 0:1], scalar2=mv4[:, tt, 1:2],
                    op0=ALU.subtract, op1=ALU.mult), r=[pbk, tag + "rs"] + mvk, w=[tag + "z%d" % b])
                pz = kb.ps[6 + b][:].bitcast(BF16)
                pzk = "ps%d" % (6 + b)
                for c in range(4):
                    S.op("pe", lambda e, c=c, b=b, pz=pz: e.transpose(
                        out=pz[:, c * 128:(c + 1) * 128], in_=z[b][:, c * 128:(c + 1) * 128], identity=kb.ident[:]),
                        r=[tag + "z%d" % b, "ident"], w=[pzk])
                for c in range(4):
                    S.op("act", lambda e, c=c, pz=pz, tok=tok: e.activation(
                        out=catT[:, c, tok:tok + 128], in_=pz[:, c * 128:(c + 1) * 128], func=AF.Silu,
                        scale=lnw[:, c:c + 1], bias=lnb[:, c:c + 1]),
                        r=[pzk, tag + "lnw", tag + "lnb"], w=[tag + "cat%d_%d" % (c, tok)])
        return S.flush()


NA_EDGE_ROWS = [0, 1, 2, 3, 28, 29, 30, 31]


def na_variant(q):
    if q < 4:
        return 0, 6, "x"
    if q >= 28:
        return 14, 6, "x"
    if q % 2 == 0:
        return q // 2, 4, "e"
    return (q - 1) // 2, 5, "o"


def na_bias_tables(rpb, j):
    NEG = -30000.0
    R0 = 32 * j
    ccol = np.arange(64)
    cs = np.clip(ccol - 8, 0, 48)

    def table(q, r_glob):
        c0, nch, kind = na_variant(q)
        start = int(np.clip(r_glob - 4, 0, 120))
        t = np.full((128, 8, nch, 64), NEG, np.float32)
        for i in range(nch):
            for rr2 in range(2):
                gr = R0 - 4 + 2 * (c0 + i) + rr2
                if not (start <= gr < start + 8):
                    continue
                ro = gr - r_glob + 7
                for kcol in range(64):
                    valid = (kcol >= cs) & (kcol < cs + 16)
                    co = kcol - ccol + 15
                    vals = rpb[:, ro, np.clip(co, 0, 30)]
                    t[rr2 * 64 + kcol, :, i, :] = np.where(valid[None, :], vals, NEG)
        return t

    be, bo = table(8, R0 + 8), table(9, R0 + 9)
    bx = np.stack([table(q, R0 + q) for q in NA_EDGE_ROWS], 0)
    return be, bo, bx


def na_phase(kb, tag, catT, qT_d, kT1_d, va1_d, be_d, bo_d, bx_d):
    nc, S = kb.nc, kb.S
    with contextlib.ExitStack() as st:
        sb = lambda n, s, d: st.enter_context(nc.sbuf_tensor(tag + n, s, d))
        qT = sb("qT", [128, 4, NLAT], BF16)
        kTh = sb("kTh", [128, 4, 2560], BF16)
        kTc = sb("kTc", [128, 4, 256], BF16)
        Vh = sb("Vh", [128, 20, 1024], BF16)
        Vc = sb("Vc", [128, 2, 1024], BF16)
        be = sb("be", [128, 8, 4, 64], F32)
        bo = sb("bo", [128, 8, 5, 64], F32)
        bx = [sb("bx%d" % i, [128, 8, 6, 64], F32) for i in range(2)]
        sbf = [sb("sbf%d" % i, [128, 384], F32) for i in range(2)]
        pT = [sb("pT%d" % i, [128, 512], BF16) for i in range(2)]
        rinv = sb("rinv", [128, 512], F32)
        S.dma("sp", qT[:], qT_d[:, :, 0:NLAT], w=[tag + "qT"])
        S.dma("sp", kTh[:, :, 0:256], kT1_d[:, :, 2304:2560], w=[tag + "kTh_a"])
        S.dma("sp", kTh[:, :, 256:2560], kT1_d[:, :, 0:2304], w=[tag + "kTh_b"])
        S.dma("sp", kTc[:], kT1_d[:, :, NLH:NACTTOK], w=[tag + "kTc"])
        vv = va1_d.rearrange("(c p) e -> p c e", p=128)
        S.dma("act", Vh[:, 0:2, :], vv[:, 18:20, :], w=[tag + "Vh0a"])
        S.dma("act", Vh[:, 2:10, :], vv[:, 0:8, :], w=[tag + "Vh0b"])
        S.dma("act", Vh[:, 10:20, :], vv[:, 8:18, :], w=[tag + "Vh1"])
        S.dma("act", Vc[:], vv[:, 20:22, :], w=[tag + "Vc"])
        S.dma("sp", be[:], be_d, w=[tag + "be"])
        S.dma("sp", bo[:], bo_d, w=[tag + "bo"])
        sbf3 = sbf + [sb("sbf2", [128, 384], F32)]
        pT3 = pT + [sb("pT2", [128, 512], BF16)]
        rowinfo = {}
        nedge = 0

        def row_setup(q):
            nonlocal nedge
            c0, nch, kind = na_variant(q)
            if kind == "x":
                bt, btk = bx[nedge % 2], tag + "bx%d" % (nedge % 2)
                S.dma("sp", bt[:], bx_d[NA_EDGE_ROWS.index(q)], w=[btk])
                nedge += 1
            elif kind == "e":
                bt, btk = be, tag + "be"
            else:
                bt, btk = bo, tag + "bo"
            rowinfo[q] = (c0, nch, bt, btk, kb.ps[3 + q % 2], "ps%d" % (3 + q % 2))

        def emit_QK(u):
            q, h = divmod(u, 8)
            if h == 0:
                row_setup(q)
            c0, nch, bt, btk, po, pok = rowinfo[q]
            ps_ = kb.ps[u % 3]
            pb, hc = (h % 2) * 64, h // 2
            for i in range(nch + 2):
                if i < nch:
                    lhsT = kTh[pb:pb + 64, hc, (c0 + i) * 128:(c0 + i + 1) * 128]
                    rk = tag + "kTh_b"
                else:
                    lhsT = kTc[pb:pb + 64, hc, (i - nch) * 128:(i - nch + 1) * 128]
                    rk = tag + "kTc"
                S.op("pe", lambda e, lhsT=lhsT, i=i, ps_=ps_, pb=pb, hc=hc, q=q: e.matmul(
                    ps_[:, i * 64:(i + 1) * 64], lhsT=lhsT, rhs=qT[pb:pb + 64, hc, q * 64:(q + 1) * 64],
                    start=True, stop=True), r=[rk, tag + "kTh_a", tag + "qT"], w=["ps%d" % (u % 3)])

        def emit_soft(u):
            q, h = divmod(u, 8)
            c0, nch, bt, btk, po, pok = rowinfo[q]
            nw = nch * 64
            ps_ = kb.ps[u % 3]
            psk = "ps%d" % (u % 3)
            b = u % 3
            S.op("dve", lambda e: e.scalar_tensor_tensor(
                out=sbf3[b][:, 0:nw], in0=ps_[:, 0:nw], scalar=0.125,
                in1=bt[:, h, :, :].rearrange("p c q -> p (c q)"), op0=ALU.mult, op1=ALU.add),
                r=[psk, btk], w=[tag + "sbf%d" % b])
            S.op("act", lambda e: e.activation(out=pT3[b][:, 0:nw], in_=sbf3[b][:, 0:nw], func=AF.Exp),
                 r=[tag + "sbf%d" % b], w=[tag + "pTa%d" % b])
            S.op("act", lambda e: e.activation(out=pT3[b][:, nw:nw + 128], in_=ps_[:, nw:nw + 128], func=AF.Exp,
                                               scale=0.125), r=[psk], w=[tag + "pTb%d" % b])

        def emit_PV(u):
            q, h = divmod(u, 8)
            c0, nch, bt, btk, po, pok = rowinfo[q]
            b = u % 3
            for i in range(nch + 2):
                if i < nch:
                    lhsT = Vh[:, c0 + i, h * 128:(h + 1) * 128]
                    rk = tag + "Vh1"
                else:
                    lhsT = Vc[:, i - nch, h * 128:(h + 1) * 128]
                    rk = tag + "Vc"
                S.op("pe", lambda e, lhsT=lhsT, i=i, po=po, h=h, b=b, nch=nch: e.matmul(
                    po[:, h * 64:(h + 1) * 64], lhsT=lhsT, rhs=pT3[b][:, i * 64:(i + 1) * 64],
                    start=(i == 0), stop=(i == nch + 1)),
                    r=[rk, tag + "Vh0a", tag + "Vh0b", tag + "pTa%d" % b, tag + "pTb%d" % b], w=[pok])

        def fin(q):
            c0, nch, bt, btk, po, pok = rowinfo[q]
            rv = rinv2[q % 2]
            rvk = tag + "rinv%d" % (q % 2)
            S.op("dve", lambda e: e.reciprocal(out=rv[64:128, :], in_=po[64:128, 0:512]), r=[pok], w=[rvk])
            for ev in range(2):
                o_v = po[0:64, 0:512].rearrange("p (hp e d) -> p hp e d", hp=4, e=2)[:, :, ev, :]
                r_v = rv[64:128, :].rearrange("p (hp e d) -> p hp e d", hp=4, e=2)[:, :, ev, :]
                S.op("dve", lambda e, o_v=o_v, r_v=r_v, ev=ev: e.tensor_tensor(
                    out=catT[ev * 64:(ev + 1) * 64, 4:8, q * 64:(q + 1) * 64], in0=o_v, in1=r_v, op=ALU.mult),
                    r=[pok, rvk], w=[tag + "cat%d_%d" % (q, ev)])

        rinv2 = [rinv, sb("rinvb", [128, 512], F32)]
        NU = 32 * 8
        emit_QK(0)
        emit_QK(1)
        pend = []
        for u in range(NU):
            emit_soft(u)
            if u + 2 < NU:
                emit_QK(u + 2)
            emit_PV(u)
            pend = [(d_ - 1, q_) for (d_, q_) in pend]
            for (d_, q_) in [x_ for x_ in pend if x_[0] <= 0]:
                fin(q_)
            pend = [x_ for x_ in pend if x_[0] > 0]
            if u % 8 == 7:
                pend.append((2, u // 8))
        for (d_, q_) in pend:
            fin(q_)
        return S.flush()


def final_phase(kb, tag, x_in, out_d, fn_d, nt):
    nc, S = kb.nc, kb.S
    with contextlib.ExitStack() as st:
        sb = lambda n, s_, d: st.enter_context(nc.sbuf_tensor(tag + n, s_, d))
        fn = sb("fn", [128, D], F32)
        xt = [sb("xt%d" % i, [128, D], F32) for i in range(8)]
        junk = sb("junk", [128, D], BF16)
        ss = [sb("ss%d" % i, [128, 4], F32) for i in range(2)]
        S.dma("sp", fn[:], fn_d, w=[tag + "fn"])
        for g in range(nt // 4):
            gb = g % 2
            sk = [tag + "ss%d_%d" % (gb, i) for i in range(4)]
            for i in range(4):
                t = g * 4 + i
                b = t % 8
                S.dma("sp" if i % 2 else "act", xt[b][:], x_in[t * 128:(t + 1) * 128, :], w=[tag + "x%d" % b])
                S.op("act", lambda e, b=b, i=i, gb=gb: e.activation(out=junk[:], in_=xt[b][:], func=AF.Square,
                                                                    accum_out=ss[gb][:, i:i + 1]),
                     r=[tag + "x%d" % b], w=[sk[i]])
            rstd_ops(S, ss[gb], 4, sk, D)
            for i in range(4):
                t = g * 4 + i
                b = t % 8
                S.op("dve", lambda e, b=b, i=i, gb=gb: e.scalar_tensor_tensor(
                    out=xt[b][:], in0=xt[b][:], scalar=ss[gb][:, i:i + 1], in1=fn[:], op0=ALU.mult, op1=ALU.mult),
                    r=[tag + "x%d" % b, tag + "fn"] + sk, w=[tag + "x%d" % b])
                S.dma("sp", out_d[t * 128:(t + 1) * 128, :], xt[b][:], r=[tag + "x%d" % b], w=[tag + "o%d" % t])
        return S.flush()


def modfull_phase(kb, csT_d, wmod_d, bmc_d, bmr_d, modc_d, modr_d):
    nc, S = kb.nc, kb.S
    with contextlib.ExitStack() as st:
        sb = lambda n, s_, d: st.enter_context(nc.sbuf_tensor("mf_" + n, s_, d))
        s2 = sb("s2", [128, 8, 2], F32)
        srep = sb("srep", [128, 8, 2, 128], F32)
        NWV = 4
        wv = [sb("wv%d" % i, [128, 8, D], F32) for i in range(NWV)]
        modc = sb("modc", [128, 2, 9, 2, 8], F32)
        bmc = sb("bmc", [128, 2, 9, 8], F32)
        bmr = [sb("bmr%d" % i, [128, D], F32) for i in range(2)]
        rowt = [sb("rowt%d" % i, [128, 2, D], F32) for i in range(2)]
        S.dma("sp", s2[:], csT_d, w=["mf_s2"])
        S.dma("sp", bmc[:], bmc_d, w=["mf_bmc"])
        S.op("act", lambda e: e.activation(out=s2[:], in_=s2[:], func=AF.Silu), r=["mf_s2"], w=["mf_s2"])
        S.op("dve", lambda e: e.tensor_copy(out=srep[:], in_=s2[:].unsqueeze(3).to_broadcast([128, 8, 2, 128])),
             r=["mf_s2"], w=["mf_srep"])
        n = 0
        gi = 0
        issued = 0
        for l in range(2):
            wl = wmod_d[l].rearrange("(kc p) f -> p kc f", p=128)
            for v in range(9):
                b = n % NWV
                wk = "mf_wv%d" % b
                while issued < min(18, n + NWV):
                    li, vi, bi_ = issued // 9, issued % 9, issued % NWV
                    wl2 = wmod_d[li].rearrange("(kc p) f -> p kc f", p=128)
                    S.dma("sp", wv[bi_][:, 0:4, :], wl2[:, 0:4, vi * D:(vi + 1) * D], w=["mf_wv%da" % bi_])
                    S.dma("act", wv[bi_][:, 4:8, :], wl2[:, 4:8, vi * D:(vi + 1) * D], w=["mf_wv%db" % bi_])
                    issued += 1
                pc = kb.ps[n % 2]
                pck = "ps%d" % (n % 2)
                for c in range(8):
                    for kc in range(8):
                        S.op("pe", lambda e, c=c, kc=kc, b=b, pc=pc: e.matmul(
                            pc[:, c * 2:(c + 1) * 2], lhsT=wv[b][:, kc, c * 128:(c + 1) * 128], rhs=s2[:, kc, :],
                            start=(kc == 0), stop=(kc == 7)), r=[wk + "a", wk + "b", "mf_s2"], w=[pck])
                S.op("dve", lambda e, l=l, v=v, pc=pc: e.tensor_tensor(
                    out=modc[:, l, v, :, :], in0=pc[:, 0:16].rearrange("p (c s) -> p s c", s=2),
                    in1=bmc[:, l, v, :].unsqueeze(1).to_broadcast([128, 2, 8]), op=ALU.add),
                    r=[pck, "mf_bmc"], w=["mf_modc"])
                if v in (2, 5, 8):
                    g = (2, 5, 8).index(v)
                    rb = gi % 2
                    gi += 1
                    S.dma("sp", bmr[rb][:], bmr_d[:, l, g, :], w=["mf_bmr%d" % rb])
                    for strm in range(2):
                        for hf in range(2):
                            pr = kb.ps[2 + (strm * 2 + hf) % 4]
                            prk = "ps%d" % (2 + (strm * 2 + hf) % 4)
                            for kc in range(8):
                                S.op("pe", lambda e, kc=kc, strm=strm, hf=hf, b=b, pr=pr: e.matmul(
                                    pr[:, 0:512], lhsT=srep[:, kc, strm, :], rhs=wv[b][:, kc, hf * 512:(hf + 1) * 512],
                                    start=(kc == 0), stop=(kc == 7)), r=[wk + "a", wk + "b", "mf_srep"], w=[prk])
                            S.op("dve", lambda e, strm=strm, hf=hf, rb=rb, pr=pr: e.tensor_tensor(
                                out=rowt[rb][:, strm, hf * 512:(hf + 1) * 512], in0=pr[:, 0:512],
                                in1=bmr[rb][:, hf * 512:(hf + 1) * 512], op=ALU.add),
                                r=[prk, "mf_bmr%d" % rb], w=["mf_rowt%d_%d_%d" % (rb, strm, hf)])
                    S.dma("sp", modr_d[:, l, g, :, :], rowt[rb][:],
                          r=["mf_rowt%d_%d_%d" % (rb, a_, b_) for a_ in range(2) for b_ in range(2)], w=["mf_ro%d_%d" % (l, g)])
                n += 1
        S.dma("sp", modc_d, modc[:], r=["mf_modc"], w=["mf_co"])
        return S.flush()


NCORES = 8
_PROGS = {}


def build_fused():
    nc = bass.Bass("TRN2", target_bir_lowering=False)
    I = lambda n, s, dt=F32: nc.dram_tensor(n, list(s), dt, kind="ExternalInput").ap()
    O = lambda n, s, dt=F32: nc.dram_tensor(n, list(s), dt, kind="ExternalOutput").ap()
    T = lambda n, s, dt=F32: nc.dram_tensor(n, list(s), dt, kind="Internal").ap()
    x = I("x", [NTOKALL, D])
    csT = I("csT", [128, 8, 2]); wmod = I("wmod", [2, D, 9 * D]); bmc = I("bmc", [128, 2, 9, 8]); bmr = I("bmr", [128, 2, 3, D])
    fw = {}
    for i in range(1, 5):
        fw[i] = (I("f%d_wg" % i, [D, DFF]), I("f%d_wu" % i, [D, DFF]), I("f%d_wd" % i, [DFF, D]))
    win0 = I("win0", [D, 1536]); wout0 = I("wout0", [D, D]); win1 = I("win1", [D, 2560]); wout1 = I("wout1", [D, D])
    gains = I("gains", [128, 1024]); cos = I("cos", [8192, 32]); sin = I("sin", [8192, 32])
    tabs = {n: I("t_" + n, s, BF16) for n, s in FTAB_SHAPES.items()}
    dww = I("dww", [128, 4, 31]); dwb = I("dwb", [128, 4]); lnw = I("lnw", [128, 4]); lnb = I("lnb", [128, 4])
    be = I("be", [128, 8, 4, 64]); bo = I("bo", [128, 8, 5, 64]); bx = I("bx", [8, 128, 8, 6, 64])
    msk = I("msk", [128, 2]); fn = I("fn", [128, D])
    out = O("out", [NLAT, D])
    modc = T("modc", [128, 2, 9, 2, 8]); modr = T("modr", [128, 2, 3, 2, D])
    x1 = T("x1", [NTOKALL, D]); qT = T("qT", [64, 12, NTOKALL], BF16); kT = T("kT", [64, 4, NTOKALL], BF16)
    va = T("va", [NTOKALL, 512], BF16); f = T("f", [NTOKALL, 256], BF16)
    x2 = T("x2", [NACTTOK, D]); x3 = T("x3", [NACTTOK, D]); x4 = T("x4", [NACTTOK, D])
    uT = T("uT", [128, 4, NLH], BF16); qT1 = T("qT1", [128, 4, NLH], BF16)
    kT1 = T("kT1", [128, 4, NACTTOK], BF16); va1 = T("va1", [NACTTOK, 1024], BF16)
    x5 = T("x5", [NLAT, D]); x6 = T("x6", [NLAT, D])
    fwb = {}
    bg = []
    for i in range(2, 5):
        fwb[i] = (T("f%d_wgb" % i, [D, DFF], BF16), T("f%d_wub" % i, [D, DFF], BF16), T("f%d_wdb" % i, [DFF, D], BF16))
        for k in range(2):
            for c0 in range(0, DFF, 704):
                bg.append((fwb[i][k][:, c0:c0 + 704], fw[i][k][:, c0:c0 + 704]))
        for r0 in range(0, DFF, 1408):
            bg.append((fwb[i][2][r0:r0 + 1408, :], fw[i][2][r0:r0 + 1408, :]))
    with contextlib.ExitStack() as st:
        kb = KB(nc, st)
        modfull_phase(kb, csT, wmod, bmc, bmr, modc, modr)
        mc = lambda l, v: modc[:, l, v]
        mr = lambda l, g: modr[:, l, g]
        ffn_phase(kb, "f1", x, x1, fw[1][0], fw[1][1], fw[1][2], mc(0, 0), mc(0, 1), mr(0, 0), TILES_ALL, bg=bg)
        inproj0_phase(kb, "p1", x1, win0, mc(0, 3), mc(0, 4), gains, cos, sin, qT, kT, va, f, TILES_ALL, needq=NEEDQ0)
        with contextlib.ExitStack() as st2:
            catT = st2.enter_context(nc.sbuf_tensor("catT", [128, 8, NACTTOK], BF16))
            fourier_phase(kb, "fo", catT, f[0:8192, :], f[8192:NTOKALL, :], tabs)
            attn0_phase(kb, "at", catT, qT, kT, va)
            wout_phase(kb, "wo", catT, x1, x2, wout0, mr(0, 1), TILES_ACT_FROM_ALL)
        ffn_phase(kb, "f2", x2, x3, fwb[2][0], fwb[2][1], fwb[2][2], mc(0, 6), mc(0, 7), mr(0, 2), TILES_ACT)
        ffn_phase(kb, "f3", x3, x4, fwb[3][0], fwb[3][1], fwb[3][2], mc(1, 0), mc(1, 1), mr(1, 0), TILES_ACT)
        inproj1_phase(kb, "p2", x4, win1, mc(1, 3), mc(1, 4), uT, qT1, kT1, va1, TILES_ACT)
        with contextlib.ExitStack() as st2:
            catT = st2.enter_context(nc.sbuf_tensor("catT1", [128, 8, NLAT], BF16))
            conv_phase(kb, "cv", catT, uT, msk, dww, dwb, lnw, lnb)
            na_phase(kb, "na", catT, qT1, kT1, va1, be, bo, bx)
            wout_phase(kb, "w1", catT, x4, x5, wout1, mr(1, 1), TILES_OWN)
        ffn_phase(kb, "f4", x5, x6, fwb[4][0], fwb[4][1], fwb[4][2], mc(1, 6), mc(1, 7), mr(1, 2), TILES_OWN)
        final_phase(kb, "fi", x6, out, fn, NT_LAT)
    return nc


def _c(a, dt=np.float32):
    return np.ascontiguousarray(a, dtype=dt)


def kernel(x, c, ctx, c_ctx, w_mod, b_mod, ffn_w_gate, ffn_w_up, ffn_w_down,
           ab_w_in, ab_w_out, ab_q_norm, ab_k_norm,
           cd_w_in, cd_w_out, cd_dw_w, cd_dw_b, cd_ln_w, cd_ln_b, cd_rpb, final_norm):
    f32 = np.float32
    A = lambda a: np.asarray(a, f32)
    x = A(x); ctx = A(ctx); c = A(c); c_ctx = A(c_ctx); w_mod = _c(w_mod); b_mod = A(b_mod)
    ffn_w_gate = A(ffn_w_gate); ffn_w_up = A(ffn_w_up); ffn_w_down = A(ffn_w_down)
    common = dict(wmod=w_mod,
                  bmc=_c(b_mod.reshape(2, 9, 8, 128).transpose(3, 0, 1, 2)),
                  bmr=_c(np.broadcast_to(b_mod.reshape(2, 9, D)[:, [2, 5, 8], :][None], (128, 2, 3, D))),
                  win0=_c(A(ab_w_in)[0]), wout0=_c(A(ab_w_out)[0]), win1=_c(A(cd_w_in)[0]), wout1=_c(A(cd_w_out)[0]),
                  gains=_c(np.broadcast_to(np.concatenate([np.tile(A(ab_q_norm)[0], 12), np.tile(A(ab_k_norm)[0], 4)])[None],
                                           (128, 1024))),
                  dww=_c(A(cd_dw_w)[0].T.reshape(4, 128, 31).transpose(1, 0, 2)),
                  dwb=_c(A(cd_dw_b)[0].reshape(4, 128).T), lnw=_c(A(cd_ln_w)[0].reshape(4, 128).T),
                  lnb=_c(A(cd_ln_b)[0].reshape(4, 128).T),
                  fn=_c(np.broadcast_to(A(final_norm)[None], (128, D))))
    k = 1
    for l in range(2):
        for half in range(2):
            common["f%d_wg" % k] = _c(ffn_w_gate[l, half]); common["f%d_wu" % k] = _c(ffn_w_up[l, half])
            common["f%d_wd" % k] = _c(ffn_w_down[l, half])
            k += 1
    inv = 10000.0 ** (-np.arange(16, dtype=np.float64) / 16.0)
    rpb = A(cd_rpb)[0]
    maps = []
    for i in range(NCORES):
        b, j = i // 4, i % 4
        m = dict(common)
        m["x"] = _c(np.concatenate([np.roll(x[b], -2048 * j, axis=0), ctx[b]], 0))
        m["csT"] = _c(np.stack([c[b], c_ctx], 0).reshape(2, 8, 128).transpose(2, 1, 0))
        t = (np.arange(8192) + 2048 * j) % 8192
        ang = np.concatenate([(t // 64)[:, None] * inv, (t % 64)[:, None] * inv], -1)
        m["cos"] = _c(np.cos(ang)); m["sin"] = _c(np.sin(ang))
        for n, a in fourier_tables(j).items():
            m["t_" + n] = a
        be, bo, bx = na_bias_tables(rpb, j)
        m["be"], m["bo"], m["bx"] = be, bo, bx
        m["msk"] = _c(np.broadcast_to(np.array([0.0 if j == 0 else 1.0, 0.0 if j == 3 else 1.0], f32)[None], (128, 2)))
        maps.append(m)
    if "fused" not in _PROGS:
        _PROGS["fused"] = build_fused()
    res = run_bass_kernel_spmd(_PROGS["fused"], maps, core_ids=list(range(NCORES)))
    out = np.empty((2, 8192, D), f32)
    for i in range(NCORES):
        b, j = i // 4, i % 4
        out[b, 2048 * j:2048 * (j + 1)] = res.results[i]["out"]
    return out
```
